# Optimizing a Trainium2 kernel written in Bass

```python
import jax, jax.numpy as jnp
from jax import lax
import numpy as np

D_MODEL = 1024
BATCH = 8
SEQ = 2048
DEPTH = 4
DEC_BATCH = 32
DEC_SEQ = 16
PAST_LEN = 1024

CHUNK = 64
N_HEADS = 16
HEAD_DIM = D_MODEL // N_HEADS
D_FF = 2816
LEFT_CHUNKS = 8
BAND = (LEFT_CHUNKS + 1) * CHUNK
A_WINDOW = LEFT_CHUNKS * CHUNK
A_CACHE = min(A_WINDOW, PAST_LEN)
A_KEEP = min(A_WINDOW, SEQ)
REL_MAX = 256
N_REL = 2 * REL_MAX + 1
Q_BLOCK = 128
N_MIXERS = 3
N_A = (DEPTH + 2) // 3
N_B = (DEPTH + 1) // 3
N_C = DEPTH // 3
RMS_EPS = 1e-6
ATTN_SCALE = HEAD_DIM ** -0.5
NEG_INF = -1e30
FORGET_BIAS_INIT = 3.0

kernel_name = 'hybrid_streaming_chunk_encoder_step'


def rms_norm(x, g):
    xf = x.astype(jnp.float32)
    y = xf * lax.rsqrt(jnp.mean(xf * xf, axis=-1, keepdims=True) + RMS_EPS)
    return (y * g.astype(jnp.float32)).astype(x.dtype)


def swiglu(h, w_gate, w_up, w_down):
    return (jax.nn.silu(h @ w_gate) * (h @ w_up)) @ w_down


def project_qkv(h, w_qkv):
    q, k, v = jnp.split(h @ w_qkv, 3, axis=-1)
    shape = h.shape[:-1] + (N_HEADS, HEAD_DIM)
    return q.reshape(shape), k.reshape(shape), v.reshape(shape)


def rel_bias(rel_table, dist):
    idx = jnp.clip(dist, -REL_MAX, REL_MAX) + REL_MAX
    return jnp.moveaxis(rel_table[idx].astype(jnp.float32), -1, 0)


def chunk_attn_prompt(q, k, v, rel_table):
    b, s = q.shape[:2]
    nc = s // CHUNK
    qc = q.reshape(b, nc, CHUNK, N_HEADS, HEAD_DIM)
    pad = ((0, 0), (LEFT_CHUNKS * CHUNK, 0), (0, 0), (0, 0))
    kp = jnp.pad(k, pad).reshape(b, nc + LEFT_CHUNKS, CHUNK, N_HEADS, HEAD_DIM)
    vp = jnp.pad(v, pad).reshape(b, nc + LEFT_CHUNKS, CHUNK, N_HEADS, HEAD_DIM)
    kb = jnp.concatenate([kp[:, o:o + nc] for o in range(LEFT_CHUNKS + 1)], axis=2)
    vb = jnp.concatenate([vp[:, o:o + nc] for o in range(LEFT_CHUNKS + 1)], axis=2)
    i = jnp.arange(CHUNK)[:, None]
    m = jnp.arange(BAND)[None, :]
    bias = rel_bias(rel_table, A_WINDOW + i - m)
    valid = (jnp.arange(nc)[:, None] - LEFT_CHUNKS + jnp.arange(BAND)[None, :] // CHUNK) >= 0
    sc = jnp.einsum('bcihd,bcmhd->bhcim', qc, kb).astype(jnp.float32) * ATTN_SCALE
    sc = sc + bias[None, :, None, :, :]
    sc = jnp.where(valid[None, None, :, None, :], sc, NEG_INF)
    p = jax.nn.softmax(sc, axis=-1).astype(v.dtype)
    o = jnp.einsum('bhcim,bcmhd->bcihd', p, vb)
    return o.reshape(b, s, N_HEADS, HEAD_DIM)


def chunk_attn_sample(q, k, v, cache_k, cache_v, rel_table):
    n_cache = cache_k.shape[1]
    L = q.shape[1]
    kk = jnp.concatenate([cache_k, k], axis=1)
    vv = jnp.concatenate([cache_v, v], axis=1)
    i = jnp.arange(L)[:, None]
    m = jnp.arange(n_cache + L)[None, :]
    bias = rel_bias(rel_table, n_cache + i - m)
    sc = jnp.einsum('bihd,bmhd->bhim', q, kk).astype(jnp.float32) * ATTN_SCALE + bias[None]
    p = jax.nn.softmax(sc, axis=-1).astype(v.dtype)
    return jnp.einsum('bhim,bmhd->bihd', p, vv)


def _fox_block(q_blk, cum_q, q_pos, k, v, cum_k, k_pos):
    sc = jnp.einsum('bqhd,bkhd->bhqk', q_blk, k).astype(jnp.float32) * ATTN_SCALE
    sc = sc + jnp.swapaxes(cum_q, 1, 2)[..., :, None] - jnp.swapaxes(cum_k, 1, 2)[..., None, :]
    sc = jnp.where(k_pos[None, :] <= q_pos[:, None], sc, NEG_INF)
    p = jax.nn.softmax(sc, axis=-1).astype(v.dtype)
    return jnp.einsum('bhqk,bkhd->bqhd', p, v)


def fox_prompt(q, k, v, log_f):
    b, s = q.shape[:2]
    nb = s // Q_BLOCK
    cum = jnp.cumsum(log_f, axis=1)
    pos = jnp.arange(s)
    qb = jnp.swapaxes(q.reshape(b, nb, Q_BLOCK, N_HEADS, HEAD_DIM), 0, 1)
    cb = jnp.swapaxes(cum.reshape(b, nb, Q_BLOCK, N_HEADS), 0, 1)
    pb = pos.reshape(nb, Q_BLOCK)
    out = lax.map(lambda a: _fox_block(a[0], a[1], a[2], k, v, cum, pos), (qb, cb, pb))
    return jnp.swapaxes(out, 0, 1).reshape(b, s, N_HEADS, HEAD_DIM)


def fox_sample(q, k, v, log_f, cache_k, cache_v, cache_log_f):
    past = cache_k.shape[1]
    L = q.shape[1]
    kk = jnp.concatenate([cache_k, k], axis=1)
    vv = jnp.concatenate([cache_v, v], axis=1)
    cum = jnp.cumsum(jnp.concatenate([cache_log_f.astype(jnp.float32), log_f], axis=1), axis=1)
    return _fox_block(q, cum[:, past:], past + jnp.arange(L), kk, vv, cum, jnp.arange(past + L))


def _stick_block(q_blk, q_pos, k, v, k_pos):
    z = jnp.einsum('bqhd,bkhd->bhqk', q_blk, k).astype(jnp.float32) * ATTN_SCALE
    earlier = k_pos[None, :] < q_pos[:, None]
    log_beta = jax.nn.log_sigmoid(z)
    log_keep = jnp.where(earlier, log_beta - z, 0.0)
    tail = lax.cumsum(log_keep, axis=3, reverse=True) - log_keep
    w = jnp.where(earlier, jnp.exp(log_beta + tail), 0.0)
    return jnp.einsum('bhqk,bkhd->bqhd', w.astype(v.dtype), v)


def stick_prompt(q, k, v):
    b, s = q.shape[:2]
    nb = s // Q_BLOCK
    pos = jnp.arange(s)
    qb = jnp.swapaxes(q.reshape(b, nb, Q_BLOCK, N_HEADS, HEAD_DIM), 0, 1)
    pb = pos.reshape(nb, Q_BLOCK)
    out = lax.map(lambda a: _stick_block(a[0], a[1], k, v, pos), (qb, pb))
    return jnp.swapaxes(out, 0, 1).reshape(b, s, N_HEADS, HEAD_DIM)


def stick_sample(q, k, v, cache_k, cache_v):
    past = cache_k.shape[1]
    L = q.shape[1]
    kk = jnp.concatenate([cache_k, k], axis=1)
    vv = jnp.concatenate([cache_v, v], axis=1)
    return _stick_block(q, past + jnp.arange(L), kk, vv, jnp.arange(past + L))


def setup_inputs(seed: int = 0) -> dict:
    key = jax.random.key(seed)
    ks = jax.random.split(key, 32)
    d = D_MODEL

    def nrm(k, shape, scale):
        return jax.random.normal(k, shape, jnp.float32) * scale

    def gain(k, shape):
        return 1.0 + 0.05 * jax.random.normal(k, shape, jnp.float32)

    return {
        'x_prompt': nrm(ks[0], (BATCH, SEQ, d), 1.0),
        'x_sample': nrm(ks[1], (DEC_BATCH, DEC_SEQ, d), 1.0),
        'cache_a_k': nrm(ks[2], (N_A, DEC_BATCH, A_CACHE, N_HEADS, HEAD_DIM), 1.0),
        'cache_a_v': nrm(ks[3], (N_A, DEC_BATCH, A_CACHE, N_HEADS, HEAD_DIM), 1.0),
        'cache_b_k': nrm(ks[4], (N_B, DEC_BATCH, PAST_LEN, N_HEADS, HEAD_DIM), 1.0),
        'cache_b_v': nrm(ks[5], (N_B, DEC_BATCH, PAST_LEN, N_HEADS, HEAD_DIM), 1.0),
        'cache_b_logf': jax.nn.log_sigmoid(FORGET_BIAS_INIT + nrm(ks[6], (N_B, DEC_BATCH, PAST_LEN, N_HEADS), 1.0)),
        'cache_c_k': nrm(ks[7], (N_C, DEC_BATCH, PAST_LEN, N_HEADS, HEAD_DIM), 1.0),
        'cache_c_v': nrm(ks[8], (N_C, DEC_BATCH, PAST_LEN, N_HEADS, HEAD_DIM), 1.0),
        'norm_ffn1': gain(ks[9], (DEPTH, d)),
        'ffn1_gate': nrm(ks[10], (DEPTH, d, D_FF), d ** -0.5),
        'ffn1_up': nrm(ks[11], (DEPTH, d, D_FF), d ** -0.5),
        'ffn1_down': nrm(ks[12], (DEPTH, D_FF, d), D_FF ** -0.5),
        'norm_mix': gain(ks[13], (DEPTH, d)),
        'norm_ffn2': gain(ks[14], (DEPTH, d)),
        'ffn2_gate': nrm(ks[15], (DEPTH, d, D_FF), d ** -0.5),
        'ffn2_up': nrm(ks[16], (DEPTH, d, D_FF), d ** -0.5),
        'ffn2_down': nrm(ks[17], (DEPTH, D_FF, d), D_FF ** -0.5),
        'a_w_qkv': nrm(ks[18], (N_A, d, 3 * d), d ** -0.5),
        'a_w_o': nrm(ks[19], (N_A, d, d), d ** -0.5),
        'a_rel_bias': nrm(ks[20], (N_A, N_REL, N_HEADS), 0.5),
        'b_w_qkv': nrm(ks[21], (N_B, d, 3 * d), d ** -0.5),
        'b_w_o': nrm(ks[22], (N_B, d, d), d ** -0.5),
        'b_w_f': nrm(ks[23], (N_B, d, N_HEADS), d ** -0.5),
        'b_b_f': FORGET_BIAS_INIT + nrm(ks[24], (N_B, N_HEADS), 0.1),
        'c_w_qkv': nrm(ks[25], (N_C, d, 3 * d), d ** -0.5),
        'c_w_o': nrm(ks[26], (N_C, d, d), d ** -0.5),
        'norm_final': gain(ks[27], (d,)),
    }


def reference(x_prompt, x_sample, cache_a_k, cache_a_v, cache_b_k, cache_b_v, cache_b_logf,
              cache_c_k, cache_c_v, norm_ffn1, ffn1_gate, ffn1_up, ffn1_down, norm_mix,
              norm_ffn2, ffn2_gate, ffn2_up, ffn2_down, a_w_qkv, a_w_o, a_rel_bias,
              b_w_qkv, b_w_o, b_w_f, b_b_f, c_w_qkv, c_w_o, norm_final):
    xp, xs = x_prompt, x_sample
    a_kp, a_vp, a_ks, a_vs = [], [], [], []
    b_kp, b_vp, b_fp, b_ks, b_vs, b_fs = [], [], [], [], [], []
    c_kp, c_vp, c_ks, c_vs = [], [], [], []
    for i in range(DEPTH):
        kind, slot = i % N_MIXERS, i // N_MIXERS
        xp = xp + 0.5 * swiglu(rms_norm(xp, norm_ffn1[i]), ffn1_gate[i], ffn1_up[i], ffn1_down[i])
        xs = xs + 0.5 * swiglu(rms_norm(xs, norm_ffn1[i]), ffn1_gate[i], ffn1_up[i], ffn1_down[i])
        hp = rms_norm(xp, norm_mix[i])
        hs = rms_norm(xs, norm_mix[i])
        if kind == 0:
            qp, kp, vp = project_qkv(hp, a_w_qkv[slot])
            qs, ks_, vs = project_qkv(hs, a_w_qkv[slot])
            op = chunk_attn_prompt(qp, kp, vp, a_rel_bias[slot])
            os_ = chunk_attn_sample(qs, ks_, vs, cache_a_k[slot], cache_a_v[slot], a_rel_bias[slot])
            a_kp.append(kp[:, kp.shape[1] - A_KEEP:])
            a_vp.append(vp[:, vp.shape[1] - A_KEEP:])
            a_ks.append(ks_)
            a_vs.append(vs)
            w_o = a_w_o[slot]
        elif kind == 1:
            qp, kp, vp = project_qkv(hp, b_w_qkv[slot])
            qs, ks_, vs = project_qkv(hs, b_w_qkv[slot])
            fp = jax.nn.log_sigmoid((hp @ b_w_f[slot] + b_b_f[slot]).astype(jnp.float32))
            fs = jax.nn.log_sigmoid((hs @ b_w_f[slot] + b_b_f[slot]).astype(jnp.float32))
            op = fox_prompt(qp, kp, vp, fp)
            os_ = fox_sample(qs, ks_, vs, fs, cache_b_k[slot], cache_b_v[slot], cache_b_logf[slot])
            b_kp.append(kp)
            b_vp.append(vp)
            b_fp.append(fp)
            b_ks.append(ks_)
            b_vs.append(vs)
            b_fs.append(fs)
            w_o = b_w_o[slot]
        else:
            qp, kp, vp = project_qkv(hp, c_w_qkv[slot])
            qs, ks_, vs = project_qkv(hs, c_w_qkv[slot])
            op = stick_prompt(qp, kp, vp)
            os_ = stick_sample(qs, ks_, vs, cache_c_k[slot], cache_c_v[slot])
            c_kp.append(kp)
            c_vp.append(vp)
            c_ks.append(ks_)
            c_vs.append(vs)
            w_o = c_w_o[slot]
        xp = xp + op.reshape(xp.shape) @ w_o
        xs = xs + os_.reshape(xs.shape) @ w_o
        xp = xp + 0.5 * swiglu(rms_norm(xp, norm_ffn2[i]), ffn2_gate[i], ffn2_up[i], ffn2_down[i])
        xs = xs + 0.5 * swiglu(rms_norm(xs, norm_ffn2[i]), ffn2_gate[i], ffn2_up[i], ffn2_down[i])
    y_prompt = rms_norm(xp, norm_final)
    y_sample = rms_norm(xs, norm_final)
    return (y_prompt, y_sample,
            jnp.stack(a_kp), jnp.stack(a_vp), jnp.stack(a_ks), jnp.stack(a_vs),
            jnp.stack(b_kp), jnp.stack(b_vp), jnp.stack(b_fp),
            jnp.stack(b_ks), jnp.stack(b_vs), jnp.stack(b_fs),
            jnp.stack(c_kp), jnp.stack(c_vp), jnp.stack(c_ks), jnp.stack(c_vs))
```

```python
import numpy as np
import concourse.bass as bass
import concourse.mybir as mybir
from concourse.bass_utils import run_bass_kernel_spmd
from contextlib import ExitStack

F32 = mybir.dt.float32
BF16 = mybir.dt.bfloat16
U8 = mybir.dt.uint8
AF = mybir.ActivationFunctionType
ALU = mybir.AluOpType

D = 1024
NCH = 8
TP = 2048
NS = 4
LS = 16
TS = NS * LS
TT = TP + TS
DFF = 2816
NFC = 22
DEPTH = 4
NH = 16
HD = 64
EPS = 1e-6
TTILES = [(0, 512), (512, 512), (1024, 512), (1536, 512), (2048, 64)]
NEG = -30000.0

ENGS = ["pe", "act", "dve", "pool", "sp"]


class Buf:
    __slots__ = ("name", "w", "r", "excl")

    def __init__(self, name, excl=False):
        self.name = name
        self.w = None
        self.r = []
        self.excl = excl


class Sched:
    def __init__(self, nc, stack):
        self.nc = nc
        self.stack = stack
        self.q = {e: [] for e in ENGS}
        self.cnt = {}
        self.semh = {}
        self.epoch = 0
        self.known = {e: {} for e in ENGS}
        self.nsem = 0

    def _key_init(self, key):
        if key not in self.cnt:
            self.cnt[key] = 0
            self.semh[key] = self.stack.enter_context(self.nc.semaphore("s%d" % self.nsem))
            self.nsem += 1

    def pkey(self, eng):
        key = ("p", eng, self.epoch)
        self._key_init(key)
        return key

    def _deps(self, eng, reads, writes, extra):
        deps = {}

        def add(tok, same_ok):
            if tok is None:
                return
            key, val = tok
            if key[0] == "p" and key[1] == eng and not same_ok:
                return
            if deps.get(key, 0) < val:
                deps[key] = val

        same = eng != "pe"
        for b in reads:
            add(b.w, same)
            if b.excl:
                for t in b.r:
                    add(t, False)
        for b in writes:
            add(b.w, same)
            for t in b.r:
                add(t, same)
        for t in extra:
            add(t, True)
        out = []
        kn = self.known[eng]
        for key, val in deps.items():
            if kn.get(key, 0) >= val:
                continue
            kn[key] = val
            out.append((key, val))
        return out

    def _post(self, tok, reads, writes):
        for b in writes:
            b.w = tok
            b.r = []
        for b in reads:
            b.r.append(tok)

    def op(self, eng, fn, reads=(), writes=(), extra=()):
        waits = self._deps(eng, reads, writes, extra)
        key = self.pkey(eng)
        self.cnt[key] += 1
        tok = (key, self.cnt[key])
        self.q[eng].append((fn, waits, (key, 1)))
        self._post(tok, reads, writes)
        return tok

    def pe_group(self, fns, reads=(), writes=(), extra=()):
        waits = self._deps("pe", reads, writes, extra)
        key = self.pkey("pe")
        self.cnt[key] += 1
        tok = (key, self.cnt[key])
        n = len(fns)
        for i, fn in enumerate(fns):
            self.q["pe"].append((fn, waits if i == 0 else [], (key, 1) if i == n - 1 else None))
        self._post(tok, reads, writes)
        return tok

    def dma(self, queue, semname, out, in_, reads=(), writes=(), extra=()):
        waits = self._deps(queue, reads, writes, extra)
        key = ("d", semname)
        self._key_init(key)
        self.cnt[key] += 16
        tok = (key, self.cnt[key])

        def fn(eng, out=out, in_=in_):
            return eng.dma_start(out=out, in_=in_)

        self.q[queue].append((fn, waits, (key, 16)))
        self._post(tok, reads, writes)
        return tok

    def barrier(self):
        toks = [(k, v) for k, v in self.cnt.items() if v > 0]
        for e in ENGS:
            waits = []
            kn = self.known[e]
            for key, val in toks:
                if key[0] == "p" and key[1] == e:
                    continue
                if kn.get(key, 0) >= val:
                    continue
                kn[key] = val
                waits.append((key, val))
            if waits:
                self.q[e].append((None, waits, None))

    def new_epoch(self):
        self.epoch += 1

    def final_wait(self, eng="sp"):
        waits = [(k, v) for k, v in self.cnt.items() if v > 0 and k[0] == "d"]
        self.q[eng].append((None, waits, None))

    def replay(self):
        nc = self.nc
        semh = self.semh

        def run(eng, lst):
            for fn, waits, inc in lst:
                for key, val in waits:
                    eng.wait_ge(semh[key], val)
                if fn is not None:
                    ins = fn(eng)
                    if inc is not None:
                        ins.then_inc(semh[inc[0]], inc[1])

        with nc.Block() as block:
            @block.tensor
            def _(e):
                run(e, self.q["pe"])

            @block.scalar
            def _(e):
                run(e, self.q["act"])

            @block.vector
            def _(e):
                run(e, self.q["dve"])

            @block.gpsimd
            def _(e):
                run(e, self.q["pool"])

            @block.sync
            def _(e):
                run(e, self.q["sp"])


def _consts_np():
    c = np.zeros((128, 9, 128), np.float32)
    k = np.arange(128)[:, None]
    q = np.arange(128)[None, :]
    c[:, 0, :] = np.eye(128, dtype=np.float32)
    c[:, 1, :] = 1.0
    c[:, 2, :] = np.where(k > q, NEG, 0.0)
    c[:, 3, :] = np.where(k >= q, NEG, 0.0)
    c[:, 4, :] = np.where(k >= q, -1.0, 0.0)
    c[:, 5, :] = -1.0
    c[:, 6, :] = np.where(k + q == 127, 1.0, 0.0)
    c[:16, 7, :16] = np.where(k[:16] + q[:, :16] == 15, 1.0, 0.0)
    c[:, 8, :] = np.where(np.abs(k - q) == 64, 1.0, 0.0)
    return np.eye(128, dtype=np.float32), c.reshape(128, 1152)


def build(stop_after=None):
    nc = bass.Bass("TRN2", target_bir_lowering=False)
    st = ExitStack()

    def din(name, shape):
        return nc.dram_tensor(name, list(shape), F32, kind="ExternalInput").ap()

    def dout(name, shape):
        return nc.dram_tensor(name, list(shape), F32, kind="ExternalOutput").ap()

    x_p = din("x_p", (TP, D))
    x_s = din("x_s", (TS, D))
    consts_f = din("consts_f", (128, 128))
    consts_b = din("consts_b", (128, 1152))
    norm_ffn1 = din("norm_ffn1", (DEPTH, D))
    norm_mix = din("norm_mix", (DEPTH, D))
    norm_ffn2 = din("norm_ffn2", (DEPTH, D))
    norm_final = din("norm_final", (1, D))
    ffn_w = {}
    if "skip_ffn" not in DBG:
        for which in (1, 2):
            ffn_w[which] = (din("ffn%d_gate" % which, (DEPTH, D, DFF)), din("ffn%d_up" % which, (DEPTH, D, DFF)),
                            din("ffn%d_down" % which, (DEPTH, DFF, D)))
    ca_k = din("ca_k", (2, NS, 512, D))
    ca_v = din("ca_v", (2, NS, 512, D))
    cb_k = din("cb_k", (NS, 1024, D))
    cb_v = din("cb_v", (NS, 1024, D))
    cb_f = din("cb_f", (NS, 1024, NH))
    cc_k = din("cc_k", (NS, 1024, D))
    cc_v = din("cc_v", (NS, 1024, D))
    a_qkv = din("a_w_qkv", (2, D, 3 * D))
    a_o = din("a_w_o", (2, D, D))
    a_rel = din("a_rel_bias", (2, 513, NH))
    b_qkv = din("b_w_qkv", (1, D, 3 * D))
    b_o = din("b_w_o", (1, D, D))
    b_wf = din("b_w_f", (1, D, NH))
    b_bf = din("b_b_f", (1, NH))
    c_qkv = din("c_w_qkv", (1, D, 3 * D))
    c_o = din("c_w_o", (1, D, D))
    y_p = dout("y_p", (TP, D))
    y_s = dout("y_s", (TS, D))
    o_a_kp = dout("a_kp", (2, 512, D))
    o_a_vp = dout("a_vp", (2, 512, D))
    o_a_ks = dout("a_ks", (2, TS, D))
    o_a_vs = dout("a_vs", (2, TS, D))
    o_b_kp = dout("b_kp", (TP, D))
    o_b_vp = dout("b_vp", (TP, D))
    o_b_fp = dout("b_fp", (TP, NH))
    o_b_ks = dout("b_ks", (TS, D))
    o_b_vs = dout("b_vs", (TS, D))
    o_b_fs = dout("b_fs", (TS, NH))
    o_c_kp = dout("c_kp", (TP, D))
    o_c_vp = dout("c_vp", (TP, D))
    o_c_ks = dout("c_ks", (TS, D))
    o_c_vs = dout("c_vs", (TS, D))
    EH = nc.dram_tensor("eh_scratch", [NH * 768], F32)

    S = Sched(nc, st)

    X = st.enter_context(nc.sbuf_tensor("X", [128, NCH, TT], F32))
    HU = st.enter_context(nc.sbuf_tensor("HU", [128, 2 * NCH * TT * 2], U8))
    Wr = st.enter_context(nc.sbuf_tensor("Wr", [128, 34816], U8))
    SQr = st.enter_context(nc.sbuf_tensor("SQr", [128, 8192], U8))
    RSr = st.enter_context(nc.sbuf_tensor("RSr", [128, 4096], U8))
    IDF = st.enter_context(nc.sbuf_tensor("IDF", [128, 128], F32))
    CB = st.enter_context(nc.sbuf_tensor("CB", [128, 9, 128], BF16))
    GAIN = st.enter_context(nc.sbuf_tensor("GAIN", [128, 13 * NCH], F32))
    QS = st.enter_context(nc.sbuf_tensor("QS", [128, NS, 2, LS], BF16))
    ESZ = 26752
    Er = st.enter_context(nc.sbuf_tensor("Er", [128, ESZ], U8))

    def view(raw, a, b, dt, pat=None, **kw):
        v = raw[:, a:b].bitcast(dt)
        if pat is not None:
            v = v.rearrange(pat, **kw)
        return v

    HB = NCH * TT * 2
    H = view(HU, 0, HB, BF16, "p (c t) -> p c t", c=NCH)
    ACT = view(HU, HB, 2 * HB, BF16, "p (c t) -> p c t", c=NCH)
    OT = ACT
    YT = view(HU, 0, 2 * HB, F32, "p (c t) -> p c t", c=NCH)
    SQ = view(SQr, 0, 8192, BF16, "p (c t) -> p c t", c=NCH)
    GST = SQr[0:104, 0:512].bitcast(F32)
    RS = view(RSr, 0, 4096, F32, "p (s t) -> p s t", s=2)
    WGU = view(Wr, 0, 16384, BF16, "p (s g c f) -> p s g c f", s=4, g=2, c=NCH)
    WD = view(Wr, 16384, 32768, BF16, "p (s f) -> p s f", s=8)
    SG = view(Wr, 32768, 34816, BF16, "p (s f) -> p s f", s=2)
    XIN = view(Wr, 0, 16384, F32, "p (s f) -> p s f", s=4)
    WQKV = view(Wr, 0, 12288, BF16, "p (s m c f) -> p s m c f", s=2, m=3, c=NCH)
    PT = view(Er, 0, 4096, BF16, "p (s f) -> p s f", s=4)
    KST = view(Er, 4096, 6144, F32, "p (s j f) -> p s j f", s=2, j=2)
    VST = view(Er, 6144, 8192, F32, "p (s j f) -> p s j f", s=2, j=2)
    KTC = view(Er, 8192, 12288, BF16, "p (s f) -> p s f", s=2)
    MV = {}

    def set_views(kind):
        MV.clear()
        MV["WO"] = view(Wr, 12288, 16384, BF16, "p (s c f) -> p s c f", s=2, c=NCH)
        MV["QT0"] = view(Wr, 16384, 20608, BF16)
        MV["KT"] = view(Wr, 20608, 24832, BF16)
        if kind != 2:
            vw = 192
            MV["VB"] = view(Wr, 24832, 31360, BF16, "p (t f) -> p t f", t=17)
            MV["BS"] = view(Wr, 31360, 31680, BF16, "p (j h i) -> p j h i", j=5, h=2)
            MV["NBP"] = view(Wr, 31360, 33920, F32, "p (t h) -> p t h", h=NH)
            MV["VCC"] = view(Er, 12288, 18432, BF16, "p (s j f) -> p s j f", s=2, j=8)
            MV["QT1"] = view(Er, 18432, 22656, BF16)
            MV["BIAS"] = view(Er, 22656, 25216, BF16, "p (h f) -> p h f", h=2)
            MV["NBS"] = view(Er, 22656, 24960, F32, "p (s j h) -> p s j h", s=NS, j=9)
            MV["VS"] = view(Er, 25216, 26752, BF16, "p (s f) -> p s f", s=NS)
        else:
            vw = 128
            MV["VB"] = view(Wr, 24832, 29184, BF16, "p (t f) -> p t f", t=17)
            MV["VS"] = view(Wr, 29184, 30208, BF16, "p (s f) -> p s f", s=NS)
            MV["VCC"] = view(Er, 12288, 16384, BF16, "p (s j f) -> p s j f", s=2, j=8)
            MV["QT1"] = view(Er, 16384, 20608, BF16)
        MV["vw"] = vw
        MV["v1"] = vw - 64

    LB = view(Er, 20608, 22656, BF16, "p (s f) -> p s f", s=2)
    ACC = view(Er, 22656, 26752, BF16, "p (s f) -> p s f", s=4)
    EF = view(SQr, 0, 4096, F32, "p (s f) -> p s f", s=2)
    KCS = view(SQr, 4096, 8192, F32, "p (j f) -> p j f", j=8)
    RCP = RS
    def tview(a, b, dt, pat=None, **kw):
        return view(HU, HB + a, HB + b, dt, pat, **kw)

    ident_f = IDF[:, :]
    ident_b = CB[:, 0, :]
    ones_b = CB[:, 1, :]
    mask_le = CB[:, 2, :]
    mask_lt = CB[:, 3, :]
    ntri_b = CB[:, 4, :]
    nones_b = CB[:, 5, :]
    anti_b = CB[:, 6, :]
    anti16_b = CB[:, 7, :]
    swap_b = CB[:, 8, :]

    banks = [st.enter_context(nc.psum_tensor("pb%d" % i, [128, 512], F32)) for i in range(8)]
    bank_buf = [Buf("bank%d" % i, excl=True) for i in range(8)]
    ring = {"s": [0, 1, 2, 3], "a": [4, 5, 6, 7]}
    ring_pos = {"s": 0, "a": 0}

    def bank(pool):
        i = ring[pool][ring_pos[pool] % len(ring[pool])]
        ring_pos[pool] += 1
        return banks[i], bank_buf[i]

    NTT = len(TTILES)
    x_buf = [[Buf("x%d_%d" % (t, c)) for c in range(NCH)] for t in range(NTT)]
    h_buf = [Buf("h%d" % t) for t in range(NTT)]
    act_buf = [[Buf("a%d_%d" % (i, t)) for t in range(NTT)] for i in range(8)]
    ot_buf = [[Buf("ot%d_%d" % (c, t)) for t in range(NTT)] for c in range(NCH)]
    wgu_buf = [Buf("wgu%d" % i) for i in range(4)]
    wd_buf = [Buf("wd%d" % i) for i in range(8)]
    sg_buf = [Buf("sg%d" % i) for i in range(2)]
    xin_buf = [Buf("xin%d" % i) for i in range(4)]
    sq_buf = Buf("sq")
    rs_buf = [Buf("rs0"), Buf("rs1")]
    idf_buf = Buf("idf")
    cb_buf = Buf("cb")
    gain_buf = Buf("gain")
    gst_buf = Buf("gst")
    wqkv_buf = [Buf("wqkv%d" % i) for i in range(2)]
    wo_buf = [Buf("wo%d" % i) for i in range(2)]
    qt_buf = Buf("qt")
    qs_buf = Buf("qs")
    kt_buf = Buf("kt")
    vb_buf = Buf("vb")
    vs_buf = Buf("vs")
    pt_buf = [Buf("pt%d" % i) for i in range(4)]
    kst_buf = [Buf("kst0"), Buf("kst1")]
    vst_buf = [Buf("vst0"), Buf("vst1")]
    ktc_buf = [Buf("ktc%d" % i) for i in range(2)]
    vcc_buf = [Buf("vcc%d" % i) for i in range(2)]
    kcs_buf = Buf("kcs")
    bias_buf = Buf("bias")
    bs_buf = Buf("bs")
    nb_buf = Buf("nb")
    rcp_buf = [Buf("rcp0"), Buf("rcp1")]
    ef_buf = [Buf("ef0"), Buf("ef1")]
    lb_buf = [Buf("lb0"), Buf("lb1")]
    acc_buf = [Buf("acc%d" % i) for i in range(4)]
    tr_buf = Buf("transient")
    cnt = {"wgu": 0, "sg": 0, "xin": 0, "wqkv": 0, "wo": 0, "pt": 0, "kst": 0, "vst": 0, "kvc": 0, "rcp": 0,
           "ef": 0, "lb": 0, "accs": 0}

    S.dma("sp", "const", IDF[:, :], consts_f, writes=[idf_buf])
    S.dma("pool", "constb", CB[:, :, :].rearrange("p a b -> p (a b)"), consts_b, writes=[cb_buf])
    for i, g in enumerate([norm_ffn1, norm_mix, norm_ffn2]):
        S.dma("sp", "const", GST[i * 32:(i + 1) * 32, :], g.rearrange("l (c p) -> (l c) p", p=128), writes=[gst_buf])
    S.dma("sp", "const", GST[96:104, :], norm_final.rearrange("l (c p) -> (l c) p", p=128), writes=[gst_buf])
    pb, pbb = bank("s")
    S.pe_group([lambda e, pb=pb: e.transpose(out=pb[:, 0:104], in_=GST[:, :], identity=ident_f[0:104, 0:104])],
               reads=[gst_buf, idf_buf], writes=[pbb])
    S.op("dve", lambda e, pb=pb: e.tensor_copy(out=GAIN[:, :], in_=pb[:, 0:104]), reads=[pbb], writes=[gain_buf])

    def gain_col(kind, l, c):
        j = {0: 0, 1: 32, 2: 64, 3: 96}[kind] + (l * 8 if kind < 3 else 0) + c
        return GAIN[:, j:j + 1]

    def load_x():
        ntile = TP // 128 + 1
        for t in range(ntile):
            rows = 128 if t < TP // 128 else TS
            src = x_p[t * 128:(t + 1) * 128, :] if t < TP // 128 else x_s[:, :]
            sl = cnt["xin"] % 4
            cnt["xin"] += 1
            S.dma("sp", "xin%d" % sl, XIN[0:rows, sl, :], src, writes=[xin_buf[sl]])
            tt = min(t // 4, 4)
            for half in range(2):
                pb, pbb = bank("s")
                fns = []
                for j in range(4):
                    c = half * 4 + j
                    fns.append(lambda e, pb=pb, j=j, c=c, sl=sl, rows=rows: e.transpose(
                        out=pb[:, j * 128:j * 128 + rows], in_=XIN[0:rows, sl, c * 128:(c + 1) * 128],
                        identity=ident_f[0:rows, 0:rows]))
                S.pe_group(fns, reads=[xin_buf[sl], idf_buf], writes=[pbb])
                dst = X[:, half * 4:half * 4 + 4, t * 128:t * 128 + rows]
                srcp = pb[:, :].rearrange("p (j k) -> p j k", j=4)[:, :, 0:rows]
                wl = [x_buf[tt][half * 4 + j] for j in range(4)]
                if half == 0:
                    S.op("act", lambda e, dst=dst, srcp=srcp: e.copy(out=dst, in_=srcp), reads=[pbb], writes=wl)
                else:
                    S.op("dve", lambda e, dst=dst, srcp=srcp: e.tensor_copy(out=dst, in_=srcp), reads=[pbb], writes=wl)

    def norm(kind, l, dst, dst_bufs):
        for tt, (t0, n) in enumerate(TTILES):
            S.op("act", lambda e, t0=t0, n=n: e.activation(out=SQ[:, :, 0:n], in_=X[:, :, t0:t0 + n], func=AF.Square),
                 reads=x_buf[tt], writes=[sq_buf])
            pb, pbb = bank("s")
            fns = [lambda e, pb=pb, c=c, n=n: e.matmul(pb[:, 0:n], lhsT=ones_b, rhs=SQ[:, c, 0:n],
                                                       start=(c == 0), stop=(c == NCH - 1)) for c in range(NCH)]
            S.pe_group(fns, reads=[sq_buf, cb_buf], writes=[pbb])
            S.op("act", lambda e, pb=pb, n=n: e.activation(out=RS[:, 0, 0:n], in_=pb[:, 0:n], func=AF.Ln,
                                                           bias=EPS, scale=1.0 / D),
                 reads=[pbb], writes=[rs_buf[0]])
            S.op("act", lambda e, n=n: e.activation(out=RS[:, 1, 0:n], in_=RS[:, 0, 0:n], func=AF.Exp, scale=-0.5),
                 reads=[rs_buf[0]], writes=[rs_buf[1]])
            for c in range(NCH):
                S.op("dve", lambda e, c=c, t0=t0, n=n: e.scalar_tensor_tensor(
                    out=dst[:, c, t0:t0 + n], in0=X[:, c, t0:t0 + n], scalar=gain_col(kind, l, c),
                    in1=RS[:, 1, 0:n], op0=ALU.mult, op1=ALU.mult),
                    reads=[x_buf[tt][c], rs_buf[1], gain_buf], writes=[dst_bufs[tt]])

    FGROUPS = [list(range(0, 8)), list(range(8, 15)), list(range(15, 22))]

    def ffn(l, which):
        if "skip_ffn" in DBG:
            return
        wg_d, wu_d, wd_d = ffn_w[which]
        norm(0 if which == 1 else 2, l, H, h_buf)
        t3, n3 = TTILES[3]
        t4, n4 = TTILES[4]
        for grp in FGROUPS:
            for i, fc in enumerate(grp):
                sl = cnt["wgu"] % 4
                cnt["wgu"] += 1
                S.dma("pool", "wg%d" % sl, WGU[:, sl, 0, :, :],
                      wg_d[l].rearrange("(c p) f -> p c f", p=128)[:, :, fc * 128:(fc + 1) * 128], writes=[wgu_buf[sl]])
                S.dma("pool", "wu%d" % sl, WGU[:, sl, 1, :, :],
                      wu_d[l].rearrange("(c p) f -> p c f", p=128)[:, :, fc * 128:(fc + 1) * 128], writes=[wgu_buf[sl]])
                S.dma("pool", "wd%d" % i, WD[:, i, :], wd_d[l][fc * 128:(fc + 1) * 128, :], writes=[wd_buf[i]])

                def evac(pg, pgb, pu, pub, tt, t0, n, i=i):
                    ss = cnt["sg"] % 2
                    cnt["sg"] += 1
                    S.op("act", lambda e, pg=pg, ss=ss, n=n: e.activation(out=SG[:, ss, 0:n], in_=pg[:, 0:n], func=AF.Silu),
                         reads=[pgb], writes=[sg_buf[ss]])
                    S.op("dve", lambda e, pu=pu, ss=ss, i=i, t0=t0, n=n: e.tensor_tensor(
                        out=ACT[:, i, t0:t0 + n], in0=pu[:, 0:n], in1=SG[:, ss, 0:n], op=ALU.mult),
                        reads=[pub, sg_buf[ss]], writes=[act_buf[i][tt]])

                for tt, (t0, n) in enumerate(TTILES[0:3]):
                    pg, pgb = bank("s")
                    pu, pub = bank("s")
                    for gi, (pp, ppb) in enumerate(((pg, pgb), (pu, pub))):
                        fns = [lambda e, pp=pp, gi=gi, c=c, sl=sl, t0=t0, n=n: e.matmul(
                            pp[:, 0:n], lhsT=WGU[:, sl, gi, c, :], rhs=H[:, c, t0:t0 + n],
                            start=(c == 0), stop=(c == NCH - 1)) for c in range(NCH)]
                        S.pe_group(fns, reads=[wgu_buf[sl], h_buf[tt]], writes=[ppb])
                    evac(pg, pgb, pu, pub, tt, t0, n)
                pg, pgb = bank("s")
                pu, pub = bank("s")
                qg, qgb = bank("a")
                qu, qub = bank("a")
                for gi, (pp, ppb, qq, qqb) in enumerate(((pg, pgb, qg, qgb), (pu, pub, qu, qub))):
                    fns = []
                    for c in range(NCH):
                        fns.append(lambda e, pp=pp, gi=gi, c=c, sl=sl: e.matmul(
                            pp[:, 0:n3], lhsT=WGU[:, sl, gi, c, :], rhs=H[:, c, t3:t3 + n3],
                            start=(c == 0), stop=(c == NCH - 1)))
                        fns.append(lambda e, qq=qq, gi=gi, c=c, sl=sl: e.matmul(
                            qq[:, 0:n4], lhsT=WGU[:, sl, gi, c, :], rhs=H[:, c, t4:t4 + n4],
                            start=(c == 0), stop=(c == NCH - 1)))
                    S.pe_group(fns, reads=[wgu_buf[sl], h_buf[3], h_buf[4]], writes=[ppb, qqb])
                evac(pg, pgb, pu, pub, 3, t3, n3)
                evac(qg, qgb, qu, qub, 4, t4, n4)
            ng = len(grp)

            def xupd(po, pob, dp, tt, t0, n):
                S.op("dve", lambda e, po=po, dp=dp, t0=t0, n=n: e.scalar_tensor_tensor(
                    out=X[:, dp, t0:t0 + n], in0=po[:, 0:n], scalar=0.5, in1=X[:, dp, t0:t0 + n],
                    op0=ALU.mult, op1=ALU.add),
                    reads=[pob, x_buf[tt][dp]], writes=[x_buf[tt][dp]])

            for tt, (t0, n) in enumerate(TTILES[0:3]):
                for dp in range(NCH):
                    po, pob = bank("a")
                    fns = [lambda e, po=po, i=i, dp=dp, t0=t0, n=n, ng=ng: e.matmul(
                        po[:, 0:n], lhsT=WD[:, i, dp * 128:(dp + 1) * 128], rhs=ACT[:, i, t0:t0 + n],
                        start=(i == 0), stop=(i == ng - 1)) for i in range(ng)]
                    S.pe_group(fns, reads=[wd_buf[i] for i in range(ng)] + [act_buf[i][tt] for i in range(ng)],
                               writes=[pob])
                    xupd(po, pob, dp, tt, t0, n)
            for dp in range(NCH):
                po, pob = bank("a")
                qo, qob = bank("a")
                fns = []
                for i in range(ng):
                    fns.append(lambda e, po=po, i=i, dp=dp, ng=ng: e.matmul(
                        po[:, 0:n3], lhsT=WD[:, i, dp * 128:(dp + 1) * 128], rhs=ACT[:, i, t3:t3 + n3],
                        start=(i == 0), stop=(i == ng - 1)))
                    fns.append(lambda e, qo=qo, i=i, dp=dp, ng=ng: e.matmul(
                        qo[:, 0:n4], lhsT=WD[:, i, dp * 128:(dp + 1) * 128], rhs=ACT[:, i, t4:t4 + n4],
                        start=(i == 0), stop=(i == ng - 1)))
                S.pe_group(fns, reads=[wd_buf[i] for i in range(ng)] + [act_buf[i][3] for i in range(ng)]
                           + [act_buf[i][4] for i in range(ng)], writes=[pob, qob])
                xupd(po, pob, dp, 3, t3, n3)
                xupd(qo, qob, dp, 4, t4, n4)

    def mm(out, lhsT, rhs, start, stop):
        return lambda e: e.matmul(out, lhsT=lhsT, rhs=rhs, start=start, stop=stop, skip_group_check=True)

    def project_pair(kind, slot, c, wqkv_d, outs):
        k_out_p, v_out_p, k_out_s, v_out_s = outs
        QT0, QT1, KT, VB, VS, v1 = MV["QT0"], MV["QT1"], MV["KT"], MV["VB"], MV["VS"], MV["v1"]
        sl = c % 2
        for tt, (t0, n) in enumerate(TTILES):
            for m in range(2):
                pb, pbb = bank("s")
                fns = [mm(pb[:, 0:n], WQKV[:, sl, m, dc, :], H[:, dc, t0:t0 + n], dc == 0, dc == NCH - 1)
                       for dc in range(NCH)]
                S.pe_group(fns, reads=[wqkv_buf[sl], h_buf[tt]], writes=[pbb])
                if m == 0 and tt == 4:
                    for hh in range(2):
                        S.op("act", lambda e, pb=pb, hh=hh: e.activation(
                            out=QS[hh * 64:(hh + 1) * 64, :, hh, :],
                            in_=pb[hh * 64:(hh + 1) * 64, 0:TS].rearrange("p (s i) -> p s i", s=NS),
                            func=AF.Copy, scale=0.125), reads=[pbb], writes=[qs_buf])
                elif m == 0:
                    S.op("act", lambda e, pb=pb, t0=t0, n=n: e.activation(out=QT0[0:64, t0:t0 + n], in_=pb[0:64, 0:n],
                                                                          func=AF.Copy, scale=0.125),
                         reads=[pbb], writes=[qt_buf])
                    S.op("act", lambda e, pb=pb, t0=t0, n=n: e.activation(out=QT1[64:128, t0:t0 + n],
                                                                          in_=pb[64:128, 0:n], func=AF.Copy, scale=0.125),
                         reads=[pbb], writes=[qt_buf])
                else:
                    S.op("dve", lambda e, pb=pb, t0=t0, n=n: e.tensor_copy(out=KT[:, t0:t0 + n], in_=pb[:, 0:n]),
                         reads=[pbb], writes=[kt_buf])
        groups = [[0, 1, 2, 3], [4, 5, 6, 7], [8, 9, 10, 11], [12, 13, 14, 15], [16]]
        for gi, grp in enumerate(groups):
            for m in (2, 1):
                if m == 1 and kind == 0 and gi < 3:
                    continue
                pb, pbb = bank("s")
                fns = []
                for jj, t in enumerate(grp):
                    rows = 128 if t < 16 else TS
                    for dc in range(NCH):
                        fns.append(mm(pb[0:rows, jj * 128:(jj + 1) * 128], H[:, dc, t * 128:t * 128 + rows],
                                      WQKV[:, sl, m, dc, :], dc == 0, dc == NCH - 1))
                rows = 128 if gi < 4 else TS
                ng = len(grp)
                S.pe_group(fns, reads=[wqkv_buf[sl], h_buf[min(gi, 4)]], writes=[pbb])
                psv = pb[0:rows, 0:ng * 128].rearrange("p (j f) -> p j f", j=ng)
                need_out = not (kind == 0 and gi < 3)
                if m == 2:
                    for hh in range(2):
                        S.op("dve", lambda e, psv=psv, rows=rows, grp=grp, ng=ng, hh=hh: e.tensor_copy(
                            out=VB[0:rows, grp[0]:grp[0] + ng, hh * v1:hh * v1 + 64],
                            in_=psv[:, :, hh * 64:(hh + 1) * 64]), reads=[pbb], writes=[vb_buf])
                if need_out:
                    key = "vst" if m == 2 else "kst"
                    stg = VST if m == 2 else KST
                    sbufs = vst_buf if m == 2 else kst_buf
                    halves = [(0, 2), (2, 2)] if gi < 4 else [(0, 1)]
                    for hi_, (j0, nj) in enumerate(halves):
                        ss = cnt[key] % 2
                        cnt[key] += 1
                        eng_ = "act" if hi_ == 0 else "dve"
                        src_ = psv[:, j0:j0 + nj, :]
                        if eng_ == "act":
                            S.op("act", lambda e, src_=src_, rows=rows, nj=nj, stg=stg, ss=ss: e.copy(
                                out=stg[0:rows, ss, 0:nj, :], in_=src_), reads=[pbb], writes=[sbufs[ss]])
                        else:
                            S.op("dve", lambda e, src_=src_, rows=rows, nj=nj, stg=stg, ss=ss: e.tensor_copy(
                                out=stg[0:rows, ss, 0:nj, :], in_=src_), reads=[pbb], writes=[sbufs[ss]])
                        if gi < 4:
                            od = v_out_p if m == 2 else k_out_p
                            r0 = (gi * 512 if kind != 0 else 0) + j0 * 128
                            dst = od[r0:r0 + 256, c * 128:(c + 1) * 128].rearrange("(j p) f -> p j f", p=128)
                            S.dma("sp", "%s%d" % (key, ss), dst, stg[:, ss, :, :], reads=[sbufs[ss]])
                        else:
                            od = v_out_s if m == 2 else k_out_s
                            S.dma("sp", "%s%d" % (key, ss), od[:, c * 128:(c + 1) * 128], stg[0:TS, ss, 0, :],
                                  reads=[sbufs[ss]])
        pb, pbb = bank("s")
        fns = []
        for s in range(NS):
            for dc in range(NCH):
                fns.append(mm(pb[0:LS, s * 128:(s + 1) * 128], H[:, dc, TP + s * LS:TP + (s + 1) * LS],
                              WQKV[:, sl, 2, dc, :], dc == 0, dc == NCH - 1))
        S.pe_group(fns, reads=[wqkv_buf[sl], h_buf[4]], writes=[pbb])
        for hh in range(2):
            S.op("dve", lambda e, pb=pb, hh=hh: e.tensor_copy(
                out=VS[0:LS, :, hh * v1:hh * v1 + 64],
                in_=pb[0:LS, :].rearrange("p (s f) -> p s f", s=NS)[:, :, hh * 64:(hh + 1) * 64]),
                reads=[pbb], writes=[vs_buf])

    def load_wqkv(c, wqkv_d):
        sl = c % 2
        for m in range(3):
            S.dma("pool", "wqkv%d_%d" % (sl, m), WQKV[:, sl, m, :, :],
                  wqkv_d.rearrange("(c p) f -> p c f", p=128)[:, :, m * D + c * 128:m * D + (c + 1) * 128],
                  writes=[wqkv_buf[sl]])

    def issue_cache_dma(s, c, ck, cv, nt, sl):
        S.dma("sp", "kcs", KCS[:, 0:nt, :], ck[s][:, c * 128:(c + 1) * 128].rearrange("(j p) f -> p j f", p=128),
              writes=[kcs_buf])
        VCC, v1 = MV["VCC"], MV["v1"]
        for hh in range(2):
            S.dma("pool", "vcc%d_%d" % (sl, hh), VCC[:, sl, 0:nt, hh * v1:hh * v1 + 64],
                  cv[s][:, c * 128 + hh * 64:c * 128 + (hh + 1) * 64].rearrange("(j p) f -> p j f", p=128),
                  writes=[vcc_buf[sl]])

    def cache_transposes(nt, sl):
        for g in range(nt // 4):
            pb, pbb = bank("s")
            fns = [lambda e, pb=pb, j=j, g=g: e.transpose(out=pb[:, j * 128:(j + 1) * 128], in_=KCS[:, g * 4 + j, :],
                                                          identity=ident_f) for j in range(4)]
            S.pe_group(fns, reads=[kcs_buf, idf_buf], writes=[pbb])
            if g % 2 == 0:
                S.op("act", lambda e, pb=pb, g=g, sl=sl: e.copy(out=KTC[:, sl, g * 512:(g + 1) * 512], in_=pb[:, :]),
                     reads=[pbb], writes=[ktc_buf[sl]])
            else:
                S.op("dve", lambda e, pb=pb, g=g, sl=sl: e.tensor_copy(out=KTC[:, sl, g * 512:(g + 1) * 512], in_=pb[:, :]),
                     reads=[pbb], writes=[ktc_buf[sl]])

    def load_cache_pair(s, c, ck, cv, nt, sl):
        issue_cache_dma(s, c, ck, cv, nt, sl)
        cache_transposes(nt, sl)

    def run_softmax_units(units, LA=2):
        pend = []

        def do_pv(item):
            u, slot = item
            nk, n = u["nk"], u["n"]
            c0 = u["ocol"]
            if "pv" in u:
                fns = [mm(u[key][0][:, c0:c0 + LS], vv, PT[0:nk, slot, hh * LS:(hh + 1) * LS], u["first"], u["last"])
                       for hh, (key, vv) in enumerate(u["pv"])]
                S.pe_group(fns, reads=[pt_buf[slot]] + u["v_reads"], writes=[u["o"][1], u["d"][1]])
            else:
                ob, obuf = u["o"] if u["hh"] == 0 else u["d"]
                fns = [mm(ob[:, c0:c0 + n], u["v"], PT[0:nk, slot, 0:n], u["first"], u["last"])]
                S.pe_group(fns, reads=[pt_buf[slot]] + u["v_reads"], writes=[obuf])
            if u.get("fin") is not None:
                u["fin"]()
            if u.get("post") is not None:
                u["post"]()

        for u in units:
            if u.get("pre") is not None:
                u["pre"]()
            nk, n = u["nk"], u["n"]
            sb, sbb = bank("s")
            nx = len(u["extras"])
            fns = [mm(sb[0:nk, 0:n], u["kT"], u["q"], True, nx == 0)]
            for xi, (xl, xr, xo, xn) in enumerate(u["extras"]):
                fns.append(mm(sb[0:nk, xo:xo + xn], xl, xr, False, xi == nx - 1))
            S.pe_group(fns, reads=u["qk_reads"] + [cb_buf], writes=[sbb])
            slot = cnt["pt"] % 4
            cnt["pt"] += 1
            bias = u.get("bias")
            if "bias2" in u:
                for hh, bb in enumerate(u["bias2"]):
                    S.op("act", lambda e, sb=sb, nk=nk, slot=slot, hh=hh, bb=bb: e.activation(
                        out=PT[0:nk, slot, hh * LS:(hh + 1) * LS], in_=sb[0:nk, hh * LS:(hh + 1) * LS], func=AF.Exp,
                        bias=bb), reads=[sbb, nb_buf], writes=[pt_buf[slot]])
            elif bias is None:
                S.op("act", lambda e, sb=sb, nk=nk, n=n, slot=slot: e.activation(
                    out=PT[0:nk, slot, 0:n], in_=sb[0:nk, 0:n], func=AF.Exp), reads=[sbb], writes=[pt_buf[slot]])
            else:
                S.op("act", lambda e, sb=sb, nk=nk, n=n, slot=slot, bias=bias: e.activation(
                    out=PT[0:nk, slot, 0:n], in_=sb[0:nk, 0:n], func=AF.Exp, bias=bias),
                    reads=[sbb, nb_buf], writes=[pt_buf[slot]])
            pend.append((u, slot))
            if len(pend) > LA:
                do_pv(pend.pop(0))
        while pend:
            do_pv(pend.pop(0))

    def softmax_fin(b0, b0buf, b1, b1buf, c, col0, n, tts):
        def fin():
            rs = cnt["rcp"] % 2
            cnt["rcp"] += 1
            slot = cnt["pt"] % 4
            cnt["pt"] += 1
            S.op("act", lambda e: e.copy(out=RCP[0:64, rs, 0:n], in_=b1[0:64, 0:n]), reads=[b1buf], writes=[rcp_buf[rs]])
            S.op("act", lambda e: e.copy(out=RCP[64:128, rs, 0:n], in_=b0[64:128, 0:n]), reads=[b0buf],
                 writes=[rcp_buf[rs]])
            S.op("dve", lambda e: e.reciprocal(out=RCP[:, rs, 0:n], in_=RCP[:, rs, 0:n]), reads=[rcp_buf[rs]],
                 writes=[rcp_buf[rs]])
            S.op("act", lambda e: e.copy(out=PT[0:64, slot, 0:n], in_=b0[0:64, 0:n]), reads=[b0buf], writes=[pt_buf[slot]])
            S.op("act", lambda e: e.copy(out=PT[64:128, slot, 0:n], in_=b1[64:128, 0:n]), reads=[b1buf],
                 writes=[pt_buf[slot]])
            xs, xsb = bank("s")
            S.pe_group([mm(xs[:, 0:n], swap_b, PT[:, slot, 0:n], True, True)], reads=[pt_buf[slot], cb_buf], writes=[xsb])
            S.op("dve", lambda e: e.tensor_tensor(out=OT[:, c, col0:col0 + n], in0=xs[:, 0:n], in1=RCP[:, rs, 0:n],
                                                  op=ALU.mult),
                 reads=[xsb, rcp_buf[rs]], writes=[ot_buf[c][t] for t in tts])
        return fin

    def run_stick_units(units):
        stA = []
        stB = []
        acc_state = {}

        def do_cs(item):
            u, sb, sbb, lslot, accprev = item
            nk, n, sc = u["nk"], u["n"], u["scol"]
            fns = [mm(sb[0:nk, sc:sc + n], ntri_b[0:nk, 0:nk], LB[0:nk, lslot, sc:sc + n], False, accprev is None)]
            rd = [lb_buf[lslot], cb_buf]
            if accprev is not None:
                aslot, ank = accprev
                fns.append(mm(sb[0:nk, sc:sc + n], nones_b[0:ank, 0:nk], ACC[0:ank, aslot, sc:sc + n], False, True))
                rd.append(acc_buf[aslot])
            S.pe_group(fns, reads=rd, writes=[sbb])
            slot = cnt["pt"] % 4
            cnt["pt"] += 1
            S.op("act", lambda e: e.activation(out=PT[0:nk, slot, sc:sc + n], in_=sb[0:nk, sc:sc + n], func=AF.Exp),
                 reads=[sbb], writes=[pt_buf[slot]])
            stB.append((u, slot))

        def do_pv(item):
            u, slot = item
            nk, n, sc = u["nk"], u["n"], u["scol"]
            ob, obuf = u["o"]
            pb0, c0 = u["pbase"], u["ocol"]
            if "pv" in u:
                fns = [mm(ob[hh * 64:(hh + 1) * 64, c0:c0 + LS], vv, PT[0:nk, slot, hh * LS:(hh + 1) * LS],
                          u["first"], u["last"]) for hh, (key, vv) in enumerate(u["pv"])]
            else:
                fns = [mm(ob[pb0:pb0 + 64, c0:c0 + n], u["v"], PT[0:nk, slot, sc:sc + n], u["first"], u["last"])]
            S.pe_group(fns, reads=[pt_buf[slot]] + u["v_reads"], writes=[obuf])
            if u.get("fin") is not None:
                u["fin"]()
            if u.get("post") is not None:
                u["post"]()

        for u in units:
            if u.get("pre") is not None:
                u["pre"]()
            nk, n, sc = u["nk"], u["n"], u["scol"]
            sb, sbb = bank("s")
            fns = [mm(sb[0:nk, sc:sc + n], u["kT"], u["q"], True, False)]
            for xi, (xl, xr, xo, xn) in enumerate(u["extras"]):
                fns.append(mm(sb[0:nk, sc + xo:sc + xo + xn], xl, xr, False, False))
            S.pe_group(fns, reads=u["qk_reads"] + [cb_buf], writes=[sbb])
            es = cnt["ef"] % 2
            cnt["ef"] += 1
            S.op("act", lambda e, sb=sb, nk=nk, n=n, sc=sc, es=es: e.activation(
                out=EF[0:nk, es, sc:sc + n], in_=sb[0:nk, sc:sc + n], func=AF.Exp), reads=[sbb], writes=[ef_buf[es]])
            ls = cnt["lb"] % 2
            cnt["lb"] += 1
            S.op("act", lambda e, nk=nk, n=n, sc=sc, es=es, ls=ls: e.activation(
                out=LB[0:nk, ls, sc:sc + n], in_=EF[0:nk, es, sc:sc + n], func=AF.Ln, bias=1.0),
                reads=[ef_buf[es]], writes=[lb_buf[ls]])
            sid = u["seq"]
            prev = None if u["seq_first"] else acc_state[sid]
            stA.append((u, sb, sbb, ls, prev))
            if not u["seq_last"]:
                base = 2 * u["hh"]
                W = sc + n
                rot = None
                if "pv" in u:
                    rot = cnt["accs"] % 4
                    cnt["accs"] += 1
                if prev is None:
                    ns_ = base if rot is None else rot
                    S.op("dve", lambda e, nk=nk, n=n, sc=sc, ls=ls, ns_=ns_: e.tensor_copy(
                        out=ACC[0:nk, ns_, sc:sc + n], in_=LB[0:nk, ls, sc:sc + n]),
                        reads=[lb_buf[ls]], writes=[acc_buf[ns_]])
                    acc_state[sid] = (ns_, nk)
                else:
                    pslot, pnk = prev
                    ns_ = base + (1 - (pslot - base)) if rot is None else rot
                    if pnk < nk:
                        S.op("dve", lambda e, nk=nk, n=n, sc=sc, ls=ls, ns_=ns_: e.tensor_copy(
                            out=ACC[0:nk, ns_, sc:sc + n], in_=LB[0:nk, ls, sc:sc + n]),
                            reads=[lb_buf[ls]], writes=[acc_buf[ns_]])
                        S.op("dve", lambda e, pnk=pnk, n=n, sc=sc, ns_=ns_, pslot=pslot: e.tensor_tensor(
                            out=ACC[0:pnk, ns_, sc:sc + n], in0=ACC[0:pnk, ns_, sc:sc + n],
                            in1=ACC[0:pnk, pslot, sc:sc + n], op=ALU.add),
                            reads=[acc_buf[ns_], acc_buf[pslot]], writes=[acc_buf[ns_]])
                    else:
                        S.op("dve", lambda e, nk=nk, n=n, sc=sc, ls=ls, ns_=ns_, pslot=pslot: e.tensor_tensor(
                            out=ACC[0:nk, ns_, sc:sc + n], in0=ACC[0:nk, pslot, sc:sc + n], in1=LB[0:nk, ls, sc:sc + n],
                            op=ALU.add),
                            reads=[acc_buf[pslot], lb_buf[ls]], writes=[acc_buf[ns_]])
                    acc_state[sid] = (ns_, max(nk, pnk))
                if sc > 0:
                    S.op("dve", lambda e, sc=sc, ns_=ns_: e.memset(ACC[:, ns_, 0:sc], 0.0), writes=[acc_buf[ns_]])
            if len(stA) > 1:
                do_cs(stA.pop(0))
            if len(stB) > 1:
                do_pv(stB.pop(0))
        while stA:
            do_cs(stA.pop(0))
        while stB:
            do_pv(stB.pop(0))

    def copy_fin(ob, obuf, c, col0, n, tts):
        def fin():
            S.op("dve", lambda e: e.tensor_copy(out=OT[:, c, col0:col0 + n], in_=ob[:, 0:n]),
                 reads=[obuf], writes=[ot_buf[c][t] for t in tts])
        return fin

    def prep_A(slot):
        RT = tview(0, 320, F32, "p (j h) -> p j h", j=5)
        TBT = tview(512, 512 + 513 * 4, F32)
        EE = tview(4096, 4096 + 3072, F32)
        S.dma("sp", "prep", RT[:, 0:4, :], a_rel[slot][0:512, :].rearrange("(j p) h -> p j h", p=128), writes=[tr_buf])
        S.dma("sp", "prep", RT[0:1, 4, :], a_rel[slot][512:513, :], writes=[tr_buf])
        pb, pbb = bank("s")
        pb2, pbb2 = bank("s")
        fns = [lambda e, j=j: e.transpose(out=pb[0:16, j * 128:(j + 1) * 128], in_=RT[:, j, :], identity=ident_f)
               for j in range(4)]
        fns.append(lambda e: e.transpose(out=pb2[0:16, 0:1], in_=RT[0:1, 4, :], identity=ident_f[0:1, 0:1]))
        S.pe_group(fns, reads=[tr_buf, idf_buf], writes=[pbb, pbb2])
        S.op("dve", lambda e: e.tensor_copy(out=TBT[0:16, 0:512], in_=pb[0:16, :]), reads=[pbb], writes=[tr_buf])
        S.op("dve", lambda e: e.tensor_copy(out=TBT[0:16, 512:513], in_=pb2[0:16, 0:1]), reads=[pbb2], writes=[tr_buf])
        S.op("dve", lambda e: e.tensor_copy(out=EE[0:16, 0:384], in_=TBT[0:16, 129:513]), reads=[tr_buf], writes=[tr_buf])
        S.op("dve", lambda e: e.tensor_copy(out=EE[0:16, 384:768], in_=TBT[0:16, 512:513].broadcast_to([16, 384])),
             reads=[tr_buf], writes=[tr_buf])
        S.dma("sp", "prep", bass.AP(EH, 0, [[768, 16], [1, 768]]), EE[0:16, :], reads=[tr_buf], writes=[tr_buf])

    def load_bias_A(c):
        BIAS, BS = MV["BIAS"], MV["BS"]
        for hh in range(2):
            h = 2 * c + hh
            S.dma("pool", "biasA", BIAS[:, hh, :], bass.AP(EH, h * 768, [[1, 128], [1, 640]]),
                  reads=[tr_buf], writes=[bias_buf])
        S.op("dve", lambda e: e.memset(BIAS[64:128, :, 576:640], NEG), writes=[bias_buf])
        S.op("dve", lambda e: e.memset(BIAS[0:64, :, 0:64], NEG), writes=[bias_buf])
        for hh in range(2):
            h = 2 * c + hh
            for j in range(5):
                rows = 128 if j < 4 else LS
                S.dma("pool", "biasS", BS[0:rows, j, hh, :],
                      bass.AP(EH, h * 768 + (512 - 128 * j if j < 4 else 112), [[1, rows], [1, LS]]),
                      reads=[tr_buf], writes=[bs_buf])

    def prep_B(slot):
        NBP, NBS = MV["NBP"], MV["NBS"]
        WF = tview(0, 256, BF16, "p (c h) -> p c h", c=NCH)
        BF_ = tview(256, 320, F32)
        ZF = tview(512, 512 + 1088, F32, "p (t h) -> p t h", t=17)
        LFN = tview(2048, 2048 + 1088, F32, "p (t h) -> p t h", t=17)
        LFT = tview(4096, 4096 + TT * 4, F32)
        TMPN = tview(4096 + 8448, 4096 + 8448 + 8192, F32)
        CL = tview(20992, 20992 + 512, F32, "p (j h) -> p j h", j=8)
        LFS = tview(21504, 21504 + 4160, F32)
        TMPS = tview(25664, 25664 + 4160, F32)
        S.dma("pool", "prepw", WF[:, :, :], b_wf[slot].rearrange("(c p) h -> p c h", p=128), writes=[tr_buf])
        S.dma("sp", "prep", BF_[:, :], bass.AP(b_bf.tensor, slot * NH, [[0, 128], [1, NH]]), writes=[tr_buf])
        pb, pbb = bank("s")
        fns = []
        for t in range(17):
            rows = 128 if t < 16 else TS
            for dc in range(NCH):
                fns.append(mm(pb[0:rows, t * 16:(t + 1) * 16], H[:, dc, t * 128:t * 128 + rows], WF[:, dc, :],
                              dc == 0, dc == NCH - 1))
        S.pe_group(fns, reads=[tr_buf] + h_buf, writes=[pbb])
        psv = pb[:, 0:272].rearrange("p (t h) -> p t h", t=17)
        S.op("dve", lambda e: e.tensor_tensor(out=ZF[:, :, :], in0=psv, in1=BF_[:, :].unsqueeze(1).broadcast_to([128, 17, 16]),
                                              op=ALU.add), reads=[pbb, tr_buf], writes=[tr_buf])
        S.op("act", lambda e: e.activation(out=ZF[:, :, :], in_=ZF[:, :, :], func=AF.Exp, scale=-1.0),
             reads=[tr_buf], writes=[tr_buf])
        S.op("act", lambda e: e.activation(out=ZF[:, :, :], in_=ZF[:, :, :], func=AF.Ln, bias=1.0),
             reads=[tr_buf], writes=[tr_buf])
        S.op("dve", lambda e: e.tensor_scalar(out=LFN[:, :, :], in0=ZF[:, :, :], scalar1=-1.0, scalar2=None, op0=ALU.mult),
             reads=[tr_buf], writes=[tr_buf])
        S.dma("sp", "prep", o_b_fp.rearrange("(t p) h -> p t h", p=128), LFN[:, 0:16, :], reads=[tr_buf])
        S.dma("sp", "prep", o_b_fs, LFN[0:TS, 16, :], reads=[tr_buf])
        for g in range(5):
            pb, pbb = bank("s")
            tl = list(range(g * 4, min(g * 4 + 4, 17)))
            fns = []
            for jj, t in enumerate(tl):
                rows = 128 if t < 16 else TS
                fns.append(lambda e, pb=pb, jj=jj, t=t, rows=rows: e.transpose(
                    out=pb[0:16, jj * 128:jj * 128 + rows], in_=LFN[0:rows, t, :], identity=ident_f[0:rows, 0:rows]))
            S.pe_group(fns, reads=[tr_buf, idf_buf], writes=[pbb])
            ncol = 512 if g < 4 else TS
            S.op("dve", lambda e, pb=pb, g=g, ncol=ncol: e.tensor_copy(out=LFT[0:16, g * 512:g * 512 + ncol],
                                                                        in_=pb[0:16, 0:ncol]), reads=[pbb], writes=[tr_buf])
        S.op("dve", lambda e: e.tensor_tensor_scan(out=LFT[0:16, 0:TP], data0=LFT[0:16, 0:TP], data1=LFT[0:16, 0:TP],
                                                   initial=0.0, op0=ALU.add, op1=ALU.bypass),
             reads=[tr_buf], writes=[tr_buf])
        for Q in range(4):
            nq = (4 * Q + 4) * 128
            ref = 512 * Q + 511
            S.op("dve", lambda e, nq=nq, ref=ref: e.tensor_scalar(out=TMPN[0:16, 0:nq], in0=LFT[0:16, 0:nq],
                                                                  scalar1=LFT[0:16, ref:ref + 1], scalar2=-1.0,
                                                                  op0=ALU.subtract, op1=ALU.mult),
                 reads=[tr_buf], writes=[tr_buf])
            pb, pbb = bank("s")
            nt = 4 * Q + 4
            fns = [lambda e, pb=pb, j=j: e.transpose(out=pb[:, j * 16:(j + 1) * 16], in_=TMPN[0:16, j * 128:(j + 1) * 128],
                                                     identity=ident_f[0:16, 0:16]) for j in range(nt)]
            S.pe_group(fns, reads=[tr_buf, idf_buf], writes=[pbb])
            base = nbp_base(Q)
            S.op("dve", lambda e, pb=pb, nt=nt, base=base: e.tensor_copy(
                out=NBP[:, base:base + nt, :], in_=pb[:, 0:nt * 16].rearrange("p (t h) -> p t h", t=nt)),
                reads=[pbb], writes=[nb_buf])
        for s in range(NS):
            S.dma("sp", "prep", CL[:, :, :], cb_f[s].rearrange("(j p) h -> p j h", p=128), writes=[tr_buf])
            for g in range(2):
                pb, pbb = bank("s")
                fns = [lambda e, pb=pb, j=j, g=g: e.transpose(out=pb[0:16, j * 128:(j + 1) * 128], in_=CL[:, g * 4 + j, :],
                                                              identity=ident_f) for j in range(4)]
                S.pe_group(fns, reads=[tr_buf, idf_buf], writes=[pbb])
                S.op("dve", lambda e, pb=pb, g=g: e.tensor_copy(out=LFS[0:16, g * 512:(g + 1) * 512], in_=pb[0:16, :]),
                     reads=[pbb], writes=[tr_buf])
            S.op("dve", lambda e, s=s: e.tensor_copy(out=LFS[0:16, 1024:1040], in_=LFT[0:16, TP + s * LS:TP + (s + 1) * LS]),
                 reads=[tr_buf], writes=[tr_buf])
            S.op("dve", lambda e: e.tensor_tensor_scan(out=LFS[0:16, 0:1040], data0=LFS[0:16, 0:1040],
                                                       data1=LFS[0:16, 0:1040], initial=0.0, op0=ALU.add, op1=ALU.bypass),
                 reads=[tr_buf], writes=[tr_buf])
            S.op("dve", lambda e: e.tensor_scalar(out=TMPS[0:16, 0:1040], in0=LFS[0:16, 0:1040],
                                                  scalar1=LFS[0:16, 1039:1040], scalar2=-1.0,
                                                  op0=ALU.subtract, op1=ALU.mult), reads=[tr_buf], writes=[tr_buf])
            pb, pbb = bank("s")
            fns = [lambda e, pb=pb, j=j: e.transpose(out=pb[:, j * 16:(j + 1) * 16], in_=TMPS[0:16, j * 128:(j + 1) * 128],
                                                     identity=ident_f[0:16, 0:16]) for j in range(8)]
            fns.append(lambda e, pb=pb: e.transpose(out=pb[0:16, 128:144], in_=TMPS[0:16, 1024:1040],
                                                    identity=ident_f[0:16, 0:16]))
            S.pe_group(fns, reads=[tr_buf, idf_buf], writes=[pbb])
            S.op("dve", lambda e, pb=pb, s=s: e.tensor_copy(out=NBS[:, s, :, :],
                                                            in_=pb[:, 0:144].rearrange("p (t h) -> p t h", t=9)),
                 reads=[pbb], writes=[nb_buf])

    def nbp_base(Q):
        return sum(4 * q + 4 for q in range(Q))

    def prompt_units(kind, c):
        QT0, QT1, KT, VB, v1 = MV["QT0"], MV["QT1"], MV["KT"], MV["VB"], MV["v1"]
        BIAS, NBP = MV.get("BIAS"), MV.get("NBP")
        units = []
        for Q in range(4):
            ob, obuf = bank("a")
            if kind != 2:
                dbk, dbuf = bank("a")
            if kind == 0:
                jl = list(range(max(0, 4 * Q - 4), 4 * Q + 4))
            elif kind == 1:
                jl = list(range(0, 4 * Q + 4))
            else:
                jl = list(range(4 * Q + 3, -1, -1))
            for ji, j in enumerate(jl):
                for hh in range(2):
                    h = 2 * c + hh
                    pb0 = 64 * hh
                    if kind == 0:
                        lo = max(0, 128 * j - 512 * Q)
                        hi = min(512, 128 * j + 640 - 512 * Q)
                    else:
                        lo = max(0, 128 * j - 512 * Q)
                        hi = 512
                    n = hi - lo
                    q0 = 512 * Q + lo
                    u = dict(nk=128, n=n, kT=KT[:, j * 128:(j + 1) * 128], q=(QT0, QT1)[hh][:, q0:q0 + n],
                             qk_reads=[kt_buf, qt_buf], extras=[],
                             v=(VB[:, j, hh * 64:(hh + 1) * 64] if kind == 2 else VB[:, j, hh * 64:hh * 64 + 128]),
                             v_reads=[vb_buf],
                             pbase=pb0, ocol=lo, scol=lo, first=(ji == 0), last=(ji == len(jl) - 1), hh=hh)
                    if kind == 0:
                        b0 = q0 - 128 * j
                        u["extras"].append((anti_b, BIAS[:, hh, b0:b0 + n], 0, n))
                        u["qk_reads"] = [kt_buf, qt_buf, bias_buf]
                    else:
                        r = j - 4 * Q
                        if r >= 0:
                            u["extras"].append((ident_b, mask_le if kind == 1 else mask_lt, 0, 128))
                    if kind == 1:
                        u["bias"] = NBP[:, nbp_base(Q) + j, h:h + 1]
                    u["o"] = (ob, obuf)
                    if kind != 2:
                        u["d"] = (dbk, dbuf)
                    else:
                        u["seq"] = ("p", Q, hh)
                        u["seq_first"] = (ji == 0)
                        u["seq_last"] = (ji == len(jl) - 1)
                    if ji == len(jl) - 1 and hh == 1:
                        if kind != 2:
                            u["fin"] = softmax_fin(ob, obuf, dbk, dbuf, c, 512 * Q, 512, [Q])
                        else:
                            u["fin"] = copy_fin(ob, obuf, c, 512 * Q, 512, [Q])
                    units.append(u)
        return units

    def sample_units(kind, c, s, ksl, nt, ob, obuf, dbk, dbuf, is_last_seq):
        KT, VS, VCC = MV["KT"], MV["VS"], MV["VCC"]
        BS, NBS = MV.get("BS"), MV.get("NBS")
        units = []
        tl = list(range(nt)) + ["new"]
        if kind == 2:
            tl = ["new"] + list(range(nt - 1, -1, -1))
        q0 = TP + s * LS
        for ji, j in enumerate(tl):
            if j == "new":
                nk = LS
                kT = KT[:, q0:q0 + LS]
                qk_reads = [kt_buf, qs_buf]
                v_reads = [vs_buf]
                vsrc = lambda a, b: VS[0:LS, s, a:b]
            else:
                nk = 128
                kT = KTC[:, ksl, j * 128:(j + 1) * 128]
                qk_reads = [ktc_buf[ksl], qs_buf]
                v_reads = [vcc_buf[ksl]]
                vsrc = lambda a, b, j=j: VCC[:, ksl, j, a:b]
            if kind == 2:
                pv = [("o", vsrc(0, 64)), ("o", vsrc(64, 128))]
            else:
                pv = [("o", vsrc(0, 128)), ("d", vsrc(64, 192))]
            u = dict(nk=nk, n=2 * LS, kT=kT, q=QS[:, s, :, :].rearrange("p h i -> p (h i)"), qk_reads=qk_reads, extras=[],
                     v=None, v_reads=v_reads, pbase=0, ocol=s * LS, scol=0, first=(ji == 0), last=(ji == len(tl) - 1),
                     hh=0, pv=pv)
            jidx = nt if j == "new" else j
            if kind == 0:
                ja = 4 if j == "new" else j
                u["extras"].append(((anti_b if nk == 128 else anti16_b[0:LS, 0:LS]),
                                    BS[0:nk, ja, :, :].rearrange("p h i -> p (h i)"), 0, 2 * LS))
                u["qk_reads"] = qk_reads + [bs_buf]
            elif j == "new":
                mk = (mask_le if kind == 1 else mask_lt)[0:LS, 0:LS]
                u["extras"].append((ident_b[0:LS, 0:LS], mk, 0, LS))
                u["extras"].append((ident_b[0:LS, 0:LS], mk, LS, LS))
            if kind == 1:
                u["bias2"] = [NBS[0:nk, s, jidx, 2 * c + hh:2 * c + hh + 1] for hh in range(2)]
            u["o"] = (ob, obuf)
            if kind != 2:
                u["d"] = (dbk, dbuf)
            else:
                u["seq"] = ("s", s)
                u["seq_first"] = (ji == 0)
                u["seq_last"] = (ji == len(tl) - 1)
            if is_last_seq and ji == len(tl) - 1:
                if kind != 2:
                    u["fin"] = softmax_fin(ob, obuf, dbk, dbuf, c, TP, TS, [4])
                else:
                    u["fin"] = copy_fin(ob, obuf, c, TP, TS, [4])
            units.append(u)
        return units

    def oproj(wo_d, swapped):
        WO = MV["WO"]
        wsrc = wo_d.rearrange("(c p) f -> p c f", p=128)
        for dp in range(NCH):
            sl = cnt["wo"] % 2
            cnt["wo"] += 1
            if swapped:
                S.dma("pool", "wo%d_a" % sl, WO[0:64, sl, :, :], wsrc[64:128, :, dp * 128:(dp + 1) * 128],
                      writes=[wo_buf[sl]])
                S.dma("pool", "wo%d_b" % sl, WO[64:128, sl, :, :], wsrc[0:64, :, dp * 128:(dp + 1) * 128],
                      writes=[wo_buf[sl]])
            else:
                S.dma("pool", "wo%d_a" % sl, WO[:, sl, :, :], wsrc[:, :, dp * 128:(dp + 1) * 128], writes=[wo_buf[sl]])
            for tt, (t0, n) in enumerate(TTILES):
                po, pob = bank("a")
                fns = [mm(po[:, 0:n], WO[:, sl, c, :], OT[:, c, t0:t0 + n], c == 0, c == NCH - 1) for c in range(NCH)]
                S.pe_group(fns, reads=[wo_buf[sl]] + [ot_buf[c][tt] for c in range(NCH)], writes=[pob])
                S.op("dve", lambda e, po=po, dp=dp, t0=t0, n=n: e.tensor_tensor(
                    out=X[:, dp, t0:t0 + n], in0=po[:, 0:n], in1=X[:, dp, t0:t0 + n], op=ALU.add),
                    reads=[pob, x_buf[tt][dp]], writes=[x_buf[tt][dp]])

    def mixer(l):
        kind, slot = l % 3, l // 3
        norm(1, l, H, h_buf)
        S.barrier()
        set_views(kind)
        QT0, QT1 = MV["QT0"], MV["QT1"]
        if kind == 0:
            wqkv_d, wo_d = a_qkv[slot], a_o[slot]
            outs = (o_a_kp[slot], o_a_vp[slot], o_a_ks[slot], o_a_vs[slot])
            ck, cv, nt = ca_k[slot], ca_v[slot], 4
            prep_A(slot)
        elif kind == 1:
            wqkv_d, wo_d = b_qkv[slot], b_o[slot]
            outs = (o_b_kp, o_b_vp, o_b_ks, o_b_vs)
            ck, cv, nt = cb_k, cb_v, 8
            prep_B(slot)
        else:
            wqkv_d, wo_d = c_qkv[slot], c_o[slot]
            outs = (o_c_kp, o_c_vp, o_c_ks, o_c_vs)
            ck, cv, nt = cc_k, cc_v, 8
        S.barrier()
        S.op("dve", lambda e: e.memset(QS[:, :, :, :], 0.0), writes=[qs_buf])
        S.op("dve", lambda e: e.memset(QT0[64:128, :], 0.0), writes=[qt_buf])
        S.op("dve", lambda e: e.memset(QT1[0:64, :], 0.0), writes=[qt_buf])
        if kind != 2:
            VB_, VS_, VCC_ = MV["VB"], MV["VS"], MV["VCC"]
            S.op("dve", lambda e: e.memset(VB_[:, :, 64:128], 1.0), writes=[vb_buf])
            S.op("dve", lambda e: e.memset(VS_[:, :, 64:128], 1.0), writes=[vs_buf])
            S.op("dve", lambda e: e.memset(VCC_[:, :, :, 64:128], 1.0), writes=vcc_buf)
        load_wqkv(0, wqkv_d)
        for c in range(NCH):
            if "no_sample" not in DBG:
                issue_cache_dma(0, c, ck, cv, nt, 0)
            project_pair(kind, slot, c, wqkv_d, outs)
            if c + 1 < NCH:
                load_wqkv(c + 1, wqkv_d)
            if kind == 0 and "no_bias" not in DBG:
                load_bias_A(c)
            units = prompt_units(kind, c)
            if "no_bias" in DBG:
                for u in units:
                    u["extras"] = [x for x in u["extras"] if x[0] is not anti_b]
                    u["qk_reads"] = [kt_buf, qt_buf]
            ob, obuf = bank("a")
            dbk = dbuf = None
            if kind != 2:
                dbk, dbuf = bank("a")
            su = []
            for s in range(NS):
                su.append(sample_units(kind, c, s, s % 2, nt, ob, obuf, dbk, dbuf, s == NS - 1))

            def mk_load(s, c=c):
                return lambda: load_cache_pair(s, c, ck, cv, nt, s % 2)

            def mk_pre(c=c):
                def pre():
                    cache_transposes(nt, 0)
                    load_cache_pair(1, c, ck, cv, nt, 1)
                return pre
            if "no_sample" not in DBG:
                units[0]["pre"] = mk_pre()
                for s in range(NS - 2):
                    su[s][-1]["post"] = mk_load(s + 2)
                for s in range(NS):
                    units += su[s]
            if "no_attn" in DBG:
                units = []
            if kind != 2:
                run_softmax_units(units)
            else:
                run_stick_units(units)
        oproj(wo_d, kind != 2)
        S.barrier()

    yt_buf = [Buf("yt%d" % t) for t in range(NTT)]

    def final_out():
        S.barrier()
        norm(3, 0, YT, yt_buf)
        S.barrier()
        ntile = TP // 128 + 1
        for t in range(ntile):
            rows = 128 if t < TP // 128 else TS
            dstd = y_p[t * 128:(t + 1) * 128, :] if t < TP // 128 else y_s[:, :]
            tt = min(t // 4, 4)
            sl = cnt["xin"] % 4
            cnt["xin"] += 1
            for half in range(2):
                pb, pbb = bank("s")
                fns = []
                for j in range(4):
                    c = half * 4 + j
                    fns.append(lambda e, pb=pb, j=j, c=c, t=t, rows=rows: e.transpose(
                        out=pb[0:rows, j * 128:(j + 1) * 128], in_=YT[:, c, t * 128:t * 128 + rows],
                        identity=ident_f))
                S.pe_group(fns, reads=[yt_buf[tt], idf_buf], writes=[pbb])
                dst = XIN[0:rows, sl, half * 512:(half + 1) * 512]
                if half == 0:
                    S.op("act", lambda e, dst=dst, pb=pb, rows=rows: e.copy(out=dst, in_=pb[0:rows, :]),
                         reads=[pbb], writes=[xin_buf[sl]])
                else:
                    S.op("dve", lambda e, dst=dst, pb=pb, rows=rows: e.tensor_copy(out=dst, in_=pb[0:rows, :]),
                         reads=[pbb], writes=[xin_buf[sl]])
            S.dma("sp", "xin%d" % sl, dstd, XIN[0:rows, sl, :], reads=[xin_buf[sl]])

    load_x()
    S.barrier()
    done = False
    for l in range(DEPTH):
        S.new_epoch()
        for stage in ("ffn1", "mix", "ffn2"):
            if stage == "ffn1":
                ffn(l, 1)
            elif stage == "mix":
                mixer(l)
            else:
                ffn(l, 2)
            if stop_after == (l, stage):
                done = True
                break
        if done:
            break
    S.new_epoch()
    final_out()
    S.final_wait("sp")
    S.replay()
    st.close()
    return nc


W_NAMES = ["norm_ffn1", "norm_mix", "norm_ffn2", "ffn1_gate", "ffn1_up", "ffn1_down", "ffn2_gate", "ffn2_up",
           "ffn2_down", "a_w_qkv", "a_w_o", "a_rel_bias", "b_w_qkv", "b_w_o", "b_w_f", "b_b_f", "c_w_qkv", "c_w_o"]

STOP_AFTER = None
DBG = set()


def kernel(**inputs):
    n = 8
    f = lambda a: np.ascontiguousarray(np.asarray(a, dtype=np.float32))
    nc = build(STOP_AFTER)
    cf, cb = _consts_np()
    shared = {k: f(inputs[k]) for k in W_NAMES if not ("skip_ffn" in DBG and k.startswith("ffn"))}
    shared["norm_final"] = f(inputs["norm_final"]).reshape(1, D)
    shared["consts_f"] = cf
    shared["consts_b"] = cb
    in_maps = []
    for c in range(n):
        m = dict(shared)
        sl = slice(NS * c, NS * (c + 1))
        m["x_p"] = f(inputs["x_prompt"][c])
        m["x_s"] = f(inputs["x_sample"][sl]).reshape(TS, D)
        m["ca_k"] = f(inputs["cache_a_k"][:, sl]).reshape(2, NS, 512, D)
        m["ca_v"] = f(inputs["cache_a_v"][:, sl]).reshape(2, NS, 512, D)
        m["cb_k"] = f(inputs["cache_b_k"][0, sl]).reshape(NS, 1024, D)
        m["cb_v"] = f(inputs["cache_b_v"][0, sl]).reshape(NS, 1024, D)
        m["cb_f"] = f(inputs["cache_b_logf"][0, sl])
        m["cc_k"] = f(inputs["cache_c_k"][0, sl]).reshape(NS, 1024, D)
        m["cc_v"] = f(inputs["cache_c_v"][0, sl]).reshape(NS, 1024, D)
        in_maps.append(m)
    res = run_bass_kernel_spmd(nc, in_maps, core_ids=list(range(n)))
    R = res.results

    def gp(name, shape):
        return np.stack([R[c][name].reshape(shape) for c in range(n)], axis=0)

    y_prompt = gp("y_p", (TP, D))
    y_sample = gp("y_s", (NS, LS, D)).reshape(n * NS, LS, D)
    a_kp = gp("a_kp", (2, 512, NH, HD)).transpose(1, 0, 2, 3, 4)
    a_vp = gp("a_vp", (2, 512, NH, HD)).transpose(1, 0, 2, 3, 4)
    a_ks = gp("a_ks", (2, NS, LS, NH, HD)).transpose(1, 0, 2, 3, 4, 5).reshape(2, n * NS, LS, NH, HD)
    a_vs = gp("a_vs", (2, NS, LS, NH, HD)).transpose(1, 0, 2, 3, 4, 5).reshape(2, n * NS, LS, NH, HD)

    def one_p(name, last):
        return gp(name, (TP,) + last)[None]

    def one_s(name, last):
        return gp(name, (NS, LS) + last).reshape((1, n * NS, LS) + last)

    outs = (y_prompt, y_sample, a_kp, a_vp, a_ks, a_vs,
            one_p("b_kp", (NH, HD)), one_p("b_vp", (NH, HD)), one_p("b_fp", (NH,)),
            one_s("b_ks", (NH, HD)), one_s("b_vs", (NH, HD)), one_s("b_fs", (NH,)),
            one_p("c_kp", (NH, HD)), one_p("c_vp", (NH, HD)), one_s("c_ks", (NH, HD)), one_s("c_vs", (NH, HD)))
    return tuple(np.ascontiguousarray(o, dtype=np.float32) for o in outs)
```

```python
import numpy as np
import concourse.bass as bass
import concourse.mybir as mybir
from concourse.bass_utils import run_bass_kernel_spmd
from contextlib import ExitStack

F32 = mybir.dt.float32
BF16 = mybir.dt.bfloat16
U8 = mybir.dt.uint8
AF = mybir.ActivationFunctionType
ALU = mybir.AluOpType

D = 1024
NCH = 8
TP = 2048
NS = 4
LS = 16
TS = NS * LS
TT = TP + TS
DFF = 2816
NFC = 22
DEPTH = 4
NH = 16
HD = 64
EPS = 1e-6
TTILES = [(0, 512), (512, 512), (1024, 512), (1536, 512), (2048, 64)]
NEG = -30000.0

ENGS = ["pe", "act", "dve", "pool", "sp"]


class Buf:
    __slots__ = ("name", "w", "r", "excl")

    def __init__(self, name, excl=False):
        self.name = name
        self.w = None
        self.r = []
        self.excl = excl


class Sched:
    def __init__(self, nc, stack):
        self.nc = nc
        self.stack = stack
        self.q = {e: [] for e in ENGS}
        self.cnt = {}
        self.semh = {}
        self.epoch = 0
        self.known = {e: {} for e in ENGS}
        self.nsem = 0

    def _key_init(self, key):
        if key not in self.cnt:
            self.cnt[key] = 0
            self.semh[key] = self.stack.enter_context(self.nc.semaphore("s%d" % self.nsem))
            self.nsem += 1

    def pkey(self, eng):
        key = ("p", eng, self.epoch)
        self._key_init(key)
        return key

    def _deps(self, eng, reads, writes, extra):
        deps = {}

        def add(tok, same_ok):
            if tok is None:
                return
            key, val = tok
            if key[0] == "p" and key[1] == eng and not same_ok:
                return
            if deps.get(key, 0) < val:
                deps[key] = val

        same = eng != "pe"
        for b in reads:
            add(b.w, same)
            if b.excl:
                for t in b.r:
                    add(t, False)
        for b in writes:
            add(b.w, same)
            for t in b.r:
                add(t, same)
        for t in extra:
            add(t, True)
        out = []
        kn = self.known[eng]
        for key, val in deps.items():
            if kn.get(key, 0) >= val:
                continue
            kn[key] = val
            out.append((key, val))
        return out

    def _post(self, tok, reads, writes):
        for b in writes:
            b.w = tok
            b.r = []
        for b in reads:
            b.r.append(tok)

    def op(self, eng, fn, reads=(), writes=(), extra=()):
        waits = self._deps(eng, reads, writes, extra)
        key = self.pkey(eng)
        self.cnt[key] += 1
        tok = (key, self.cnt[key])
        self.q[eng].append((fn, waits, (key, 1)))
        self._post(tok, reads, writes)
        return tok

    def pe_group(self, fns, reads=(), writes=(), extra=()):
        waits = self._deps("pe", reads, writes, extra)
        key = self.pkey("pe")
        self.cnt[key] += 1
        tok = (key, self.cnt[key])
        n = len(fns)
        for i, fn in enumerate(fns):
            self.q["pe"].append((fn, waits if i == 0 else [], (key, 1) if i == n - 1 else None))
        self._post(tok, reads, writes)
        return tok

    def dma(self, queue, semname, out, in_, reads=(), writes=(), extra=()):
        waits = self._deps(queue, reads, writes, extra)
        key = ("d", semname)
        self._key_init(key)
        self.cnt[key] += 16
        tok = (key, self.cnt[key])

        def fn(eng, out=out, in_=in_):
            return eng.dma_start(out=out, in_=in_)

        self.q[queue].append((fn, waits, (key, 16)))
        self._post(tok, reads, writes)
        return tok

    def barrier(self):
        toks = [(k, v) for k, v in self.cnt.items() if v > 0]
        for e in ENGS:
            waits = []
            kn = self.known[e]
            for key, val in toks:
                if key[0] == "p" and key[1] == e:
                    continue
                if kn.get(key, 0) >= val:
                    continue
                kn[key] = val
                waits.append((key, val))
            if waits:
                self.q[e].append((None, waits, None))

    def new_epoch(self):
        self.epoch += 1

    def final_wait(self, eng="sp"):
        waits = [(k, v) for k, v in self.cnt.items() if v > 0 and k[0] == "d"]
        self.q[eng].append((None, waits, None))

    def replay(self):
        nc = self.nc
        semh = self.semh

        def run(eng, lst):
            for fn, waits, inc in lst:
                for key, val in waits:
                    eng.wait_ge(semh[key], val)
                if fn is not None:
                    ins = fn(eng)
                    if inc is not None:
                        ins.then_inc(semh[inc[0]], inc[1])

        with nc.Block() as block:
            @block.tensor
            def _(e):
                run(e, self.q["pe"])

            @block.scalar
            def _(e):
                run(e, self.q["act"])

            @block.vector
            def _(e):
                run(e, self.q["dve"])

            @block.gpsimd
            def _(e):
                run(e, self.q["pool"])

            @block.sync
            def _(e):
                run(e, self.q["sp"])


def _consts_np():
    c = np.zeros((128, 9, 128), np.float32)
    k = np.arange(128)[:, None]
    q = np.arange(128)[None, :]
    c[:, 0, :] = np.eye(128, dtype=np.float32)
    c[:, 1, :] = 1.0
    c[:, 2, :] = np.where(k > q, NEG, 0.0)
    c[:, 3, :] = np.where(k >= q, NEG, 0.0)
    c[:, 4, :] = np.where(k >= q, -1.0, 0.0)
    c[:, 5, :] = -1.0
    c[:, 6, :] = np.where(k + q == 127, 1.0, 0.0)
    c[:16, 7, :16] = np.where(k[:16] + q[:, :16] == 15, 1.0, 0.0)
    c[:, 8, :] = np.where(np.abs(k - q) == 64, 1.0, 0.0)
    return np.eye(128, dtype=np.float32), c.reshape(128, 1152)


def build(stop_after=None):
    nc = bass.Bass("TRN2", target_bir_lowering=False)
    st = ExitStack()

    def din(name, shape):
        return nc.dram_tensor(name, list(shape), F32, kind="ExternalInput").ap()

    def dout(name, shape):
        return nc.dram_tensor(name, list(shape), F32, kind="ExternalOutput").ap()

    x_p = din("x_p", (TP, D))
    x_s = din("x_s", (TS, D))
    consts_f = din("consts_f", (128, 128))
    consts_b = din("consts_b", (128, 1152))
    norm_ffn1 = din("norm_ffn1", (DEPTH, D))
    norm_mix = din("norm_mix", (DEPTH, D))
    norm_ffn2 = din("norm_ffn2", (DEPTH, D))
    norm_final = din("norm_final", (1, D))
    ffn_w = {}
    if "skip_ffn" not in DBG:
        for which in (1, 2):
            ffn_w[which] = (din("ffn%d_gate" % which, (DEPTH, D, DFF)), din("ffn%d_up" % which, (DEPTH, D, DFF)),
                            din("ffn%d_down" % which, (DEPTH, DFF, D)))
    ca_k = din("ca_k", (2, NS, 512, D))
    ca_v = din("ca_v", (2, NS, 512, D))
    cb_k = din("cb_k", (NS, 1024, D))
    cb_v = din("cb_v", (NS, 1024, D))
    cb_f = din("cb_f", (NS, 1024, NH))
    cc_k = din("cc_k", (NS, 1024, D))
    cc_v = din("cc_v", (NS, 1024, D))
    a_qkv = din("a_w_qkv", (2, D, 3 * D))
    a_o = din("a_w_o", (2, D, D))
    a_rel = din("a_rel_bias", (2, 513, NH))
    b_qkv = din("b_w_qkv", (1, D, 3 * D))
    b_o = din("b_w_o", (1, D, D))
    b_wf = din("b_w_f", (1, D, NH))
    b_bf = din("b_b_f", (1, NH))
    c_qkv = din("c_w_qkv", (1, D, 3 * D))
    c_o = din("c_w_o", (1, D, D))
    y_p = dout("y_p", (TP, D))
    y_s = dout("y_s", (TS, D))
    o_a_kp = dout("a_kp", (2, 512, D))
    o_a_vp = dout("a_vp", (2, 512, D))
    o_a_ks = dout("a_ks", (2, TS, D))
    o_a_vs = dout("a_vs", (2, TS, D))
    o_b_kp = dout("b_kp", (TP, D))
    o_b_vp = dout("b_vp", (TP, D))
    o_b_fp = dout("b_fp", (TP, NH))
    o_b_ks = dout("b_ks", (TS, D))
    o_b_vs = dout("b_vs", (TS, D))
    o_b_fs = dout("b_fs", (TS, NH))
    o_c_kp = dout("c_kp", (TP, D))
    o_c_vp = dout("c_vp", (TP, D))
    o_c_ks = dout("c_ks", (TS, D))
    o_c_vs = dout("c_vs", (TS, D))
    EH = nc.dram_tensor("eh_scratch", [NH * 768], F32)

    S = Sched(nc, st)

    X = st.enter_context(nc.sbuf_tensor("X", [128, NCH, TT], F32))
    HU = st.enter_context(nc.sbuf_tensor("HU", [128, 2 * NCH * TT * 2], U8))
    Wr = st.enter_context(nc.sbuf_tensor("Wr", [128, 34816], U8))
    SQr = st.enter_context(nc.sbuf_tensor("SQr", [128, 8192], U8))
    RSr = st.enter_context(nc.sbuf_tensor("RSr", [128, 4096], U8))
    IDF = st.enter_context(nc.sbuf_tensor("IDF", [128, 128], F32))
    CB = st.enter_context(nc.sbuf_tensor("CB", [128, 9, 128], BF16))
    GAIN = st.enter_context(nc.sbuf_tensor("GAIN", [128, 13 * NCH], F32))
    QS = st.enter_context(nc.sbuf_tensor("QS", [128, NS, 2, LS], BF16))
    ESZ = 26752
    Er = st.enter_context(nc.sbuf_tensor("Er", [128, ESZ], U8))

    def view(raw, a, b, dt, pat=None, **kw):
        v = raw[:, a:b].bitcast(dt)
        if pat is not None:
            v = v.rearrange(pat, **kw)
        return v

    HB = NCH * TT * 2
    H = view(HU, 0, HB, BF16, "p (c t) -> p c t", c=NCH)
    ACT = view(HU, HB, 2 * HB, BF16, "p (c t) -> p c t", c=NCH)
    OT = ACT
    YT = view(HU, 0, 2 * HB, F32, "p (c t) -> p c t", c=NCH)
    SQ = view(SQr, 0, 8192, BF16, "p (c t) -> p c t", c=NCH)
    GST = SQr[0:104, 0:512].bitcast(F32)
    RS = view(RSr, 0, 4096, F32, "p (s t) -> p s t", s=2)
    WGU = view(Wr, 0, 16384, BF16, "p (s g c f) -> p s g c f", s=4, g=2, c=NCH)
    WD = view(Wr, 16384, 32768, BF16, "p (s f) -> p s f", s=8)
    SG = view(Wr, 32768, 34816, BF16, "p (s f) -> p s f", s=2)
    XIN = view(Wr, 0, 16384, F32, "p (s f) -> p s f", s=4)
    WQKV = view(Wr, 0, 12288, BF16, "p (s m c f) -> p s m c f", s=2, m=3, c=NCH)
    PT = view(Er, 0, 4096, BF16, "p (s f) -> p s f", s=4)
    KST = view(Er, 4096, 6144, F32, "p (s j f) -> p s j f", s=2, j=2)
    VST = view(Er, 6144, 8192, F32, "p (s j f) -> p s j f", s=2, j=2)
    KTC = view(Er, 8192, 12288, BF16, "p (s f) -> p s f", s=2)
    MV = {}

    def set_views(kind):
        MV.clear()
        MV["WO"] = view(Wr, 12288, 16384, BF16, "p (s c f) -> p s c f", s=2, c=NCH)
        MV["QT0"] = view(Wr, 16384, 20608, BF16)
        MV["KT"] = view(Wr, 20608, 24832, BF16)
        if kind != 2:
            vw = 192
            MV["VB"] = view(Wr, 24832, 31360, BF16, "p (t f) -> p t f", t=17)
            MV["BS"] = view(Wr, 31360, 31680, BF16, "p (j h i) -> p j h i", j=5, h=2)
            MV["NBP"] = view(Wr, 31360, 33920, F32, "p (t h) -> p t h", h=NH)
            MV["VCC"] = view(Er, 12288, 18432, BF16, "p (s j f) -> p s j f", s=2, j=8)
            MV["QT1"] = view(Er, 18432, 22656, BF16)
            MV["BIAS"] = view(Er, 22656, 25216, BF16, "p (h f) -> p h f", h=2)
            MV["NBS"] = view(Er, 22656, 24960, F32, "p (s j h) -> p s j h", s=NS, j=9)
            MV["VS"] = view(Er, 25216, 26752, BF16, "p (s f) -> p s f", s=NS)
        else:
            vw = 128
            MV["VB"] = view(Wr, 24832, 29184, BF16, "p (t f) -> p t f", t=17)
            MV["VS"] = view(Wr, 29184, 30208, BF16, "p (s f) -> p s f", s=NS)
            MV["VCC"] = view(Er, 12288, 16384, BF16, "p (s j f) -> p s j f", s=2, j=8)
            MV["QT1"] = view(Er, 16384, 20608, BF16)
        MV["vw"] = vw
        MV["v1"] = vw - 64

    LB = view(Er, 20608, 22656, BF16, "p (s f) -> p s f", s=2)
    ACC = view(Er, 22656, 26752, BF16, "p (s f) -> p s f", s=4)
    EF = view(SQr, 0, 4096, F32, "p (s f) -> p s f", s=2)
    KCS = view(SQr, 4096, 8192, F32, "p (j f) -> p j f", j=8)
    RCP = RS
    def tview(a, b, dt, pat=None, **kw):
        return view(HU, HB + a, HB + b, dt, pat, **kw)

    ident_f = IDF[:, :]
    ident_b = CB[:, 0, :]
    ones_b = CB[:, 1, :]
    mask_le = CB[:, 2, :]
    mask_lt = CB[:, 3, :]
    ntri_b = CB[:, 4, :]
    nones_b = CB[:, 5, :]
    anti_b = CB[:, 6, :]
    anti16_b = CB[:, 7, :]
    swap_b = CB[:, 8, :]

    banks = [st.enter_context(nc.psum_tensor("pb%d" % i, [128, 512], F32)) for i in range(8)]
    bank_buf = [Buf("bank%d" % i, excl=True) for i in range(8)]
    ring = {"s": [0, 1, 2, 3], "a": [4, 5, 6, 7]}
    ring_pos = {"s": 0, "a": 0}

    def bank(pool):
        i = ring[pool][ring_pos[pool] % len(ring[pool])]
        ring_pos[pool] += 1
        return banks[i], bank_buf[i]

    NTT = len(TTILES)
    x_buf = [[Buf("x%d_%d" % (t, c)) for c in range(NCH)] for t in range(NTT)]
    h_buf = [Buf("h%d" % t) for t in range(NTT)]
    act_buf = [[Buf("a%d_%d" % (i, t)) for t in range(NTT)] for i in range(8)]
    ot_buf = [[Buf("ot%d_%d" % (c, t)) for t in range(NTT)] for c in range(NCH)]
    wgu_buf = [Buf("wgu%d" % i) for i in range(4)]
    wd_buf = [Buf("wd%d" % i) for i in range(8)]
    sg_buf = [Buf("sg%d" % i) for i in range(2)]
    xin_buf = [Buf("xin%d" % i) for i in range(4)]
    sq_buf = Buf("sq")
    rs_buf = [Buf("rs0"), Buf("rs1")]
    idf_buf = Buf("idf")
    cb_buf = Buf("cb")
    gain_buf = Buf("gain")
    gst_buf = Buf("gst")
    wqkv_buf = [Buf("wqkv%d" % i) for i in range(2)]
    wo_buf = [Buf("wo%d" % i) for i in range(2)]
    qt_buf = Buf("qt")
    qs_buf = Buf("qs")
    kt_buf = Buf("kt")
    vb_buf = Buf("vb")
    vs_buf = Buf("vs")
    pt_buf = [Buf("pt%d" % i) for i in range(4)]
    kst_buf = [Buf("kst0"), Buf("kst1")]
    vst_buf = [Buf("vst0"), Buf("vst1")]
    ktc_buf = [Buf("ktc%d" % i) for i in range(2)]
    vcc_buf = [Buf("vcc%d" % i) for i in range(2)]
    kcs_buf = Buf("kcs")
    bias_buf = Buf("bias")
    bs_buf = Buf("bs")
    nb_buf = Buf("nb")
    rcp_buf = [Buf("rcp0"), Buf("rcp1")]
    ef_buf = [Buf("ef0"), Buf("ef1")]
    lb_buf = [Buf("lb0"), Buf("lb1")]
    acc_buf = [Buf("acc%d" % i) for i in range(4)]
    tr_buf = Buf("transient")
    cnt = {"wgu": 0, "sg": 0, "xin": 0, "wqkv": 0, "wo": 0, "pt": 0, "kst": 0, "vst": 0, "kvc": 0, "rcp": 0,
           "ef": 0, "lb": 0, "accs": 0}

    S.dma("sp", "const", IDF[:, :], consts_f, writes=[idf_buf])
    S.dma("pool", "constb", CB[:, :, :].rearrange("p a b -> p (a b)"), consts_b, writes=[cb_buf])
    for i, g in enumerate([norm_ffn1, norm_mix, norm_ffn2]):
        S.dma("sp", "const", GST[i * 32:(i + 1) * 32, :], g.rearrange("l (c p) -> (l c) p", p=128), writes=[gst_buf])
    S.dma("sp", "const", GST[96:104, :], norm_final.rearrange("l (c p) -> (l c) p", p=128), writes=[gst_buf])
    pb, pbb = bank("s")
    S.pe_group([lambda e, pb=pb: e.transpose(out=pb[:, 0:104], in_=GST[:, :], identity=ident_f[0:104, 0:104])],
               reads=[gst_buf, idf_buf], writes=[pbb])
    S.op("dve", lambda e, pb=pb: e.tensor_copy(out=GAIN[:, :], in_=pb[:, 0:104]), reads=[pbb], writes=[gain_buf])

    def gain_col(kind, l, c):
        j = {0: 0, 1: 32, 2: 64, 3: 96}[kind] + (l * 8 if kind < 3 else 0) + c
        return GAIN[:, j:j + 1]

    def load_x():
        ntile = TP // 128 + 1
        for t in range(ntile):
            rows = 128 if t < TP // 128 else TS
            src = x_p[t * 128:(t + 1) * 128, :] if t < TP // 128 else x_s[:, :]
            sl = cnt["xin"] % 4
            cnt["xin"] += 1
            S.dma("sp", "xin%d" % sl, XIN[0:rows, sl, :], src, writes=[xin_buf[sl]])
            tt = min(t // 4, 4)
            for half in range(2):
                pb, pbb = bank("s")
                fns = []
                for j in range(4):
                    c = half * 4 + j
                    fns.append(lambda e, pb=pb, j=j, c=c, sl=sl, rows=rows: e.transpose(
                        out=pb[:, j * 128:j * 128 + rows], in_=XIN[0:rows, sl, c * 128:(c + 1) * 128],
                        identity=ident_f[0:rows, 0:rows]))
                S.pe_group(fns, reads=[xin_buf[sl], idf_buf], writes=[pbb])
                dst = X[:, half * 4:half * 4 + 4, t * 128:t * 128 + rows]
                srcp = pb[:, :].rearrange("p (j k) -> p j k", j=4)[:, :, 0:rows]
                wl = [x_buf[tt][half * 4 + j] for j in range(4)]
                if half == 0:
                    S.op("act", lambda e, dst=dst, srcp=srcp: e.copy(out=dst, in_=srcp), reads=[pbb], writes=wl)
                else:
                    S.op("dve", lambda e, dst=dst, srcp=srcp: e.tensor_copy(out=dst, in_=srcp), reads=[pbb], writes=wl)

    def norm(kind, l, dst, dst_bufs):
        for tt, (t0, n) in enumerate(TTILES):
            S.op("act", lambda e, t0=t0, n=n: e.activation(out=SQ[:, :, 0:n], in_=X[:, :, t0:t0 + n], func=AF.Square),
                 reads=x_buf[tt], writes=[sq_buf])
            pb, pbb = bank("s")
            fns = [lambda e, pb=pb, c=c, n=n: e.matmul(pb[:, 0:n], lhsT=ones_b, rhs=SQ[:, c, 0:n],
                                                       start=(c == 0), stop=(c == NCH - 1)) for c in range(NCH)]
            S.pe_group(fns, reads=[sq_buf, cb_buf], writes=[pbb])
            S.op("act", lambda e, pb=pb, n=n: e.activation(out=RS[:, 0, 0:n], in_=pb[:, 0:n], func=AF.Ln,
                                                           bias=EPS, scale=1.0 / D),
                 reads=[pbb], writes=[rs_buf[0]])
            S.op("act", lambda e, n=n: e.activation(out=RS[:, 1, 0:n], in_=RS[:, 0, 0:n], func=AF.Exp, scale=-0.5),
                 reads=[rs_buf[0]], writes=[rs_buf[1]])
            for c in range(NCH):
                S.op("dve", lambda e, c=c, t0=t0, n=n: e.scalar_tensor_tensor(
                    out=dst[:, c, t0:t0 + n], in0=X[:, c, t0:t0 + n], scalar=gain_col(kind, l, c),
                    in1=RS[:, 1, 0:n], op0=ALU.mult, op1=ALU.mult),
                    reads=[x_buf[tt][c], rs_buf[1], gain_buf], writes=[dst_bufs[tt]])

    FGROUPS = [list(range(0, 8)), list(range(8, 15)), list(range(15, 22))]

    def ffn(l, which):
        if "skip_ffn" in DBG:
            return
        wg_d, wu_d, wd_d = ffn_w[which]
        norm(0 if which == 1 else 2, l, H, h_buf)
        t3, n3 = TTILES[3]
        t4, n4 = TTILES[4]
        for grp in FGROUPS:
            for i, fc in enumerate(grp):
                sl = cnt["wgu"] % 4
                cnt["wgu"] += 1
                S.dma("pool", "wg%d" % sl, WGU[:, sl, 0, :, :],
                      wg_d[l].rearrange("(c p) f -> p c f", p=128)[:, :, fc * 128:(fc + 1) * 128], writes=[wgu_buf[sl]])
                S.dma("pool", "wu%d" % sl, WGU[:, sl, 1, :, :],
                      wu_d[l].rearrange("(c p) f -> p c f", p=128)[:, :, fc * 128:(fc + 1) * 128], writes=[wgu_buf[sl]])
                S.dma("pool", "wd%d" % i, WD[:, i, :], wd_d[l][fc * 128:(fc + 1) * 128, :], writes=[wd_buf[i]])

                def evac(pg, pgb, pu, pub, tt, t0, n, i=i):
                    ss = cnt["sg"] % 2
                    cnt["sg"] += 1
                    S.op("act", lambda e, pg=pg, ss=ss, n=n: e.activation(out=SG[:, ss, 0:n], in_=pg[:, 0:n], func=AF.Silu),
                         reads=[pgb], writes=[sg_buf[ss]])
                    S.op("dve", lambda e, pu=pu, ss=ss, i=i, t0=t0, n=n: e.tensor_tensor(
                        out=ACT[:, i, t0:t0 + n], in0=pu[:, 0:n], in1=SG[:, ss, 0:n], op=ALU.mult),
                        reads=[pub, sg_buf[ss]], writes=[act_buf[i][tt]])

                for tt, (t0, n) in enumerate(TTILES[0:3]):
                    pg, pgb = bank("s")
                    pu, pub = bank("s")
                    for gi, (pp, ppb) in enumerate(((pg, pgb), (pu, pub))):
                        fns = [lambda e, pp=pp, gi=gi, c=c, sl=sl, t0=t0, n=n: e.matmul(
                            pp[:, 0:n], lhsT=WGU[:, sl, gi, c, :], rhs=H[:, c, t0:t0 + n],
                            start=(c == 0), stop=(c == NCH - 1)) for c in range(NCH)]
                        S.pe_group(fns, reads=[wgu_buf[sl], h_buf[tt]], writes=[ppb])
                    evac(pg, pgb, pu, pub, tt, t0, n)
                pg, pgb = bank("s")
                pu, pub = bank("s")
                qg, qgb = bank("a")
                qu, qub = bank("a")
                for gi, (pp, ppb, qq, qqb) in enumerate(((pg, pgb, qg, qgb), (pu, pub, qu, qub))):
                    fns = []
                    for c in range(NCH):
                        fns.append(lambda e, pp=pp, gi=gi, c=c, sl=sl: e.matmul(
                            pp[:, 0:n3], lhsT=WGU[:, sl, gi, c, :], rhs=H[:, c, t3:t3 + n3],
                            start=(c == 0), stop=(c == NCH - 1)))
                        fns.append(lambda e, qq=qq, gi=gi, c=c, sl=sl: e.matmul(
                            qq[:, 0:n4], lhsT=WGU[:, sl, gi, c, :], rhs=H[:, c, t4:t4 + n4],
                            start=(c == 0), stop=(c == NCH - 1)))
                    S.pe_group(fns, reads=[wgu_buf[sl], h_buf[3], h_buf[4]], writes=[ppb, qqb])
                evac(pg, pgb, pu, pub, 3, t3, n3)
                evac(qg, qgb, qu, qub, 4, t4, n4)
            ng = len(grp)

            def xupd(po, pob, dp, tt, t0, n):
                S.op("dve", lambda e, po=po, dp=dp, t0=t0, n=n: e.scalar_tensor_tensor(
                    out=X[:, dp, t0:t0 + n], in0=po[:, 0:n], scalar=0.5, in1=X[:, dp, t0:t0 + n],
                    op0=ALU.mult, op1=ALU.add),
                    reads=[pob, x_buf[tt][dp]], writes=[x_buf[tt][dp]])

            for tt, (t0, n) in enumerate(TTILES[0:3]):
                for dp in range(NCH):
                    po, pob = bank("a")
                    fns = [lambda e, po=po, i=i, dp=dp, t0=t0, n=n, ng=ng: e.matmul(
                        po[:, 0:n], lhsT=WD[:, i, dp * 128:(dp + 1) * 128], rhs=ACT[:, i, t0:t0 + n],
                        start=(i == 0), stop=(i == ng - 1)) for i in range(ng)]
                    S.pe_group(fns, reads=[wd_buf[i] for i in range(ng)] + [act_buf[i][tt] for i in range(ng)],
                               writes=[pob])
                    xupd(po, pob, dp, tt, t0, n)
            for dp in range(NCH):
                po, pob = bank("a")
                qo, qob = bank("a")
                fns = []
                for i in range(ng):
                    fns.append(lambda e, po=po, i=i, dp=dp, ng=ng: e.matmul(
                        po[:, 0:n3], lhsT=WD[:, i, dp * 128:(dp + 1) * 128], rhs=ACT[:, i, t3:t3 + n3],
                        start=(i == 0), stop=(i == ng - 1)))
                    fns.append(lambda e, qo=qo, i=i, dp=dp, ng=ng: e.matmul(
                        qo[:, 0:n4], lhsT=WD[:, i, dp * 128:(dp + 1) * 128], rhs=ACT[:, i, t4:t4 + n4],
                        start=(i == 0), stop=(i == ng - 1)))
                S.pe_group(fns, reads=[wd_buf[i] for i in range(ng)] + [act_buf[i][3] for i in range(ng)]
                           + [act_buf[i][4] for i in range(ng)], writes=[pob, qob])
                xupd(po, pob, dp, 3, t3, n3)
                xupd(qo, qob, dp, 4, t4, n4)

    def mm(out, lhsT, rhs, start, stop):
        return lambda e: e.matmul(out, lhsT=lhsT, rhs=rhs, start=start, stop=stop, skip_group_check=True)

    def project_pair(kind, slot, c, wqkv_d, outs):
        k_out_p, v_out_p, k_out_s, v_out_s = outs
        QT0, QT1, KT, VB, VS, v1 = MV["QT0"], MV["QT1"], MV["KT"], MV["VB"], MV["VS"], MV["v1"]
        sl = c % 2
        for tt, (t0, n) in enumerate(TTILES):
            for m in range(2):
                pb, pbb = bank("s")
                fns = [mm(pb[:, 0:n], WQKV[:, sl, m, dc, :], H[:, dc, t0:t0 + n], dc == 0, dc == NCH - 1)
                       for dc in range(NCH)]
                S.pe_group(fns, reads=[wqkv_buf[sl], h_buf[tt]], writes=[pbb])
                if m == 0 and tt == 4:
                    for hh in range(2):
                        S.op("act", lambda e, pb=pb, hh=hh: e.activation(
                            out=QS[hh * 64:(hh + 1) * 64, :, hh, :],
                            in_=pb[hh * 64:(hh + 1) * 64, 0:TS].rearrange("p (s i) -> p s i", s=NS),
                            func=AF.Copy, scale=0.125), reads=[pbb], writes=[qs_buf])
                elif m == 0:
                    S.op("act", lambda e, pb=pb, t0=t0, n=n: e.activation(out=QT0[0:64, t0:t0 + n], in_=pb[0:64, 0:n],
                                                                          func=AF.Copy, scale=0.125),
                         reads=[pbb], writes=[qt_buf])
                    S.op("act", lambda e, pb=pb, t0=t0, n=n: e.activation(out=QT1[64:128, t0:t0 + n],
                                                                          in_=pb[64:128, 0:n], func=AF.Copy, scale=0.125),
                         reads=[pbb], writes=[qt_buf])
                else:
                    S.op("dve", lambda e, pb=pb, t0=t0, n=n: e.tensor_copy(out=KT[:, t0:t0 + n], in_=pb[:, 0:n]),
                         reads=[pbb], writes=[kt_buf])
        groups = [[0, 1, 2, 3], [4, 5, 6, 7], [8, 9, 10, 11], [12, 13, 14, 15], [16]]
        for gi, grp in enumerate(groups):
            for m in (2, 1):
                if m == 1 and kind == 0 and gi < 3:
                    continue
                pb, pbb = bank("s")
                fns = []
                for jj, t in enumerate(grp):
                    rows = 128 if t < 16 else TS
                    for dc in range(NCH):
                        fns.append(mm(pb[0:rows, jj * 128:(jj + 1) * 128], H[:, dc, t * 128:t * 128 + rows],
                                      WQKV[:, sl, m, dc, :], dc == 0, dc == NCH - 1))
                rows = 128 if gi < 4 else TS
                ng = len(grp)
                S.pe_group(fns, reads=[wqkv_buf[sl], h_buf[min(gi, 4)]], writes=[pbb])
                psv = pb[0:rows, 0:ng * 128].rearrange("p (j f) -> p j f", j=ng)
                need_out = not (kind == 0 and gi < 3)
                if m == 2:
                    for hh in range(2):
                        S.op("dve", lambda e, psv=psv, rows=rows, grp=grp, ng=ng, hh=hh: e.tensor_copy(
                            out=VB[0:rows, grp[0]:grp[0] + ng, hh * v1:hh * v1 + 64],
                            in_=psv[:, :, hh * 64:(hh + 1) * 64]), reads=[pbb], writes=[vb_buf])
                if need_out:
                    key = "vst" if m == 2 else "kst"
                    stg = VST if m == 2 else KST
                    sbufs = vst_buf if m == 2 else kst_buf
                    halves = [(0, 2), (2, 2)] if gi < 4 else [(0, 1)]
                    for hi_, (j0, nj) in enumerate(halves):
                        ss = cnt[key] % 2
                        cnt[key] += 1
                        eng_ = "act" if hi_ == 0 else "dve"
                        src_ = psv[:, j0:j0 + nj, :]
                        if eng_ == "act":
                            S.op("act", lambda e, src_=src_, rows=rows, nj=nj, stg=stg, ss=ss: e.copy(
                                out=stg[0:rows, ss, 0:nj, :], in_=src_), reads=[pbb], writes=[sbufs[ss]])
                        else:
                            S.op("dve", lambda e, src_=src_, rows=rows, nj=nj, stg=stg, ss=ss: e.tensor_copy(
                                out=stg[0:rows, ss, 0:nj, :], in_=src_), reads=[pbb], writes=[sbufs[ss]])
                        if gi < 4:
                            od = v_out_p if m == 2 else k_out_p
                            r0 = (gi * 512 if kind != 0 else 0) + j0 * 128
                            dst = od[r0:r0 + 256, c * 128:(c + 1) * 128].rearrange("(j p) f -> p j f", p=128)
                            S.dma("sp", "%s%d" % (key, ss), dst, stg[:, ss, :, :], reads=[sbufs[ss]])
                        else:
                            od = v_out_s if m == 2 else k_out_s
                            S.dma("sp", "%s%d" % (key, ss), od[:, c * 128:(c + 1) * 128], stg[0:TS, ss, 0, :],
                                  reads=[sbufs[ss]])
        pb, pbb = bank("s")
        fns = []
        for s in range(NS):
            for dc in range(NCH):
                fns.append(mm(pb[0:LS, s * 128:(s + 1) * 128], H[:, dc, TP + s * LS:TP + (s + 1) * LS],
                              WQKV[:, sl, 2, dc, :], dc == 0, dc == NCH - 1))
        S.pe_group(fns, reads=[wqkv_buf[sl], h_buf[4]], writes=[pbb])
        for hh in range(2):
            S.op("dve", lambda e, pb=pb, hh=hh: e.tensor_copy(
                out=VS[0:LS, :, hh * v1:hh * v1 + 64],
                in_=pb[0:LS, :].rearrange("p (s f) -> p s f", s=NS)[:, :, hh * 64:(hh + 1) * 64]),
                reads=[pbb], writes=[vs_buf])

    def load_wqkv(c, wqkv_d):
        sl = c % 2
        for m in range(3):
            S.dma("pool", "wqkv%d_%d" % (sl, m), WQKV[:, sl, m, :, :],
                  wqkv_d.rearrange("(c p) f -> p c f", p=128)[:, :, m * D + c * 128:m * D + (c + 1) * 128],
                  writes=[wqkv_buf[sl]])

    def issue_cache_dma(s, c, ck, cv, nt, sl):
        S.dma("sp", "kcs", KCS[:, 0:nt, :], ck[s][:, c * 128:(c + 1) * 128].rearrange("(j p) f -> p j f", p=128),
              writes=[kcs_buf])
        VCC, v1 = MV["VCC"], MV["v1"]
        for hh in range(2):
            S.dma("pool", "vcc%d_%d" % (sl, hh), VCC[:, sl, 0:nt, hh * v1:hh * v1 + 64],
                  cv[s][:, c * 128 + hh * 64:c * 128 + (hh + 1) * 64].rearrange("(j p) f -> p j f", p=128),
                  writes=[vcc_buf[sl]])

    def cache_transposes(nt, sl):
        for g in range(nt // 4):
            pb, pbb = bank("s")
            fns = [lambda e, pb=pb, j=j, g=g: e.transpose(out=pb[:, j * 128:(j + 1) * 128], in_=KCS[:, g * 4 + j, :],
                                                          identity=ident_f) for j in range(4)]
            S.pe_group(fns, reads=[kcs_buf, idf_buf], writes=[pbb])
            if g % 2 == 0:
                S.op("act", lambda e, pb=pb, g=g, sl=sl: e.copy(out=KTC[:, sl, g * 512:(g + 1) * 512], in_=pb[:, :]),
                     reads=[pbb], writes=[ktc_buf[sl]])
            else:
                S.op("dve", lambda e, pb=pb, g=g, sl=sl: e.tensor_copy(out=KTC[:, sl, g * 512:(g + 1) * 512], in_=pb[:, :]),
                     reads=[pbb], writes=[ktc_buf[sl]])

    def load_cache_pair(s, c, ck, cv, nt, sl):
        issue_cache_dma(s, c, ck, cv, nt, sl)
        cache_transposes(nt, sl)

    def run_softmax_units(units, LA=2):
        pend = []
        deferred = []

        def do_pv(item):
            u, slot = item
            nk, n = u["nk"], u["n"]
            c0 = u["ocol"]
            if "pv" in u:
                fns = [mm(u[key][0][:, c0:c0 + LS], vv, PT[0:nk, slot, hh * LS:(hh + 1) * LS], u["first"], u["last"])
                       for hh, (key, vv) in enumerate(u["pv"])]
                S.pe_group(fns, reads=[pt_buf[slot]] + u["v_reads"], writes=[u["o"][1], u["d"][1]])
            else:
                ob, obuf = u["o"] if u["hh"] == 0 else u["d"]
                fns = [mm(ob[:, c0:c0 + n], u["v"], PT[0:nk, slot, 0:n], u["first"], u["last"])]
                S.pe_group(fns, reads=[pt_buf[slot]] + u["v_reads"], writes=[obuf])
            if u.get("fin") is not None:
                fb = u["fin"]()
                if fb is not None:
                    deferred.append([2, fb])
            if u.get("post") is not None:
                u["post"]()

        for u in units:
            for d in deferred:
                d[0] -= 1
            while deferred and deferred[0][0] <= 0:
                deferred.pop(0)[1]()
            if u.get("pre") is not None:
                u["pre"]()
            nk, n = u["nk"], u["n"]
            sb, sbb = bank("s")
            nx = len(u["extras"])
            fns = [mm(sb[0:nk, 0:n], u["kT"], u["q"], True, nx == 0)]
            for xi, (xl, xr, xo, xn) in enumerate(u["extras"]):
                fns.append(mm(sb[0:nk, xo:xo + xn], xl, xr, False, xi == nx - 1))
            S.pe_group(fns, reads=u["qk_reads"] + [cb_buf], writes=[sbb])
            slot = cnt["pt"] % 4
            cnt["pt"] += 1
            bias = u.get("bias")
            if "bias2" in u:
                for hh, bb in enumerate(u["bias2"]):
                    S.op("act", lambda e, sb=sb, nk=nk, slot=slot, hh=hh, bb=bb: e.activation(
                        out=PT[0:nk, slot, hh * LS:(hh + 1) * LS], in_=sb[0:nk, hh * LS:(hh + 1) * LS], func=AF.Exp,
                        bias=bb), reads=[sbb, nb_buf], writes=[pt_buf[slot]])
            elif bias is None:
                S.op("act", lambda e, sb=sb, nk=nk, n=n, slot=slot: e.activation(
                    out=PT[0:nk, slot, 0:n], in_=sb[0:nk, 0:n], func=AF.Exp), reads=[sbb], writes=[pt_buf[slot]])
            else:
                S.op("act", lambda e, sb=sb, nk=nk, n=n, slot=slot, bias=bias: e.activation(
                    out=PT[0:nk, slot, 0:n], in_=sb[0:nk, 0:n], func=AF.Exp, bias=bias),
                    reads=[sbb, nb_buf], writes=[pt_buf[slot]])
            pend.append((u, slot))
            if len(pend) > LA:
                do_pv(pend.pop(0))
        while pend:
            do_pv(pend.pop(0))
        while deferred:
            deferred.pop(0)[1]()

    def softmax_fin(b0, b0buf, b1, b1buf, c, col0, n, tts):
        def fin():
            rs = cnt["rcp"] % 2
            cnt["rcp"] += 1
            slot = cnt["pt"] % 4
            cnt["pt"] += 1
            S.op("act", lambda e: e.copy(out=RCP[0:64, rs, 0:n], in_=b1[0:64, 0:n]), reads=[b1buf], writes=[rcp_buf[rs]])
            S.op("act", lambda e: e.copy(out=RCP[64:128, rs, 0:n], in_=b0[64:128, 0:n]), reads=[b0buf],
                 writes=[rcp_buf[rs]])
            S.op("dve", lambda e: e.reciprocal(out=RCP[:, rs, 0:n], in_=RCP[:, rs, 0:n]), reads=[rcp_buf[rs]],
                 writes=[rcp_buf[rs]])
            S.op("act", lambda e: e.copy(out=PT[0:64, slot, 0:n], in_=b0[0:64, 0:n]), reads=[b0buf], writes=[pt_buf[slot]])
            S.op("act", lambda e: e.copy(out=PT[64:128, slot, 0:n], in_=b1[64:128, 0:n]), reads=[b1buf],
                 writes=[pt_buf[slot]])

            def fin_b():
                xs, xsb = bank("s")
                S.pe_group([mm(xs[:, 0:n], swap_b, PT[:, slot, 0:n], True, True)], reads=[pt_buf[slot], cb_buf],
                           writes=[xsb])
                S.op("dve", lambda e: e.tensor_tensor(out=OT[:, c, col0:col0 + n], in0=xs[:, 0:n], in1=RCP[:, rs, 0:n],
                                                      op=ALU.mult),
                     reads=[xsb, rcp_buf[rs]], writes=[ot_buf[c][t] for t in tts])
            return fin_b
        return fin

    def run_stick_units(units):
        stA = []
        stB = []
        acc_state = {}

        def do_cs(item):
            u, sb, sbb, lslot, accprev = item
            nk, n, sc = u["nk"], u["n"], u["scol"]
            fns = [mm(sb[0:nk, sc:sc + n], ntri_b[0:nk, 0:nk], LB[0:nk, lslot, sc:sc + n], False, accprev is None)]
            rd = [lb_buf[lslot], cb_buf]
            if accprev is not None:
                aslot, ank = accprev
                fns.append(mm(sb[0:nk, sc:sc + n], nones_b[0:ank, 0:nk], ACC[0:ank, aslot, sc:sc + n], False, True))
                rd.append(acc_buf[aslot])
            S.pe_group(fns, reads=rd, writes=[sbb])
            slot = cnt["pt"] % 4
            cnt["pt"] += 1
            S.op("act", lambda e: e.activation(out=PT[0:nk, slot, sc:sc + n], in_=sb[0:nk, sc:sc + n], func=AF.Exp),
                 reads=[sbb], writes=[pt_buf[slot]])
            stB.append((u, slot))

        def do_pv(item):
            u, slot = item
            nk, n, sc = u["nk"], u["n"], u["scol"]
            ob, obuf = u["o"]
            pb0, c0 = u["pbase"], u["ocol"]
            if "pv" in u:
                fns = [mm(ob[hh * 64:(hh + 1) * 64, c0:c0 + LS], vv, PT[0:nk, slot, hh * LS:(hh + 1) * LS],
                          u["first"], u["last"]) for hh, (key, vv) in enumerate(u["pv"])]
            else:
                fns = [mm(ob[pb0:pb0 + 64, c0:c0 + n], u["v"], PT[0:nk, slot, sc:sc + n], u["first"], u["last"])]
            S.pe_group(fns, reads=[pt_buf[slot]] + u["v_reads"], writes=[obuf])
            if u.get("fin") is not None:
                u["fin"]()
            if u.get("post") is not None:
                u["post"]()

        for u in units:
            if u.get("pre") is not None:
                u["pre"]()
            nk, n, sc = u["nk"], u["n"], u["scol"]
            sb, sbb = bank("s")
            fns = [mm(sb[0:nk, sc:sc + n], u["kT"], u["q"], True, False)]
            for xi, (xl, xr, xo, xn) in enumerate(u["extras"]):
                fns.append(mm(sb[0:nk, sc + xo:sc + xo + xn], xl, xr, False, False))
            S.pe_group(fns, reads=u["qk_reads"] + [cb_buf], writes=[sbb])
            es = cnt["ef"] % 2
            cnt["ef"] += 1
            S.op("act", lambda e, sb=sb, nk=nk, n=n, sc=sc, es=es: e.activation(
                out=EF[0:nk, es, sc:sc + n], in_=sb[0:nk, sc:sc + n], func=AF.Exp), reads=[sbb], writes=[ef_buf[es]])
            ls = cnt["lb"] % 2
            cnt["lb"] += 1
            S.op("act", lambda e, nk=nk, n=n, sc=sc, es=es, ls=ls: e.activation(
                out=LB[0:nk, ls, sc:sc + n], in_=EF[0:nk, es, sc:sc + n], func=AF.Ln, bias=1.0),
                reads=[ef_buf[es]], writes=[lb_buf[ls]])
            sid = u["seq"]
            prev = None if u["seq_first"] else acc_state[sid]
            stA.append((u, sb, sbb, ls, prev))
            if not u["seq_last"]:
                base = 2 * u["hh"]
                W = sc + n
                rot = None
                if "pv" in u:
                    rot = cnt["accs"] % 4
                    cnt["accs"] += 1
                if prev is None:
                    ns_ = base if rot is None else rot
                    S.op("dve", lambda e, nk=nk, n=n, sc=sc, ls=ls, ns_=ns_: e.tensor_copy(
                        out=ACC[0:nk, ns_, sc:sc + n], in_=LB[0:nk, ls, sc:sc + n]),
                        reads=[lb_buf[ls]], writes=[acc_buf[ns_]])
                    acc_state[sid] = (ns_, nk)
                else:
                    pslot, pnk = prev
                    ns_ = base + (1 - (pslot - base)) if rot is None else rot
                    if pnk < nk:
                        S.op("dve", lambda e, nk=nk, n=n, sc=sc, ls=ls, ns_=ns_: e.tensor_copy(
                            out=ACC[0:nk, ns_, sc:sc + n], in_=LB[0:nk, ls, sc:sc + n]),
                            reads=[lb_buf[ls]], writes=[acc_buf[ns_]])
                        S.op("dve", lambda e, pnk=pnk, n=n, sc=sc, ns_=ns_, pslot=pslot: e.tensor_tensor(
                            out=ACC[0:pnk, ns_, sc:sc + n], in0=ACC[0:pnk, ns_, sc:sc + n],
                            in1=ACC[0:pnk, pslot, sc:sc + n], op=ALU.add),
                            reads=[acc_buf[ns_], acc_buf[pslot]], writes=[acc_buf[ns_]])
                    else:
                        S.op("dve", lambda e, nk=nk, n=n, sc=sc, ls=ls, ns_=ns_, pslot=pslot: e.tensor_tensor(
                            out=ACC[0:nk, ns_, sc:sc + n], in0=ACC[0:nk, pslot, sc:sc + n], in1=LB[0:nk, ls, sc:sc + n],
                            op=ALU.add),
                            reads=[acc_buf[pslot], lb_buf[ls]], writes=[acc_buf[ns_]])
                    acc_state[sid] = (ns_, max(nk, pnk))
                if sc > 0:
                    S.op("dve", lambda e, sc=sc, ns_=ns_: e.memset(ACC[:, ns_, 0:sc], 0.0), writes=[acc_buf[ns_]])
            if len(stA) > 1:
                do_cs(stA.pop(0))
            if len(stB) > 1:
                do_pv(stB.pop(0))
        while stA:
            do_cs(stA.pop(0))
        while stB:
            do_pv(stB.pop(0))

    def copy_fin(ob, obuf, c, col0, n, tts):
        def fin():
            S.op("dve", lambda e: e.tensor_copy(out=OT[:, c, col0:col0 + n], in_=ob[:, 0:n]),
                 reads=[obuf], writes=[ot_buf[c][t] for t in tts])
        return fin

    def prep_A(slot):
        RT = tview(0, 320, F32, "p (j h) -> p j h", j=5)
        TBT = tview(512, 512 + 513 * 4, F32)
        EE = tview(4096, 4096 + 3072, F32)
        S.dma("sp", "prep", RT[:, 0:4, :], a_rel[slot][0:512, :].rearrange("(j p) h -> p j h", p=128), writes=[tr_buf])
        S.dma("sp", "prep", RT[0:1, 4, :], a_rel[slot][512:513, :], writes=[tr_buf])
        pb, pbb = bank("s")
        pb2, pbb2 = bank("s")
        fns = [lambda e, j=j: e.transpose(out=pb[0:16, j * 128:(j + 1) * 128], in_=RT[:, j, :], identity=ident_f)
               for j in range(4)]
        fns.append(lambda e: e.transpose(out=pb2[0:16, 0:1], in_=RT[0:1, 4, :], identity=ident_f[0:1, 0:1]))
        S.pe_group(fns, reads=[tr_buf, idf_buf], writes=[pbb, pbb2])
        S.op("dve", lambda e: e.tensor_copy(out=TBT[0:16, 0:512], in_=pb[0:16, :]), reads=[pbb], writes=[tr_buf])
        S.op("dve", lambda e: e.tensor_copy(out=TBT[0:16, 512:513], in_=pb2[0:16, 0:1]), reads=[pbb2], writes=[tr_buf])
        S.op("dve", lambda e: e.tensor_copy(out=EE[0:16, 0:384], in_=TBT[0:16, 129:513]), reads=[tr_buf], writes=[tr_buf])
        S.op("dve", lambda e: e.tensor_copy(out=EE[0:16, 384:768], in_=TBT[0:16, 512:513].broadcast_to([16, 384])),
             reads=[tr_buf], writes=[tr_buf])
        S.dma("sp", "prep", bass.AP(EH, 0, [[768, 16], [1, 768]]), EE[0:16, :], reads=[tr_buf], writes=[tr_buf])

    def load_bias_A(c):
        BIAS, BS = MV["BIAS"], MV["BS"]
        for hh in range(2):
            h = 2 * c + hh
            S.dma("pool", "biasA", BIAS[:, hh, :], bass.AP(EH, h * 768, [[1, 128], [1, 640]]),
                  reads=[tr_buf], writes=[bias_buf])
        S.op("dve", lambda e: e.memset(BIAS[64:128, :, 576:640], NEG), writes=[bias_buf])
        S.op("dve", lambda e: e.memset(BIAS[0:64, :, 0:64], NEG), writes=[bias_buf])
        for hh in range(2):
            h = 2 * c + hh
            for j in range(5):
                rows = 128 if j < 4 else LS
                S.dma("pool", "biasS", BS[0:rows, j, hh, :],
                      bass.AP(EH, h * 768 + (512 - 128 * j if j < 4 else 112), [[1, rows], [1, LS]]),
                      reads=[tr_buf], writes=[bs_buf])

    def prep_B(slot):
        NBP, NBS = MV["NBP"], MV["NBS"]
        WF = tview(0, 256, BF16, "p (c h) -> p c h", c=NCH)
        BF_ = tview(256, 320, F32)
        ZF = tview(512, 512 + 1088, F32, "p (t h) -> p t h", t=17)
        LFN = tview(2048, 2048 + 1088, F32, "p (t h) -> p t h", t=17)
        LFT = tview(4096, 4096 + TT * 4, F32)
        TMPN = tview(4096 + 8448, 4096 + 8448 + 8192, F32)
        CL = tview(20992, 20992 + 512, F32, "p (j h) -> p j h", j=8)
        LFS = tview(21504, 21504 + 4160, F32)
        TMPS = tview(25664, 25664 + 4160, F32)
        S.dma("pool", "prepw", WF[:, :, :], b_wf[slot].rearrange("(c p) h -> p c h", p=128), writes=[tr_buf])
        S.dma("sp", "prep", BF_[:, :], bass.AP(b_bf.tensor, slot * NH, [[0, 128], [1, NH]]), writes=[tr_buf])
        pb, pbb = bank("s")
        fns = []
        for t in range(17):
            rows = 128 if t < 16 else TS
            for dc in range(NCH):
                fns.append(mm(pb[0:rows, t * 16:(t + 1) * 16], H[:, dc, t * 128:t * 128 + rows], WF[:, dc, :],
                              dc == 0, dc == NCH - 1))
        S.pe_group(fns, reads=[tr_buf] + h_buf, writes=[pbb])
        psv = pb[:, 0:272].rearrange("p (t h) -> p t h", t=17)
        S.op("dve", lambda e: e.tensor_tensor(out=ZF[:, :, :], in0=psv, in1=BF_[:, :].unsqueeze(1).broadcast_to([128, 17, 16]),
                                              op=ALU.add), reads=[pbb, tr_buf], writes=[tr_buf])
        S.op("act", lambda e: e.activation(out=ZF[:, :, :], in_=ZF[:, :, :], func=AF.Exp, scale=-1.0),
             reads=[tr_buf], writes=[tr_buf])
        S.op("act", lambda e: e.activation(out=ZF[:, :, :], in_=ZF[:, :, :], func=AF.Ln, bias=1.0),
             reads=[tr_buf], writes=[tr_buf])
        S.op("dve", lambda e: e.tensor_scalar(out=LFN[:, :, :], in0=ZF[:, :, :], scalar1=-1.0, scalar2=None, op0=ALU.mult),
             reads=[tr_buf], writes=[tr_buf])
        S.dma("sp", "prep", o_b_fp.rearrange("(t p) h -> p t h", p=128), LFN[:, 0:16, :], reads=[tr_buf])
        S.dma("sp", "prep", o_b_fs, LFN[0:TS, 16, :], reads=[tr_buf])
        for g in range(5):
            pb, pbb = bank("s")
            tl = list(range(g * 4, min(g * 4 + 4, 17)))
            fns = []
            for jj, t in enumerate(tl):
                rows = 128 if t < 16 else TS
                fns.append(lambda e, pb=pb, jj=jj, t=t, rows=rows: e.transpose(
                    out=pb[0:16, jj * 128:jj * 128 + rows], in_=LFN[0:rows, t, :], identity=ident_f[0:rows, 0:rows]))
            S.pe_group(fns, reads=[tr_buf, idf_buf], writes=[pbb])
            ncol = 512 if g < 4 else TS
            S.op("dve", lambda e, pb=pb, g=g, ncol=ncol: e.tensor_copy(out=LFT[0:16, g * 512:g * 512 + ncol],
                                                                        in_=pb[0:16, 0:ncol]), reads=[pbb], writes=[tr_buf])
        S.op("dve", lambda e: e.tensor_tensor_scan(out=LFT[0:16, 0:TP], data0=LFT[0:16, 0:TP], data1=LFT[0:16, 0:TP],
                                                   initial=0.0, op0=ALU.add, op1=ALU.bypass),
             reads=[tr_buf], writes=[tr_buf])
        for Q in range(4):
            nq = (4 * Q + 4) * 128
            ref = 512 * Q + 511
            S.op("dve", lambda e, nq=nq, ref=ref: e.tensor_scalar(out=TMPN[0:16, 0:nq], in0=LFT[0:16, 0:nq],
                                                                  scalar1=LFT[0:16, ref:ref + 1], scalar2=-1.0,
                                                                  op0=ALU.subtract, op1=ALU.mult),
                 reads=[tr_buf], writes=[tr_buf])
            pb, pbb = bank("s")
            nt = 4 * Q + 4
            fns = [lambda e, pb=pb, j=j: e.transpose(out=pb[:, j * 16:(j + 1) * 16], in_=TMPN[0:16, j * 128:(j + 1) * 128],
                                                     identity=ident_f[0:16, 0:16]) for j in range(nt)]
            S.pe_group(fns, reads=[tr_buf, idf_buf], writes=[pbb])
            base = nbp_base(Q)
            S.op("dve", lambda e, pb=pb, nt=nt, base=base: e.tensor_copy(
                out=NBP[:, base:base + nt, :], in_=pb[:, 0:nt * 16].rearrange("p (t h) -> p t h", t=nt)),
                reads=[pbb], writes=[nb_buf])
        for s in range(NS):
            S.dma("sp", "prep", CL[:, :, :], cb_f[s].rearrange("(j p) h -> p j h", p=128), writes=[tr_buf])
            for g in range(2):
                pb, pbb = bank("s")
                fns = [lambda e, pb=pb, j=j, g=g: e.transpose(out=pb[0:16, j * 128:(j + 1) * 128], in_=CL[:, g * 4 + j, :],
                                                              identity=ident_f) for j in range(4)]
                S.pe_group(fns, reads=[tr_buf, idf_buf], writes=[pbb])
                S.op("dve", lambda e, pb=pb, g=g: e.tensor_copy(out=LFS[0:16, g * 512:(g + 1) * 512], in_=pb[0:16, :]),
                     reads=[pbb], writes=[tr_buf])
            S.op("dve", lambda e, s=s: e.tensor_copy(out=LFS[0:16, 1024:1040], in_=LFT[0:16, TP + s * LS:TP + (s + 1) * LS]),
                 reads=[tr_buf], writes=[tr_buf])
            S.op("dve", lambda e: e.tensor_tensor_scan(out=LFS[0:16, 0:1040], data0=LFS[0:16, 0:1040],
                                                       data1=LFS[0:16, 0:1040], initial=0.0, op0=ALU.add, op1=ALU.bypass),
                 reads=[tr_buf], writes=[tr_buf])
            S.op("dve", lambda e: e.tensor_scalar(out=TMPS[0:16, 0:1040], in0=LFS[0:16, 0:1040],
                                                  scalar1=LFS[0:16, 1039:1040], scalar2=-1.0,
                                                  op0=ALU.subtract, op1=ALU.mult), reads=[tr_buf], writes=[tr_buf])
            pb, pbb = bank("s")
            fns = [lambda e, pb=pb, j=j: e.transpose(out=pb[:, j * 16:(j + 1) * 16], in_=TMPS[0:16, j * 128:(j + 1) * 128],
                                                     identity=ident_f[0:16, 0:16]) for j in range(8)]
            fns.append(lambda e, pb=pb: e.transpose(out=pb[0:16, 128:144], in_=TMPS[0:16, 1024:1040],
                                                    identity=ident_f[0:16, 0:16]))
            S.pe_group(fns, reads=[tr_buf, idf_buf], writes=[pbb])
            S.op("dve", lambda e, pb=pb, s=s: e.tensor_copy(out=NBS[:, s, :, :],
                                                            in_=pb[:, 0:144].rearrange("p (t h) -> p t h", t=9)),
                 reads=[pbb], writes=[nb_buf])

    def nbp_base(Q):
        return sum(4 * q + 4 for q in range(Q))

    def prompt_units(kind, c):
        QT0, QT1, KT, VB, v1 = MV["QT0"], MV["QT1"], MV["KT"], MV["VB"], MV["v1"]
        BIAS, NBP = MV.get("BIAS"), MV.get("NBP")
        units = []
        for Q in range(4):
            ob, obuf = bank("a")
            if kind != 2:
                dbk, dbuf = bank("a")
            if kind == 0:
                jl = list(range(max(0, 4 * Q - 4), 4 * Q + 4))
            elif kind == 1:
                jl = list(range(0, 4 * Q + 4))
            else:
                jl = list(range(4 * Q + 3, -1, -1))
            for ji, j in enumerate(jl):
                for hh in range(2):
                    h = 2 * c + hh
                    pb0 = 64 * hh
                    if kind == 0:
                        lo = max(0, 128 * j - 512 * Q)
                        hi = min(512, 128 * j + 640 - 512 * Q)
                    else:
                        lo = max(0, 128 * j - 512 * Q)
                        hi = 512
                    n = hi - lo
                    q0 = 512 * Q + lo
                    u = dict(nk=128, n=n, kT=KT[:, j * 128:(j + 1) * 128], q=(QT0, QT1)[hh][:, q0:q0 + n],
                             qk_reads=[kt_buf, qt_buf], extras=[],
                             v=(VB[:, j, hh * 64:(hh + 1) * 64] if kind == 2 else VB[:, j, hh * 64:hh * 64 + 128]),
                             v_reads=[vb_buf],
                             pbase=pb0, ocol=lo, scol=lo, first=(ji == 0), last=(ji == len(jl) - 1), hh=hh)
                    if kind == 0:
                        b0 = q0 - 128 * j
                        u["extras"].append((anti_b, BIAS[:, hh, b0:b0 + n], 0, n))
                        u["qk_reads"] = [kt_buf, qt_buf, bias_buf]
                    else:
                        r = j - 4 * Q
                        if r >= 0:
                            u["extras"].append((ident_b, mask_le if kind == 1 else mask_lt, 0, 128))
                    if kind == 1:
                        u["bias"] = NBP[:, nbp_base(Q) + j, h:h + 1]
                    u["o"] = (ob, obuf)
                    if kind != 2:
                        u["d"] = (dbk, dbuf)
                    else:
                        u["seq"] = ("p", Q, hh)
                        u["seq_first"] = (ji == 0)
                        u["seq_last"] = (ji == len(jl) - 1)
                    if ji == len(jl) - 1 and hh == 1:
                        if kind != 2:
                            u["fin"] = softmax_fin(ob, obuf, dbk, dbuf, c, 512 * Q, 512, [Q])
                        else:
                            u["fin"] = copy_fin(ob, obuf, c, 512 * Q, 512, [Q])
                    units.append(u)
        return units

    def sample_units(kind, c, s, ksl, nt, ob, obuf, dbk, dbuf, is_last_seq):
        KT, VS, VCC = MV["KT"], MV["VS"], MV["VCC"]
        BS, NBS = MV.get("BS"), MV.get("NBS")
        units = []
        tl = list(range(nt)) + ["new"]
        if kind == 2:
            tl = ["new"] + list(range(nt - 1, -1, -1))
        q0 = TP + s * LS
        for ji, j in enumerate(tl):
            if j == "new":
                nk = LS
                kT = KT[:, q0:q0 + LS]
                qk_reads = [kt_buf, qs_buf]
                v_reads = [vs_buf]
                vsrc = lambda a, b: VS[0:LS, s, a:b]
            else:
                nk = 128
                kT = KTC[:, ksl, j * 128:(j + 1) * 128]
                qk_reads = [ktc_buf[ksl], qs_buf]
                v_reads = [vcc_buf[ksl]]
                vsrc = lambda a, b, j=j: VCC[:, ksl, j, a:b]
            if kind == 2:
                pv = [("o", vsrc(0, 64)), ("o", vsrc(64, 128))]
            else:
                pv = [("o", vsrc(0, 128)), ("d", vsrc(64, 192))]
            u = dict(nk=nk, n=2 * LS, kT=kT, q=QS[:, s, :, :].rearrange("p h i -> p (h i)"), qk_reads=qk_reads, extras=[],
                     v=None, v_reads=v_reads, pbase=0, ocol=s * LS, scol=0, first=(ji == 0), last=(ji == len(tl) - 1),
                     hh=0, pv=pv)
            jidx = nt if j == "new" else j
            if kind == 0:
                ja = 4 if j == "new" else j
                u["extras"].append(((anti_b if nk == 128 else anti16_b[0:LS, 0:LS]),
                                    BS[0:nk, ja, :, :].rearrange("p h i -> p (h i)"), 0, 2 * LS))
                u["qk_reads"] = qk_reads + [bs_buf]
            elif j == "new":
                mk = (mask_le if kind == 1 else mask_lt)[0:LS, 0:LS]
                u["extras"].append((ident_b[0:LS, 0:LS], mk, 0, LS))
                u["extras"].append((ident_b[0:LS, 0:LS], mk, LS, LS))
            if kind == 1:
                u["bias2"] = [NBS[0:nk, s, jidx, 2 * c + hh:2 * c + hh + 1] for hh in range(2)]
            u["o"] = (ob, obuf)
            if kind != 2:
                u["d"] = (dbk, dbuf)
            else:
                u["seq"] = ("s", s)
                u["seq_first"] = (ji == 0)
                u["seq_last"] = (ji == len(tl) - 1)
            if is_last_seq and ji == len(tl) - 1:
                if kind != 2:
                    u["fin"] = softmax_fin(ob, obuf, dbk, dbuf, c, TP, TS, [4])
                else:
                    u["fin"] = copy_fin(ob, obuf, c, TP, TS, [4])
            units.append(u)
        return units

    def oproj(wo_d, swapped):
        WO = MV["WO"]
        wsrc = wo_d.rearrange("(c p) f -> p c f", p=128)
        for dp in range(NCH):
            sl = cnt["wo"] % 2
            cnt["wo"] += 1
            if swapped:
                S.dma("pool", "wo%d_a" % sl, WO[0:64, sl, :, :], wsrc[64:128, :, dp * 128:(dp + 1) * 128],
                      writes=[wo_buf[sl]])
                S.dma("pool", "wo%d_b" % sl, WO[64:128, sl, :, :], wsrc[0:64, :, dp * 128:(dp + 1) * 128],
                      writes=[wo_buf[sl]])
            else:
                S.dma("pool", "wo%d_a" % sl, WO[:, sl, :, :], wsrc[:, :, dp * 128:(dp + 1) * 128], writes=[wo_buf[sl]])
            for tt, (t0, n) in enumerate(TTILES):
                po, pob = bank("a")
                fns = [mm(po[:, 0:n], WO[:, sl, c, :], OT[:, c, t0:t0 + n], c == 0, c == NCH - 1) for c in range(NCH)]
                S.pe_group(fns, reads=[wo_buf[sl]] + [ot_buf[c][tt] for c in range(NCH)], writes=[pob])
                S.op("dve", lambda e, po=po, dp=dp, t0=t0, n=n: e.tensor_tensor(
                    out=X[:, dp, t0:t0 + n], in0=po[:, 0:n], in1=X[:, dp, t0:t0 + n], op=ALU.add),
                    reads=[pob, x_buf[tt][dp]], writes=[x_buf[tt][dp]])

    def mixer(l):
        kind, slot = l % 3, l // 3
        norm(1, l, H, h_buf)
        S.barrier()
        set_views(kind)
        QT0, QT1 = MV["QT0"], MV["QT1"]
        if kind == 0:
            wqkv_d, wo_d = a_qkv[slot], a_o[slot]
            outs = (o_a_kp[slot], o_a_vp[slot], o_a_ks[slot], o_a_vs[slot])
            ck, cv, nt = ca_k[slot], ca_v[slot], 4
            prep_A(slot)
        elif kind == 1:
            wqkv_d, wo_d = b_qkv[slot], b_o[slot]
            outs = (o_b_kp, o_b_vp, o_b_ks, o_b_vs)
            ck, cv, nt = cb_k, cb_v, 8
            prep_B(slot)
        else:
            wqkv_d, wo_d = c_qkv[slot], c_o[slot]
            outs = (o_c_kp, o_c_vp, o_c_ks, o_c_vs)
            ck, cv, nt = cc_k, cc_v, 8
        S.barrier()
        S.op("dve", lambda e: e.memset(QS[:, :, :, :], 0.0), writes=[qs_buf])
        S.op("dve", lambda e: e.memset(QT0[64:128, :], 0.0), writes=[qt_buf])
        S.op("dve", lambda e: e.memset(QT1[0:64, :], 0.0), writes=[qt_buf])
        if kind != 2:
            VB_, VS_, VCC_ = MV["VB"], MV["VS"], MV["VCC"]
            S.op("dve", lambda e: e.memset(VB_[:, :, 64:128], 1.0), writes=[vb_buf])
            S.op("dve", lambda e: e.memset(VS_[:, :, 64:128], 1.0), writes=[vs_buf])
            S.op("dve", lambda e: e.memset(VCC_[:, :, :, 64:128], 1.0), writes=vcc_buf)
        load_wqkv(0, wqkv_d)
        for c in range(NCH):
            if "no_sample" not in DBG:
                issue_cache_dma(0, c, ck, cv, nt, 0)
            project_pair(kind, slot, c, wqkv_d, outs)
            if c + 1 < NCH:
                load_wqkv(c + 1, wqkv_d)
            if kind == 0 and "no_bias" not in DBG:
                load_bias_A(c)
            units = prompt_units(kind, c)
            if "no_bias" in DBG:
                for u in units:
                    u["extras"] = [x for x in u["extras"] if x[0] is not anti_b]
                    u["qk_reads"] = [kt_buf, qt_buf]
            ob, obuf = bank("a")
            dbk = dbuf = None
            if kind != 2:
                dbk, dbuf = bank("a")
            su = []
            for s in range(NS):
                su.append(sample_units(kind, c, s, s % 2, nt, ob, obuf, dbk, dbuf, s == NS - 1))

            def mk_load(s, c=c):
                return lambda: load_cache_pair(s, c, ck, cv, nt, s % 2)

            def mk_pre(c=c):
                def pre():
                    cache_transposes(nt, 0)
                    load_cache_pair(1, c, ck, cv, nt, 1)
                return pre
            if "no_sample" not in DBG:
                units[0]["pre"] = mk_pre()
                for s in range(NS - 2):
                    su[s][-1]["post"] = mk_load(s + 2)
                for s in range(NS):
                    units += su[s]
            if "no_attn" in DBG:
                units = []
            if kind != 2:
                run_softmax_units(units)
            else:
                run_stick_units(units)
        oproj(wo_d, kind != 2)
        S.barrier()

    yt_buf = [Buf("yt%d" % t) for t in range(NTT)]

    def final_out():
        S.barrier()
        norm(3, 0, YT, yt_buf)
        S.barrier()
        ntile = TP // 128 + 1
        for t in range(ntile):
            rows = 128 if t < TP // 128 else TS
            dstd = y_p[t * 128:(t + 1) * 128, :] if t < TP // 128 else y_s[:, :]
            tt = min(t // 4, 4)
            sl = cnt["xin"] % 4
            cnt["xin"] += 1
            for half in range(2):
                pb, pbb = bank("s")
                fns = []
                for j in range(4):
                    c = half * 4 + j
                    fns.append(lambda e, pb=pb, j=j, c=c, t=t, rows=rows: e.transpose(
                        out=pb[0:rows, j * 128:(j + 1) * 128], in_=YT[:, c, t * 128:t * 128 + rows],
                        identity=ident_f))
                S.pe_group(fns, reads=[yt_buf[tt], idf_buf], writes=[pbb])
                dst = XIN[0:rows, sl, half * 512:(half + 1) * 512]
                if half == 0:
                    S.op("act", lambda e, dst=dst, pb=pb, rows=rows: e.copy(out=dst, in_=pb[0:rows, :]),
                         reads=[pbb], writes=[xin_buf[sl]])
                else:
                    S.op("dve", lambda e, dst=dst, pb=pb, rows=rows: e.tensor_copy(out=dst, in_=pb[0:rows, :]),
                         reads=[pbb], writes=[xin_buf[sl]])
            S.dma("sp", "xin%d" % sl, dstd, XIN[0:rows, sl, :], reads=[xin_buf[sl]])

    load_x()
    S.barrier()
    done = False
    for l in range(DEPTH):
        S.new_epoch()
        for stage in ("ffn1", "mix", "ffn2"):
            if stage == "ffn1":
                ffn(l, 1)
            elif stage == "mix":
                mixer(l)
            else:
                ffn(l, 2)
            if stop_after == (l, stage):
                done = True
                break
        if done:
            break
    S.new_epoch()
    final_out()
    S.final_wait("sp")
    S.replay()
    st.close()
    return nc


W_NAMES = ["norm_ffn1", "norm_mix", "norm_ffn2", "ffn1_gate", "ffn1_up", "ffn1_down", "ffn2_gate", "ffn2_up",
           "ffn2_down", "a_w_qkv", "a_w_o", "a_rel_bias", "b_w_qkv", "b_w_o", "b_w_f", "b_b_f", "c_w_qkv", "c_w_o"]

STOP_AFTER = None
DBG = set()


def kernel(**inputs):
    n = 8
    f = lambda a: np.ascontiguousarray(np.asarray(a, dtype=np.float32))
    nc = build(STOP_AFTER)
    cf, cb = _consts_np()
    shared = {k: f(inputs[k]) for k in W_NAMES if not ("skip_ffn" in DBG and k.startswith("ffn"))}
    shared["norm_final"] = f(inputs["norm_final"]).reshape(1, D)
    shared["consts_f"] = cf
    shared["consts_b"] = cb
    in_maps = []
    for c in range(n):
        m = dict(shared)
        sl = slice(NS * c, NS * (c + 1))
        m["x_p"] = f(inputs["x_prompt"][c])
        m["x_s"] = f(inputs["x_sample"][sl]).reshape(TS, D)
        m["ca_k"] = f(inputs["cache_a_k"][:, sl]).reshape(2, NS, 512, D)
        m["ca_v"] = f(inputs["cache_a_v"][:, sl]).reshape(2, NS, 512, D)
        m["cb_k"] = f(inputs["cache_b_k"][0, sl]).reshape(NS, 1024, D)
        m["cb_v"] = f(inputs["cache_b_v"][0, sl]).reshape(NS, 1024, D)
        m["cb_f"] = f(inputs["cache_b_logf"][0, sl])
        m["cc_k"] = f(inputs["cache_c_k"][0, sl]).reshape(NS, 1024, D)
        m["cc_v"] = f(inputs["cache_c_v"][0, sl]).reshape(NS, 1024, D)
        in_maps.append(m)
    res = run_bass_kernel_spmd(nc, in_maps, core_ids=list(range(n)))
    R = res.results

    def gp(name, shape):
        return np.stack([R[c][name].reshape(shape) for c in range(n)], axis=0)

    y_prompt = gp("y_p", (TP, D))
    y_sample = gp("y_s", (NS, LS, D)).reshape(n * NS, LS, D)
    a_kp = gp("a_kp", (2, 512, NH, HD)).transpose(1, 0, 2, 3, 4)
    a_vp = gp("a_vp", (2, 512, NH, HD)).transpose(1, 0, 2, 3, 4)
    a_ks = gp("a_ks", (2, NS, LS, NH, HD)).transpose(1, 0, 2, 3, 4, 5).reshape(2, n * NS, LS, NH, HD)
    a_vs = gp("a_vs", (2, NS, LS, NH, HD)).transpose(1, 0, 2, 3, 4, 5).reshape(2, n * NS, LS, NH, HD)

    def one_p(name, last):
        return gp(name, (TP,) + last)[None]

    def one_s(name, last):
        return gp(name, (NS, LS) + last).reshape((1, n * NS, LS) + last)

    outs = (y_prompt, y_sample, a_kp, a_vp, a_ks, a_vs,
            one_p("b_kp", (NH, HD)), one_p("b_vp", (NH, HD)), one_p("b_fp", (NH,)),
            one_s("b_ks", (NH, HD)), one_s("b_vs", (NH, HD)), one_s("b_fs", (NH,)),
            one_p("c_kp", (NH, HD)), one_p("c_vp", (NH, HD)), one_s("c_ks", (NH, HD)), one_s("c_vs", (NH, HD)))
    return tuple(np.ascontiguousarray(o, dtype=np.float32) for o in outs)
```

```python
import numpy as np
import concourse.bass as bass
import concourse.mybir as mybir
from concourse.bass_utils import run_bass_kernel_spmd
from contextlib import ExitStack

F32 = mybir.dt.float32
BF16 = mybir.dt.bfloat16
U8 = mybir.dt.uint8
AF = mybir.ActivationFunctionType
ALU = mybir.AluOpType

D = 1024
NCH = 8
TP = 2048
NS = 4
LS = 16
TS = NS * LS
TT = TP + TS
DFF = 2816
NFC = 22
DEPTH = 4
NH = 16
HD = 64
EPS = 1e-6
TTILES = [(0, 512), (512, 512), (1024, 512), (1536, 512), (2048, 64)]
NEG = -30000.0

ENGS = ["pe", "act", "dve", "pool", "sp"]


class Buf:
    __slots__ = ("name", "w", "r", "excl")

    def __init__(self, name, excl=False):
        self.name = name
        self.w = None
        self.r = []
        self.excl = excl


class Sched:
    def __init__(self, nc, stack):
        self.nc = nc
        self.stack = stack
        self.q = {e: [] for e in ENGS}
        self.cnt = {}
        self.semh = {}
        self.epoch = 0
        self.known = {e: {} for e in ENGS}
        self.nsem = 0

    def _key_init(self, key):
        if key not in self.cnt:
            self.cnt[key] = 0
            self.semh[key] = self.stack.enter_context(self.nc.semaphore("s%d" % self.nsem))
            self.nsem += 1

    def pkey(self, eng):
        key = ("p", eng, self.epoch)
        self._key_init(key)
        return key

    def _deps(self, eng, reads, writes, extra):
        deps = {}

        def add(tok, same_ok):
            if tok is None:
                return
            key, val = tok
            if key[0] == "p" and key[1] == eng and not same_ok:
                return
            if deps.get(key, 0) < val:
                deps[key] = val

        same = eng != "pe"
        for b in reads:
            add(b.w, same)
            if b.excl:
                for t in b.r:
                    add(t, False)
        for b in writes:
            add(b.w, same)
            for t in b.r:
                add(t, same)
        for t in extra:
            add(t, True)
        out = []
        kn = self.known[eng]
        for key, val in deps.items():
            if kn.get(key, 0) >= val:
                continue
            kn[key] = val
            out.append((key, val))
        return out

    def _post(self, tok, reads, writes):
        for b in writes:
            b.w = tok
            b.r = []
        for b in reads:
            b.r.append(tok)

    def op(self, eng, fn, reads=(), writes=(), extra=()):
        waits = self._deps(eng, reads, writes, extra)
        key = self.pkey(eng)
        self.cnt[key] += 1
        tok = (key, self.cnt[key])
        self.q[eng].append((fn, waits, (key, 1)))
        self._post(tok, reads, writes)
        return tok

    def pe_group(self, fns, reads=(), writes=(), extra=()):
        waits = self._deps("pe", reads, writes, extra)
        key = self.pkey("pe")
        self.cnt[key] += 1
        tok = (key, self.cnt[key])
        n = len(fns)
        for i, fn in enumerate(fns):
            self.q["pe"].append((fn, waits if i == 0 else [], (key, 1) if i == n - 1 else None))
        self._post(tok, reads, writes)
        return tok

    def dma(self, queue, semname, out, in_, reads=(), writes=(), extra=()):
        waits = self._deps(queue, reads, writes, extra)
        key = ("d", semname)
        self._key_init(key)
        self.cnt[key] += 16
        tok = (key, self.cnt[key])

        def fn(eng, out=out, in_=in_):
            return eng.dma_start(out=out, in_=in_)

        self.q[queue].append((fn, waits, (key, 16)))
        self._post(tok, reads, writes)
        return tok

    def barrier(self):
        toks = [(k, v) for k, v in self.cnt.items() if v > 0]
        for e in ENGS:
            waits = []
            kn = self.known[e]
            for key, val in toks:
                if key[0] == "p" and key[1] == e:
                    continue
                if kn.get(key, 0) >= val:
                    continue
                kn[key] = val
                waits.append((key, val))
            if waits:
                self.q[e].append((None, waits, None))

    def new_epoch(self):
        self.epoch += 1

    def final_wait(self, eng="sp"):
        waits = [(k, v) for k, v in self.cnt.items() if v > 0 and k[0] == "d"]
        self.q[eng].append((None, waits, None))

    def replay(self):
        nc = self.nc
        semh = self.semh

        def run(eng, lst):
            for fn, waits, inc in lst:
                for key, val in waits:
                    eng.wait_ge(semh[key], val)
                if fn is not None:
                    ins = fn(eng)
                    if inc is not None:
                        ins.then_inc(semh[inc[0]], inc[1])

        with nc.Block() as block:
            @block.tensor
            def _(e):
                run(e, self.q["pe"])

            @block.scalar
            def _(e):
                run(e, self.q["act"])

            @block.vector
            def _(e):
                run(e, self.q["dve"])

            @block.gpsimd
            def _(e):
                run(e, self.q["pool"])

            @block.sync
            def _(e):
                run(e, self.q["sp"])


def _consts_np():
    c = np.zeros((128, 9, 128), np.float32)
    k = np.arange(128)[:, None]
    q = np.arange(128)[None, :]
    c[:, 0, :] = np.eye(128, dtype=np.float32)
    c[:, 1, :] = 1.0
    c[:, 2, :] = np.where(k > q, NEG, 0.0)
    c[:, 3, :] = np.where(k >= q, NEG, 0.0)
    c[:, 4, :] = np.where(k >= q, -1.0, 0.0)
    c[:, 5, :] = -1.0
    c[:, 6, :] = np.where(k + q == 127, 1.0, 0.0)
    c[:16, 7, :16] = np.where(k[:16] + q[:, :16] == 15, 1.0, 0.0)
    c[:, 8, :] = np.where(np.abs(k - q) == 64, 1.0, 0.0)
    return np.eye(128, dtype=np.float32), c.reshape(128, 1152)


def build(stop_after=None):
    nc = bass.Bass("TRN2", target_bir_lowering=False)
    st = ExitStack()

    def din(name, shape):
        return nc.dram_tensor(name, list(shape), F32, kind="ExternalInput").ap()

    def dout(name, shape):
        return nc.dram_tensor(name, list(shape), F32, kind="ExternalOutput").ap()

    x_p = din("x_p", (TP, D))
    x_s = din("x_s", (TS, D))
    consts_f = din("consts_f", (128, 128))
    consts_b = din("consts_b", (128, 1152))
    norm_ffn1 = din("norm_ffn1", (DEPTH, D))
    norm_mix = din("norm_mix", (DEPTH, D))
    norm_ffn2 = din("norm_ffn2", (DEPTH, D))
    norm_final = din("norm_final", (1, D))
    ffn_w = {}
    if "skip_ffn" not in DBG:
        for which in (1, 2):
            ffn_w[which] = (din("ffn%d_gate" % which, (DEPTH, D, DFF)), din("ffn%d_up" % which, (DEPTH, D, DFF)),
                            din("ffn%d_down" % which, (DEPTH, DFF, D)))
    ca_k = din("ca_k", (2, NS, 512, D))
    ca_v = din("ca_v", (2, NS, 512, D))
    cb_k = din("cb_k", (NS, 1024, D))
    cb_v = din("cb_v", (NS, 1024, D))
    cb_f = din("cb_f", (NS, 1024, NH))
    cc_k = din("cc_k", (NS, 1024, D))
    cc_v = din("cc_v", (NS, 1024, D))
    a_qkv = din("a_w_qkv", (2, D, 3 * D))
    a_o = din("a_w_o", (2, D, D))
    a_rel = din("a_rel_bias", (2, 513, NH))
    b_qkv = din("b_w_qkv", (1, D, 3 * D))
    b_o = din("b_w_o", (1, D, D))
    b_wf = din("b_w_f", (1, D, NH))
    b_bf = din("b_b_f", (1, NH))
    c_qkv = din("c_w_qkv", (1, D, 3 * D))
    c_o = din("c_w_o", (1, D, D))
    y_p = dout("y_p", (TP, D))
    y_s = dout("y_s", (TS, D))
    o_a_kp = dout("a_kp", (2, 512, D))
    o_a_vp = dout("a_vp", (2, 512, D))
    o_a_ks = dout("a_ks", (2, TS, D))
    o_a_vs = dout("a_vs", (2, TS, D))
    o_b_kp = dout("b_kp", (TP, D))
    o_b_vp = dout("b_vp", (TP, D))
    o_b_fp = dout("b_fp", (TP, NH))
    o_b_ks = dout("b_ks", (TS, D))
    o_b_vs = dout("b_vs", (TS, D))
    o_b_fs = dout("b_fs", (TS, NH))
    o_c_kp = dout("c_kp", (TP, D))
    o_c_vp = dout("c_vp", (TP, D))
    o_c_ks = dout("c_ks", (TS, D))
    o_c_vs = dout("c_vs", (TS, D))
    EH = nc.dram_tensor("eh_scratch", [NH * 768], F32)

    S = Sched(nc, st)

    X = st.enter_context(nc.sbuf_tensor("X", [128, NCH, TT], F32))
    HU = st.enter_context(nc.sbuf_tensor("HU", [128, 2 * NCH * TT * 2], U8))
    Wr = st.enter_context(nc.sbuf_tensor("Wr", [128, 34816], U8))
    SQr = st.enter_context(nc.sbuf_tensor("SQr", [128, 8192], U8))
    RSr = st.enter_context(nc.sbuf_tensor("RSr", [128, 4096], U8))
    IDF = st.enter_context(nc.sbuf_tensor("IDF", [128, 128], F32))
    CB = st.enter_context(nc.sbuf_tensor("CB", [128, 9, 128], BF16))
    GAIN = st.enter_context(nc.sbuf_tensor("GAIN", [128, 13 * NCH], F32))
    QS = st.enter_context(nc.sbuf_tensor("QS", [128, NS, 2, LS], BF16))
    ESZ = 26752
    Er = st.enter_context(nc.sbuf_tensor("Er", [128, ESZ], U8))

    def view(raw, a, b, dt, pat=None, **kw):
        v = raw[:, a:b].bitcast(dt)
        if pat is not None:
            v = v.rearrange(pat, **kw)
        return v

    HB = NCH * TT * 2
    H = view(HU, 0, HB, BF16, "p (c t) -> p c t", c=NCH)
    ACT = view(HU, HB, 2 * HB, BF16, "p (c t) -> p c t", c=NCH)
    OT = ACT
    YT = view(HU, 0, 2 * HB, F32, "p (c t) -> p c t", c=NCH)
    SQ = view(SQr, 0, 8192, BF16, "p (c t) -> p c t", c=NCH)
    GST = SQr[0:104, 0:512].bitcast(F32)
    RS = view(RSr, 0, 4096, F32, "p (s t) -> p s t", s=2)
    WGU = view(Wr, 0, 16384, BF16, "p (s g c f) -> p s g c f", s=4, g=2, c=NCH)
    WD = view(Wr, 16384, 32768, BF16, "p (s f) -> p s f", s=8)
    SG = view(Wr, 32768, 34816, BF16, "p (s f) -> p s f", s=2)
    XIN = view(Wr, 0, 16384, F32, "p (s f) -> p s f", s=4)
    WQKV = view(Wr, 0, 12288, BF16, "p (s m c f) -> p s m c f", s=2, m=3, c=NCH)
    PT = view(Er, 0, 4096, BF16, "p (s f) -> p s f", s=4)
    KST = view(Er, 4096, 6144, F32, "p (s j f) -> p s j f", s=2, j=2)
    VST = view(Er, 6144, 8192, F32, "p (s j f) -> p s j f", s=2, j=2)
    KTC = view(Er, 8192, 12288, BF16, "p (s f) -> p s f", s=2)
    MV = {}

    def set_views(kind):
        MV.clear()
        MV["WO"] = view(Wr, 12288, 16384, BF16, "p (s c f) -> p s c f", s=2, c=NCH)
        MV["QT0"] = view(Wr, 16384, 20608, BF16)
        MV["KT"] = view(Wr, 20608, 24832, BF16)
        if kind != 2:
            vw = 192
            MV["VB"] = view(Wr, 24832, 31360, BF16, "p (t f) -> p t f", t=17)
            MV["BS"] = view(Wr, 31360, 31680, BF16, "p (j h i) -> p j h i", j=5, h=2)
            MV["NBP"] = view(Wr, 31360, 33920, F32, "p (t h) -> p t h", h=NH)
            MV["VCC"] = view(Er, 12288, 18432, BF16, "p (s j f) -> p s j f", s=2, j=8)
            MV["QT1"] = view(Er, 18432, 22656, BF16)
            MV["BIAS"] = view(Er, 22656, 25216, BF16, "p (h f) -> p h f", h=2)
            MV["NBS"] = view(Er, 22656, 24960, F32, "p (s j h) -> p s j h", s=NS, j=9)
            MV["VS"] = view(Er, 25216, 26752, BF16, "p (s f) -> p s f", s=NS)
        else:
            vw = 128
            MV["VB"] = view(Wr, 24832, 29184, BF16, "p (t f) -> p t f", t=17)
            MV["VS"] = view(Wr, 29184, 30208, BF16, "p (s f) -> p s f", s=NS)
            MV["VCC"] = view(Er, 12288, 16384, BF16, "p (s j f) -> p s j f", s=2, j=8)
            MV["QT1"] = view(Er, 16384, 20608, BF16)
        MV["vw"] = vw
        MV["v1"] = vw - 64

    LB = view(Er, 20608, 22656, BF16, "p (s f) -> p s f", s=2)
    ACC = view(Er, 22656, 26752, BF16, "p (s f) -> p s f", s=4)
    EF = view(SQr, 0, 4096, F32, "p (s f) -> p s f", s=2)
    KCS = view(SQr, 4096, 8192, F32, "p (j f) -> p j f", j=8)
    RCP = RS
    def tview(a, b, dt, pat=None, **kw):
        return view(HU, HB + a, HB + b, dt, pat, **kw)

    ident_f = IDF[:, :]
    ident_b = CB[:, 0, :]
    ones_b = CB[:, 1, :]
    mask_le = CB[:, 2, :]
    mask_lt = CB[:, 3, :]
    ntri_b = CB[:, 4, :]
    nones_b = CB[:, 5, :]
    anti_b = CB[:, 6, :]
    anti16_b = CB[:, 7, :]
    swap_b = CB[:, 8, :]

    banks = [st.enter_context(nc.psum_tensor("pb%d" % i, [128, 512], F32)) for i in range(8)]
    bank_buf = [Buf("bank%d" % i, excl=True) for i in range(8)]
    ring = {"s": [0, 1, 2, 3], "a": [4, 5, 6, 7]}
    ring_pos = {"s": 0, "a": 0}

    def bank(pool):
        i = ring[pool][ring_pos[pool] % len(ring[pool])]
        ring_pos[pool] += 1
        return banks[i], bank_buf[i]

    NTT = len(TTILES)
    x_buf = [[Buf("x%d_%d" % (t, c)) for c in range(NCH)] for t in range(NTT)]
    h_buf = [Buf("h%d" % t) for t in range(NTT)]
    act_buf = [[Buf("a%d_%d" % (i, t)) for t in range(NTT)] for i in range(8)]
    ot_buf = [[Buf("ot%d_%d" % (c, t)) for t in range(NTT)] for c in range(NCH)]
    wgu_buf = [Buf("wgu%d" % i) for i in range(4)]
    wd_buf = [Buf("wd%d" % i) for i in range(8)]
    sg_buf = [Buf("sg%d" % i) for i in range(2)]
    xin_buf = [Buf("xin%d" % i) for i in range(4)]
    sq_buf = Buf("sq")
    rs_buf = [Buf("rs0"), Buf("rs1")]
    idf_buf = Buf("idf")
    cb_buf = Buf("cb")
    gain_buf = Buf("gain")
    gst_buf = Buf("gst")
    wqkv_buf = [Buf("wqkv%d" % i) for i in range(2)]
    wo_buf = [Buf("wo%d" % i) for i in range(2)]
    qt_buf = Buf("qt")
    qs_buf = Buf("qs")
    kt_buf = Buf("kt")
    vb_buf = Buf("vb")
    vs_buf = Buf("vs")
    pt_buf = [Buf("pt%d" % i) for i in range(4)]
    kst_buf = [Buf("kst0"), Buf("kst1")]
    vst_buf = [Buf("vst0"), Buf("vst1")]
    ktc_buf = [Buf("ktc%d" % i) for i in range(2)]
    vcc_buf = [Buf("vcc%d" % i) for i in range(2)]
    kcs_buf = Buf("kcs")
    bias_buf = Buf("bias")
    bs_buf = Buf("bs")
    nb_buf = Buf("nb")
    rcp_buf = [Buf("rcp0"), Buf("rcp1")]
    ef_buf = [Buf("ef0"), Buf("ef1")]
    lb_buf = [Buf("lb0"), Buf("lb1")]
    acc_buf = [Buf("acc%d" % i) for i in range(4)]
    tr_buf = Buf("transient")
    cnt = {"wgu": 0, "sg": 0, "xin": 0, "wqkv": 0, "wo": 0, "pt": 0, "kst": 0, "vst": 0, "kvc": 0, "rcp": 0,
           "ef": 0, "lb": 0, "accs": 0}

    S.dma("sp", "const", IDF[:, :], consts_f, writes=[idf_buf])
    S.dma("pool", "constb", CB[:, :, :].rearrange("p a b -> p (a b)"), consts_b, writes=[cb_buf])
    for i, g in enumerate([norm_ffn1, norm_mix, norm_ffn2]):
        S.dma("sp", "const", GST[i * 32:(i + 1) * 32, :], g.rearrange("l (c p) -> (l c) p", p=128), writes=[gst_buf])
    S.dma("sp", "const", GST[96:104, :], norm_final.rearrange("l (c p) -> (l c) p", p=128), writes=[gst_buf])
    pb, pbb = bank("s")
    S.pe_group([lambda e, pb=pb: e.transpose(out=pb[:, 0:104], in_=GST[:, :], identity=ident_f[0:104, 0:104])],
               reads=[gst_buf, idf_buf], writes=[pbb])
    S.op("dve", lambda e, pb=pb: e.tensor_copy(out=GAIN[:, :], in_=pb[:, 0:104]), reads=[pbb], writes=[gain_buf])

    def gain_col(kind, l, c):
        j = {0: 0, 1: 32, 2: 64, 3: 96}[kind] + (l * 8 if kind < 3 else 0) + c
        return GAIN[:, j:j + 1]

    def load_x():
        ntile = TP // 128 + 1
        for t in range(ntile):
            rows = 128 if t < TP // 128 else TS
            src = x_p[t * 128:(t + 1) * 128, :] if t < TP // 128 else x_s[:, :]
            sl = cnt["xin"] % 4
            cnt["xin"] += 1
            S.dma("sp", "xin%d" % sl, XIN[0:rows, sl, :], src, writes=[xin_buf[sl]])
            tt = min(t // 4, 4)
            for half in range(2):
                pb, pbb = bank("s")
                fns = []
                for j in range(4):
                    c = half * 4 + j
                    fns.append(lambda e, pb=pb, j=j, c=c, sl=sl, rows=rows: e.transpose(
                        out=pb[:, j * 128:j * 128 + rows], in_=XIN[0:rows, sl, c * 128:(c + 1) * 128],
                        identity=ident_f[0:rows, 0:rows]))
                S.pe_group(fns, reads=[xin_buf[sl], idf_buf], writes=[pbb])
                dst = X[:, half * 4:half * 4 + 4, t * 128:t * 128 + rows]
                srcp = pb[:, :].rearrange("p (j k) -> p j k", j=4)[:, :, 0:rows]
                wl = [x_buf[tt][half * 4 + j] for j in range(4)]
                if half == 0:
                    S.op("act", lambda e, dst=dst, srcp=srcp: e.copy(out=dst, in_=srcp), reads=[pbb], writes=wl)
                else:
                    S.op("dve", lambda e, dst=dst, srcp=srcp: e.tensor_copy(out=dst, in_=srcp), reads=[pbb], writes=wl)

    def norm(kind, l, dst, dst_bufs):
        for tt, (t0, n) in enumerate(TTILES):
            S.op("act", lambda e, t0=t0, n=n: e.activation(out=SQ[:, :, 0:n], in_=X[:, :, t0:t0 + n], func=AF.Square),
                 reads=x_buf[tt], writes=[sq_buf])
            pb, pbb = bank("s")
            fns = [lambda e, pb=pb, c=c, n=n: e.matmul(pb[:, 0:n], lhsT=ones_b, rhs=SQ[:, c, 0:n],
                                                       start=(c == 0), stop=(c == NCH - 1)) for c in range(NCH)]
            S.pe_group(fns, reads=[sq_buf, cb_buf], writes=[pbb])
            S.op("act", lambda e, pb=pb, n=n: e.activation(out=RS[:, 0, 0:n], in_=pb[:, 0:n], func=AF.Ln,
                                                           bias=EPS, scale=1.0 / D),
                 reads=[pbb], writes=[rs_buf[0]])
            S.op("act", lambda e, n=n: e.activation(out=RS[:, 1, 0:n], in_=RS[:, 0, 0:n], func=AF.Exp, scale=-0.5),
                 reads=[rs_buf[0]], writes=[rs_buf[1]])
            for c in range(NCH):
                S.op("dve", lambda e, c=c, t0=t0, n=n: e.scalar_tensor_tensor(
                    out=dst[:, c, t0:t0 + n], in0=X[:, c, t0:t0 + n], scalar=gain_col(kind, l, c),
                    in1=RS[:, 1, 0:n], op0=ALU.mult, op1=ALU.mult),
                    reads=[x_buf[tt][c], rs_buf[1], gain_buf], writes=[dst_bufs[tt]])

    FGROUPS = [list(range(0, 8)), list(range(8, 15)), list(range(15, 22))]

    def ffn(l, which):
        if "skip_ffn" in DBG:
            return
        wg_d, wu_d, wd_d = ffn_w[which]
        norm(0 if which == 1 else 2, l, H, h_buf)
        t3, n3 = TTILES[3]
        t4, n4 = TTILES[4]
        for grp in FGROUPS:
            for i, fc in enumerate(grp):
                sl = cnt["wgu"] % 4
                cnt["wgu"] += 1
                S.dma("pool", "wg%d" % sl, WGU[:, sl, 0, :, :],
                      wg_d[l].rearrange("(c p) f -> p c f", p=128)[:, :, fc * 128:(fc + 1) * 128], writes=[wgu_buf[sl]])
                S.dma("pool", "wu%d" % sl, WGU[:, sl, 1, :, :],
                      wu_d[l].rearrange("(c p) f -> p c f", p=128)[:, :, fc * 128:(fc + 1) * 128], writes=[wgu_buf[sl]])
                S.dma("pool", "wd%d" % i, WD[:, i, :], wd_d[l][fc * 128:(fc + 1) * 128, :], writes=[wd_buf[i]])

                def evac(pg, pgb, pu, pub, tt, t0, n, i=i):
                    ss = cnt["sg"] % 2
                    cnt["sg"] += 1
                    S.op("act", lambda e, pg=pg, ss=ss, n=n: e.activation(out=SG[:, ss, 0:n], in_=pg[:, 0:n], func=AF.Silu),
                         reads=[pgb], writes=[sg_buf[ss]])
                    S.op("dve", lambda e, pu=pu, ss=ss, i=i, t0=t0, n=n: e.tensor_tensor(
                        out=ACT[:, i, t0:t0 + n], in0=pu[:, 0:n], in1=SG[:, ss, 0:n], op=ALU.mult),
                        reads=[pub, sg_buf[ss]], writes=[act_buf[i][tt]])

                for tt, (t0, n) in enumerate(TTILES[0:3]):
                    pg, pgb = bank("s")
                    pu, pub = bank("s")
                    for gi, (pp, ppb) in enumerate(((pg, pgb), (pu, pub))):
                        fns = [lambda e, pp=pp, gi=gi, c=c, sl=sl, t0=t0, n=n: e.matmul(
                            pp[:, 0:n], lhsT=WGU[:, sl, gi, c, :], rhs=H[:, c, t0:t0 + n],
                            start=(c == 0), stop=(c == NCH - 1)) for c in range(NCH)]
                        S.pe_group(fns, reads=[wgu_buf[sl], h_buf[tt]], writes=[ppb])
                    evac(pg, pgb, pu, pub, tt, t0, n)
                pg, pgb = bank("s")
                pu, pub = bank("s")
                qg, qgb = bank("a")
                qu, qub = bank("a")
                for gi, (pp, ppb, qq, qqb) in enumerate(((pg, pgb, qg, qgb), (pu, pub, qu, qub))):
                    fns = []
                    for c in range(NCH):
                        fns.append(lambda e, pp=pp, gi=gi, c=c, sl=sl: e.matmul(
                            pp[:, 0:n3], lhsT=WGU[:, sl, gi, c, :], rhs=H[:, c, t3:t3 + n3],
                            start=(c == 0), stop=(c == NCH - 1)))
                        fns.append(lambda e, qq=qq, gi=gi, c=c, sl=sl: e.matmul(
                            qq[:, 0:n4], lhsT=WGU[:, sl, gi, c, :], rhs=H[:, c, t4:t4 + n4],
                            start=(c == 0), stop=(c == NCH - 1)))
                    S.pe_group(fns, reads=[wgu_buf[sl], h_buf[3], h_buf[4]], writes=[ppb, qqb])
                evac(pg, pgb, pu, pub, 3, t3, n3)
                evac(qg, qgb, qu, qub, 4, t4, n4)
            ng = len(grp)

            def xupd(po, pob, dp, tt, t0, n):
                S.op("dve", lambda e, po=po, dp=dp, t0=t0, n=n: e.scalar_tensor_tensor(
                    out=X[:, dp, t0:t0 + n], in0=po[:, 0:n], scalar=0.5, in1=X[:, dp, t0:t0 + n],
                    op0=ALU.mult, op1=ALU.add),
                    reads=[pob, x_buf[tt][dp]], writes=[x_buf[tt][dp]])

            for tt, (t0, n) in enumerate(TTILES[0:3]):
                for dp in range(NCH):
                    po, pob = bank("a")
                    fns = [lambda e, po=po, i=i, dp=dp, t0=t0, n=n, ng=ng: e.matmul(
                        po[:, 0:n], lhsT=WD[:, i, dp * 128:(dp + 1) * 128], rhs=ACT[:, i, t0:t0 + n],
                        start=(i == 0), stop=(i == ng - 1)) for i in range(ng)]
                    S.pe_group(fns, reads=[wd_buf[i] for i in range(ng)] + [act_buf[i][tt] for i in range(ng)],
                               writes=[pob])
                    xupd(po, pob, dp, tt, t0, n)
            for dp in range(NCH):
                po, pob = bank("a")
                qo, qob = bank("a")
                fns = []
                for i in range(ng):
                    fns.append(lambda e, po=po, i=i, dp=dp, ng=ng: e.matmul(
                        po[:, 0:n3], lhsT=WD[:, i, dp * 128:(dp + 1) * 128], rhs=ACT[:, i, t3:t3 + n3],
                        start=(i == 0), stop=(i == ng - 1)))
                    fns.append(lambda e, qo=qo, i=i, dp=dp, ng=ng: e.matmul(
                        qo[:, 0:n4], lhsT=WD[:, i, dp * 128:(dp + 1) * 128], rhs=ACT[:, i, t4:t4 + n4],
                        start=(i == 0), stop=(i == ng - 1)))
                S.pe_group(fns, reads=[wd_buf[i] for i in range(ng)] + [act_buf[i][3] for i in range(ng)]
                           + [act_buf[i][4] for i in range(ng)], writes=[pob, qob])
                xupd(po, pob, dp, 3, t3, n3)
                xupd(qo, qob, dp, 4, t4, n4)

    def mm(out, lhsT, rhs, start, stop):
        return lambda e: e.matmul(out, lhsT=lhsT, rhs=rhs, start=start, stop=stop, skip_group_check=True)

    def project_pair(kind, slot, c, wqkv_d, outs):
        k_out_p, v_out_p, k_out_s, v_out_s = outs
        QT0, QT1, KT, VB, VS, v1 = MV["QT0"], MV["QT1"], MV["KT"], MV["VB"], MV["VS"], MV["v1"]
        sl = c % 2
        def evac_qk(m, tt, pb, pbb, t0, n):
            if m == 0 and tt == 4:
                for hh in range(2):
                    S.op("act", lambda e, pb=pb, hh=hh: e.activation(
                        out=QS[hh * 64:(hh + 1) * 64, :, hh, :],
                        in_=pb[hh * 64:(hh + 1) * 64, 0:TS].rearrange("p (s i) -> p s i", s=NS),
                        func=AF.Copy, scale=0.125), reads=[pbb], writes=[qs_buf])
            elif m == 0:
                S.op("act", lambda e, pb=pb, t0=t0, n=n: e.activation(out=QT0[0:64, t0:t0 + n], in_=pb[0:64, 0:n],
                                                                      func=AF.Copy, scale=0.125),
                     reads=[pbb], writes=[qt_buf])
                S.op("act", lambda e, pb=pb, t0=t0, n=n: e.activation(out=QT1[64:128, t0:t0 + n],
                                                                      in_=pb[64:128, 0:n], func=AF.Copy, scale=0.125),
                     reads=[pbb], writes=[qt_buf])
            else:
                S.op("dve", lambda e, pb=pb, t0=t0, n=n: e.tensor_copy(out=KT[:, t0:t0 + n], in_=pb[:, 0:n]),
                     reads=[pbb], writes=[kt_buf])

        for tt, (t0, n) in enumerate(TTILES[0:3]):
            for m in range(2):
                pb, pbb = bank("s")
                fns = [mm(pb[:, 0:n], WQKV[:, sl, m, dc, :], H[:, dc, t0:t0 + n], dc == 0, dc == NCH - 1)
                       for dc in range(NCH)]
                S.pe_group(fns, reads=[wqkv_buf[sl], h_buf[tt]], writes=[pbb])
                evac_qk(m, tt, pb, pbb, t0, n)
        (t3, n3), (t4, n4) = TTILES[3], TTILES[4]
        for m in range(2):
            pb3, pbb3 = bank("s")
            pb4, pbb4 = bank("s")
            fns = []
            for dc in range(NCH):
                fns.append(mm(pb3[:, 0:n3], WQKV[:, sl, m, dc, :], H[:, dc, t3:t3 + n3], dc == 0, dc == NCH - 1))
                fns.append(mm(pb4[:, 0:n4], WQKV[:, sl, m, dc, :], H[:, dc, t4:t4 + n4], dc == 0, dc == NCH - 1))
            S.pe_group(fns, reads=[wqkv_buf[sl], h_buf[3], h_buf[4]], writes=[pbb3, pbb4])
            evac_qk(m, 3, pb3, pbb3, t3, n3)
            evac_qk(m, 4, pb4, pbb4, t4, n4)
        groups = [[0, 1, 2, 3], [4, 5, 6, 7], [8, 9, 10, 11], [12, 13, 14, 15], [16]]
        for gi, grp in enumerate(groups):
            for m in (2, 1):
                if m == 1 and kind == 0 and gi < 3:
                    continue
                pb, pbb = bank("s")
                fns = []
                for jj, t in enumerate(grp):
                    rows = 128 if t < 16 else TS
                    for dc in range(NCH):
                        fns.append(mm(pb[0:rows, jj * 128:(jj + 1) * 128], H[:, dc, t * 128:t * 128 + rows],
                                      WQKV[:, sl, m, dc, :], dc == 0, dc == NCH - 1))
                rows = 128 if gi < 4 else TS
                ng = len(grp)
                S.pe_group(fns, reads=[wqkv_buf[sl], h_buf[min(gi, 4)]], writes=[pbb])
                psv = pb[0:rows, 0:ng * 128].rearrange("p (j f) -> p j f", j=ng)
                need_out = not (kind == 0 and gi < 3)
                if m == 2:
                    for hh in range(2):
                        S.op("dve", lambda e, psv=psv, rows=rows, grp=grp, ng=ng, hh=hh: e.tensor_copy(
                            out=VB[0:rows, grp[0]:grp[0] + ng, hh * v1:hh * v1 + 64],
                            in_=psv[:, :, hh * 64:(hh + 1) * 64]), reads=[pbb], writes=[vb_buf])
                if need_out:
                    key = "vst" if m == 2 else "kst"
                    stg = VST if m == 2 else KST
                    sbufs = vst_buf if m == 2 else kst_buf
                    halves = [(0, 2), (2, 2)] if gi < 4 else [(0, 1)]
                    for hi_, (j0, nj) in enumerate(halves):
                        ss = cnt[key] % 2
                        cnt[key] += 1
                        eng_ = "act" if hi_ == 0 else "dve"
                        src_ = psv[:, j0:j0 + nj, :]
                        if eng_ == "act":
                            S.op("act", lambda e, src_=src_, rows=rows, nj=nj, stg=stg, ss=ss: e.copy(
                                out=stg[0:rows, ss, 0:nj, :], in_=src_), reads=[pbb], writes=[sbufs[ss]])
                        else:
                            S.op("dve", lambda e, src_=src_, rows=rows, nj=nj, stg=stg, ss=ss: e.tensor_copy(
                                out=stg[0:rows, ss, 0:nj, :], in_=src_), reads=[pbb], writes=[sbufs[ss]])
                        if gi < 4:
                            od = v_out_p if m == 2 else k_out_p
                            r0 = (gi * 512 if kind != 0 else 0) + j0 * 128
                            dst = od[r0:r0 + 256, c * 128:(c + 1) * 128].rearrange("(j p) f -> p j f", p=128)
                            S.dma("sp", "%s%d" % (key, ss), dst, stg[:, ss, :, :], reads=[sbufs[ss]])
                        else:
                            od = v_out_s if m == 2 else k_out_s
                            S.dma("sp", "%s%d" % (key, ss), od[:, c * 128:(c + 1) * 128], stg[0:TS, ss, 0, :],
                                  reads=[sbufs[ss]])
        pb, pbb = bank("s")
        fns = []
        for s in range(NS):
            for dc in range(NCH):
                fns.append(mm(pb[0:LS, s * 128:(s + 1) * 128], H[:, dc, TP + s * LS:TP + (s + 1) * LS],
                              WQKV[:, sl, 2, dc, :], dc == 0, dc == NCH - 1))
        S.pe_group(fns, reads=[wqkv_buf[sl], h_buf[4]], writes=[pbb])
        for hh in range(2):
            S.op("dve", lambda e, pb=pb, hh=hh: e.tensor_copy(
                out=VS[0:LS, :, hh * v1:hh * v1 + 64],
                in_=pb[0:LS, :].rearrange("p (s f) -> p s f", s=NS)[:, :, hh * 64:(hh + 1) * 64]),
                reads=[pbb], writes=[vs_buf])

    def load_wqkv(c, wqkv_d):
        sl = c % 2
        for m in range(3):
            S.dma("pool", "wqkv%d_%d" % (sl, m), WQKV[:, sl, m, :, :],
                  wqkv_d.rearrange("(c p) f -> p c f", p=128)[:, :, m * D + c * 128:m * D + (c + 1) * 128],
                  writes=[wqkv_buf[sl]])

    def issue_cache_dma(s, c, ck, cv, nt, sl):
        S.dma("sp", "kcs", KCS[:, 0:nt, :], ck[s][:, c * 128:(c + 1) * 128].rearrange("(j p) f -> p j f", p=128),
              writes=[kcs_buf])
        VCC, v1 = MV["VCC"], MV["v1"]
        for hh in range(2):
            S.dma("pool", "vcc%d_%d" % (sl, hh), VCC[:, sl, 0:nt, hh * v1:hh * v1 + 64],
                  cv[s][:, c * 128 + hh * 64:c * 128 + (hh + 1) * 64].rearrange("(j p) f -> p j f", p=128),
                  writes=[vcc_buf[sl]])

    def cache_transposes(nt, sl):
        for g in range(nt // 4):
            pb, pbb = bank("s")
            fns = [lambda e, pb=pb, j=j, g=g: e.transpose(out=pb[:, j * 128:(j + 1) * 128], in_=KCS[:, g * 4 + j, :],
                                                          identity=ident_f) for j in range(4)]
            S.pe_group(fns, reads=[kcs_buf, idf_buf], writes=[pbb])
            if g % 2 == 0:
                S.op("act", lambda e, pb=pb, g=g, sl=sl: e.copy(out=KTC[:, sl, g * 512:(g + 1) * 512], in_=pb[:, :]),
                     reads=[pbb], writes=[ktc_buf[sl]])
            else:
                S.op("dve", lambda e, pb=pb, g=g, sl=sl: e.tensor_copy(out=KTC[:, sl, g * 512:(g + 1) * 512], in_=pb[:, :]),
                     reads=[pbb], writes=[ktc_buf[sl]])

    def load_cache_pair(s, c, ck, cv, nt, sl):
        issue_cache_dma(s, c, ck, cv, nt, sl)
        cache_transposes(nt, sl)

    def run_softmax_units(units, LA=2):
        pend = []
        deferred = []

        def do_pv(item):
            u, slot = item
            nk, n = u["nk"], u["n"]
            c0 = u["ocol"]
            if "pv" in u:
                fns = [mm(u[key][0][:, c0:c0 + LS], vv, PT[0:nk, slot, hh * LS:(hh + 1) * LS], u["first"], u["last"])
                       for hh, (key, vv) in enumerate(u["pv"])]
                S.pe_group(fns, reads=[pt_buf[slot]] + u["v_reads"], writes=[u["o"][1], u["d"][1]])
            else:
                ob, obuf = u["o"] if u["hh"] == 0 else u["d"]
                fns = [mm(ob[:, c0:c0 + n], u["v"], PT[0:nk, slot, 0:n], u["first"], u["last"])]
                S.pe_group(fns, reads=[pt_buf[slot]] + u["v_reads"], writes=[obuf])
            if u.get("fin") is not None:
                fb = u["fin"]()
                if fb is not None:
                    deferred.append([2, fb])
            if u.get("post") is not None:
                u["post"]()

        for u in units:
            for d in deferred:
                d[0] -= 1
            while deferred and deferred[0][0] <= 0:
                deferred.pop(0)[1]()
            if u.get("pre") is not None:
                u["pre"]()
            nk, n = u["nk"], u["n"]
            sb, sbb = bank("s")
            nx = len(u["extras"])
            fns = [mm(sb[0:nk, 0:n], u["kT"], u["q"], True, nx == 0)]
            for xi, (xl, xr, xo, xn) in enumerate(u["extras"]):
                fns.append(mm(sb[0:nk, xo:xo + xn], xl, xr, False, xi == nx - 1))
            S.pe_group(fns, reads=u["qk_reads"] + [cb_buf], writes=[sbb])
            slot = cnt["pt"] % 4
            cnt["pt"] += 1
            bias = u.get("bias")
            if "bias2" in u:
                for hh, bb in enumerate(u["bias2"]):
                    S.op("act", lambda e, sb=sb, nk=nk, slot=slot, hh=hh, bb=bb: e.activation(
                        out=PT[0:nk, slot, hh * LS:(hh + 1) * LS], in_=sb[0:nk, hh * LS:(hh + 1) * LS], func=AF.Exp,
                        bias=bb), reads=[sbb, nb_buf], writes=[pt_buf[slot]])
            elif bias is None:
                S.op("act", lambda e, sb=sb, nk=nk, n=n, slot=slot: e.activation(
                    out=PT[0:nk, slot, 0:n], in_=sb[0:nk, 0:n], func=AF.Exp), reads=[sbb], writes=[pt_buf[slot]])
            else:
                S.op("act", lambda e, sb=sb, nk=nk, n=n, slot=slot, bias=bias: e.activation(
                    out=PT[0:nk, slot, 0:n], in_=sb[0:nk, 0:n], func=AF.Exp, bias=bias),
                    reads=[sbb, nb_buf], writes=[pt_buf[slot]])
            pend.append((u, slot))
            if len(pend) > LA:
                do_pv(pend.pop(0))
        while pend:
            do_pv(pend.pop(0))
        while deferred:
            deferred.pop(0)[1]()

    def softmax_fin(b0, b0buf, b1, b1buf, c, col0, n, tts):
        def fin():
            rs = cnt["rcp"] % 2
            cnt["rcp"] += 1
            slot = cnt["pt"] % 4
            cnt["pt"] += 1
            S.op("act", lambda e: e.copy(out=RCP[0:64, rs, 0:n], in_=b1[0:64, 0:n]), reads=[b1buf], writes=[rcp_buf[rs]])
            S.op("act", lambda e: e.copy(out=RCP[64:128, rs, 0:n], in_=b0[64:128, 0:n]), reads=[b0buf],
                 writes=[rcp_buf[rs]])
            S.op("dve", lambda e: e.reciprocal(out=RCP[:, rs, 0:n], in_=RCP[:, rs, 0:n]), reads=[rcp_buf[rs]],
                 writes=[rcp_buf[rs]])
            S.op("act", lambda e: e.copy(out=PT[0:64, slot, 0:n], in_=b0[0:64, 0:n]), reads=[b0buf], writes=[pt_buf[slot]])
            S.op("act", lambda e: e.copy(out=PT[64:128, slot, 0:n], in_=b1[64:128, 0:n]), reads=[b1buf],
                 writes=[pt_buf[slot]])

            def fin_b():
                xs, xsb = bank("s")
                S.pe_group([mm(xs[:, 0:n], swap_b, PT[:, slot, 0:n], True, True)], reads=[pt_buf[slot], cb_buf],
                           writes=[xsb])
                S.op("dve", lambda e: e.tensor_tensor(out=OT[:, c, col0:col0 + n], in0=xs[:, 0:n], in1=RCP[:, rs, 0:n],
                                                      op=ALU.mult),
                     reads=[xsb, rcp_buf[rs]], writes=[ot_buf[c][t] for t in tts])
            return fin_b
        return fin

    def run_stick_units(units):
        stA = []
        stB = []
        acc_state = {}

        def do_cs(item):
            u, sb, sbb, lslot, accprev = item
            nk, n, sc = u["nk"], u["n"], u["scol"]
            fns = [mm(sb[0:nk, sc:sc + n], ntri_b[0:nk, 0:nk], LB[0:nk, lslot, sc:sc + n], False, accprev is None)]
            rd = [lb_buf[lslot], cb_buf]
            if accprev is not None:
                aslot, ank = accprev
                fns.append(mm(sb[0:nk, sc:sc + n], nones_b[0:ank, 0:nk], ACC[0:ank, aslot, sc:sc + n], False, True))
                rd.append(acc_buf[aslot])
            S.pe_group(fns, reads=rd, writes=[sbb])
            slot = cnt["pt"] % 4
            cnt["pt"] += 1
            S.op("act", lambda e: e.activation(out=PT[0:nk, slot, sc:sc + n], in_=sb[0:nk, sc:sc + n], func=AF.Exp),
                 reads=[sbb], writes=[pt_buf[slot]])
            stB.append((u, slot))

        def do_pv(item):
            u, slot = item
            nk, n, sc = u["nk"], u["n"], u["scol"]
            ob, obuf = u["o"]
            pb0, c0 = u["pbase"], u["ocol"]
            if "pv" in u:
                fns = [mm(ob[hh * 64:(hh + 1) * 64, c0:c0 + LS], vv, PT[0:nk, slot, hh * LS:(hh + 1) * LS],
                          u["first"], u["last"]) for hh, (key, vv) in enumerate(u["pv"])]
            else:
                fns = [mm(ob[pb0:pb0 + 64, c0:c0 + n], u["v"], PT[0:nk, slot, sc:sc + n], u["first"], u["last"])]
            S.pe_group(fns, reads=[pt_buf[slot]] + u["v_reads"], writes=[obuf])
            if u.get("fin") is not None:
                u["fin"]()
            if u.get("post") is not None:
                u["post"]()

        for u in units:
            if u.get("pre") is not None:
                u["pre"]()
            nk, n, sc = u["nk"], u["n"], u["scol"]
            sb, sbb = bank("s")
            fns = [mm(sb[0:nk, sc:sc + n], u["kT"], u["q"], True, False)]
            for xi, (xl, xr, xo, xn) in enumerate(u["extras"]):
                fns.append(mm(sb[0:nk, sc + xo:sc + xo + xn], xl, xr, False, False))
            S.pe_group(fns, reads=u["qk_reads"] + [cb_buf], writes=[sbb])
            es = cnt["ef"] % 2
            cnt["ef"] += 1
            S.op("act", lambda e, sb=sb, nk=nk, n=n, sc=sc, es=es: e.activation(
                out=EF[0:nk, es, sc:sc + n], in_=sb[0:nk, sc:sc + n], func=AF.Exp), reads=[sbb], writes=[ef_buf[es]])
            ls = cnt["lb"] % 2
            cnt["lb"] += 1
            S.op("act", lambda e, nk=nk, n=n, sc=sc, es=es, ls=ls: e.activation(
                out=LB[0:nk, ls, sc:sc + n], in_=EF[0:nk, es, sc:sc + n], func=AF.Ln, bias=1.0),
                reads=[ef_buf[es]], writes=[lb_buf[ls]])
            sid = u["seq"]
            prev = None if u["seq_first"] else acc_state[sid]
            stA.append((u, sb, sbb, ls, prev))
            if not u["seq_last"]:
                base = 2 * u["hh"]
                W = sc + n
                rot = None
                if "pv" in u:
                    rot = cnt["accs"] % 4
                    cnt["accs"] += 1
                if prev is None:
                    ns_ = base if rot is None else rot
                    S.op("dve", lambda e, nk=nk, n=n, sc=sc, ls=ls, ns_=ns_: e.tensor_copy(
                        out=ACC[0:nk, ns_, sc:sc + n], in_=LB[0:nk, ls, sc:sc + n]),
                        reads=[lb_buf[ls]], writes=[acc_buf[ns_]])
                    acc_state[sid] = (ns_, nk)
                else:
                    pslot, pnk = prev
                    ns_ = base + (1 - (pslot - base)) if rot is None else rot
                    if pnk < nk:
                        S.op("dve", lambda e, nk=nk, n=n, sc=sc, ls=ls, ns_=ns_: e.tensor_copy(
                            out=ACC[0:nk, ns_, sc:sc + n], in_=LB[0:nk, ls, sc:sc + n]),
                            reads=[lb_buf[ls]], writes=[acc_buf[ns_]])
                        S.op("dve", lambda e, pnk=pnk, n=n, sc=sc, ns_=ns_, pslot=pslot: e.tensor_tensor(
                            out=ACC[0:pnk, ns_, sc:sc + n], in0=ACC[0:pnk, ns_, sc:sc + n],
                            in1=ACC[0:pnk, pslot, sc:sc + n], op=ALU.add),
                            reads=[acc_buf[ns_], acc_buf[pslot]], writes=[acc_buf[ns_]])
                    else:
                        S.op("dve", lambda e, nk=nk, n=n, sc=sc, ls=ls, ns_=ns_, pslot=pslot: e.tensor_tensor(
                            out=ACC[0:nk, ns_, sc:sc + n], in0=ACC[0:nk, pslot, sc:sc + n], in1=LB[0:nk, ls, sc:sc + n],
                            op=ALU.add),
                            reads=[acc_buf[pslot], lb_buf[ls]], writes=[acc_buf[ns_]])
                    acc_state[sid] = (ns_, max(nk, pnk))
                if sc > 0:
                    S.op("dve", lambda e, sc=sc, ns_=ns_: e.memset(ACC[:, ns_, 0:sc], 0.0), writes=[acc_buf[ns_]])
            if len(stA) > 1:
                do_cs(stA.pop(0))
            if len(stB) > 1:
                do_pv(stB.pop(0))
        while stA:
            do_cs(stA.pop(0))
        while stB:
            do_pv(stB.pop(0))

    def copy_fin(ob, obuf, c, col0, n, tts):
        def fin():
            S.op("dve", lambda e: e.tensor_copy(out=OT[:, c, col0:col0 + n], in_=ob[:, 0:n]),
                 reads=[obuf], writes=[ot_buf[c][t] for t in tts])
        return fin

    def prep_A(slot):
        RT = tview(0, 320, F32, "p (j h) -> p j h", j=5)
        TBT = tview(512, 512 + 513 * 4, F32)
        EE = tview(4096, 4096 + 3072, F32)
        S.dma("sp", "prep", RT[:, 0:4, :], a_rel[slot][0:512, :].rearrange("(j p) h -> p j h", p=128), writes=[tr_buf])
        S.dma("sp", "prep", RT[0:1, 4, :], a_rel[slot][512:513, :], writes=[tr_buf])
        pb, pbb = bank("s")
        pb2, pbb2 = bank("s")
        fns = [lambda e, j=j: e.transpose(out=pb[0:16, j * 128:(j + 1) * 128], in_=RT[:, j, :], identity=ident_f)
               for j in range(4)]
        fns.append(lambda e: e.transpose(out=pb2[0:16, 0:1], in_=RT[0:1, 4, :], identity=ident_f[0:1, 0:1]))
        S.pe_group(fns, reads=[tr_buf, idf_buf], writes=[pbb, pbb2])
        S.op("dve", lambda e: e.tensor_copy(out=TBT[0:16, 0:512], in_=pb[0:16, :]), reads=[pbb], writes=[tr_buf])
        S.op("dve", lambda e: e.tensor_copy(out=TBT[0:16, 512:513], in_=pb2[0:16, 0:1]), reads=[pbb2], writes=[tr_buf])
        S.op("dve", lambda e: e.tensor_copy(out=EE[0:16, 0:384], in_=TBT[0:16, 129:513]), reads=[tr_buf], writes=[tr_buf])
        S.op("dve", lambda e: e.tensor_copy(out=EE[0:16, 384:768], in_=TBT[0:16, 512:513].broadcast_to([16, 384])),
             reads=[tr_buf], writes=[tr_buf])
        S.dma("sp", "prep", bass.AP(EH, 0, [[768, 16], [1, 768]]), EE[0:16, :], reads=[tr_buf], writes=[tr_buf])

    def load_bias_A(c):
        BIAS, BS = MV["BIAS"], MV["BS"]
        for hh in range(2):
            h = 2 * c + hh
            S.dma("pool", "biasA", BIAS[:, hh, :], bass.AP(EH, h * 768, [[1, 128], [1, 640]]),
                  reads=[tr_buf], writes=[bias_buf])
        S.op("dve", lambda e: e.memset(BIAS[64:128, :, 576:640], NEG), writes=[bias_buf])
        S.op("dve", lambda e: e.memset(BIAS[0:64, :, 0:64], NEG), writes=[bias_buf])
        for hh in range(2):
            h = 2 * c + hh
            for j in range(5):
                rows = 128 if j < 4 else LS
                S.dma("pool", "biasS", BS[0:rows, j, hh, :],
                      bass.AP(EH, h * 768 + (512 - 128 * j if j < 4 else 112), [[1, rows], [1, LS]]),
                      reads=[tr_buf], writes=[bs_buf])

    def prep_B(slot):
        NBP, NBS = MV["NBP"], MV["NBS"]
        WF = tview(0, 256, BF16, "p (c h) -> p c h", c=NCH)
        BF_ = tview(256, 320, F32)
        ZF = tview(512, 512 + 1088, F32, "p (t h) -> p t h", t=17)
        LFN = tview(2048, 2048 + 1088, F32, "p (t h) -> p t h", t=17)
        LFT = tview(4096, 4096 + TT * 4, F32)
        TMPN = tview(4096 + 8448, 4096 + 8448 + 8192, F32)
        CL = tview(20992, 20992 + 512, F32, "p (j h) -> p j h", j=8)
        LFS = tview(21504, 21504 + 4160, F32)
        TMPS = tview(25664, 25664 + 4160, F32)
        S.dma("pool", "prepw", WF[:, :, :], b_wf[slot].rearrange("(c p) h -> p c h", p=128), writes=[tr_buf])
        S.dma("sp", "prep", BF_[:, :], bass.AP(b_bf.tensor, slot * NH, [[0, 128], [1, NH]]), writes=[tr_buf])
        pb, pbb = bank("s")
        fns = []
        for t in range(17):
            rows = 128 if t < 16 else TS
            for dc in range(NCH):
                fns.append(mm(pb[0:rows, t * 16:(t + 1) * 16], H[:, dc, t * 128:t * 128 + rows], WF[:, dc, :],
                              dc == 0, dc == NCH - 1))
        S.pe_group(fns, reads=[tr_buf] + h_buf, writes=[pbb])
        psv = pb[:, 0:272].rearrange("p (t h) -> p t h", t=17)
        S.op("dve", lambda e: e.tensor_tensor(out=ZF[:, :, :], in0=psv, in1=BF_[:, :].unsqueeze(1).broadcast_to([128, 17, 16]),
                                              op=ALU.add), reads=[pbb, tr_buf], writes=[tr_buf])
        S.op("act", lambda e: e.activation(out=ZF[:, :, :], in_=ZF[:, :, :], func=AF.Exp, scale=-1.0),
             reads=[tr_buf], writes=[tr_buf])
        S.op("act", lambda e: e.activation(out=ZF[:, :, :], in_=ZF[:, :, :], func=AF.Ln, bias=1.0),
             reads=[tr_buf], writes=[tr_buf])
        S.op("dve", lambda e: e.tensor_scalar(out=LFN[:, :, :], in0=ZF[:, :, :], scalar1=-1.0, scalar2=None, op0=ALU.mult),
             reads=[tr_buf], writes=[tr_buf])
        S.dma("sp", "prep", o_b_fp.rearrange("(t p) h -> p t h", p=128), LFN[:, 0:16, :], reads=[tr_buf])
        S.dma("sp", "prep", o_b_fs, LFN[0:TS, 16, :], reads=[tr_buf])
        for g in range(5):
            pb, pbb = bank("s")
            tl = list(range(g * 4, min(g * 4 + 4, 17)))
            fns = []
            for jj, t in enumerate(tl):
                rows = 128 if t < 16 else TS
                fns.append(lambda e, pb=pb, jj=jj, t=t, rows=rows: e.transpose(
                    out=pb[0:16, jj * 128:jj * 128 + rows], in_=LFN[0:rows, t, :], identity=ident_f[0:rows, 0:rows]))
            S.pe_group(fns, reads=[tr_buf, idf_buf], writes=[pbb])
            ncol = 512 if g < 4 else TS
            S.op("dve", lambda e, pb=pb, g=g, ncol=ncol: e.tensor_copy(out=LFT[0:16, g * 512:g * 512 + ncol],
                                                                        in_=pb[0:16, 0:ncol]), reads=[pbb], writes=[tr_buf])
        S.op("dve", lambda e: e.tensor_tensor_scan(out=LFT[0:16, 0:TP], data0=LFT[0:16, 0:TP], data1=LFT[0:16, 0:TP],
                                                   initial=0.0, op0=ALU.add, op1=ALU.bypass),
             reads=[tr_buf], writes=[tr_buf])
        for Q in range(4):
            nq = (4 * Q + 4) * 128
            ref = 512 * Q + 511
            S.op("dve", lambda e, nq=nq, ref=ref: e.tensor_scalar(out=TMPN[0:16, 0:nq], in0=LFT[0:16, 0:nq],
                                                                  scalar1=LFT[0:16, ref:ref + 1], scalar2=-1.0,
                                                                  op0=ALU.subtract, op1=ALU.mult),
                 reads=[tr_buf], writes=[tr_buf])
            pb, pbb = bank("s")
            nt = 4 * Q + 4
            fns = [lambda e, pb=pb, j=j: e.transpose(out=pb[:, j * 16:(j + 1) * 16], in_=TMPN[0:16, j * 128:(j + 1) * 128],
                                                     identity=ident_f[0:16, 0:16]) for j in range(nt)]
            S.pe_group(fns, reads=[tr_buf, idf_buf], writes=[pbb])
            base = nbp_base(Q)
            S.op("dve", lambda e, pb=pb, nt=nt, base=base: e.tensor_copy(
                out=NBP[:, base:base + nt, :], in_=pb[:, 0:nt * 16].rearrange("p (t h) -> p t h", t=nt)),
                reads=[pbb], writes=[nb_buf])
        for s in range(NS):
            S.dma("sp", "prep", CL[:, :, :], cb_f[s].rearrange("(j p) h -> p j h", p=128), writes=[tr_buf])
            for g in range(2):
                pb, pbb = bank("s")
                fns = [lambda e, pb=pb, j=j, g=g: e.transpose(out=pb[0:16, j * 128:(j + 1) * 128], in_=CL[:, g * 4 + j, :],
                                                              identity=ident_f) for j in range(4)]
                S.pe_group(fns, reads=[tr_buf, idf_buf], writes=[pbb])
                S.op("dve", lambda e, pb=pb, g=g: e.tensor_copy(out=LFS[0:16, g * 512:(g + 1) * 512], in_=pb[0:16, :]),
                     reads=[pbb], writes=[tr_buf])
            S.op("dve", lambda e, s=s: e.tensor_copy(out=LFS[0:16, 1024:1040], in_=LFT[0:16, TP + s * LS:TP + (s + 1) * LS]),
                 reads=[tr_buf], writes=[tr_buf])
            S.op("dve", lambda e: e.tensor_tensor_scan(out=LFS[0:16, 0:1040], data0=LFS[0:16, 0:1040],
                                                       data1=LFS[0:16, 0:1040], initial=0.0, op0=ALU.add, op1=ALU.bypass),
                 reads=[tr_buf], writes=[tr_buf])
            S.op("dve", lambda e: e.tensor_scalar(out=TMPS[0:16, 0:1040], in0=LFS[0:16, 0:1040],
                                                  scalar1=LFS[0:16, 1039:1040], scalar2=-1.0,
                                                  op0=ALU.subtract, op1=ALU.mult), reads=[tr_buf], writes=[tr_buf])
            pb, pbb = bank("s")
            fns = [lambda e, pb=pb, j=j: e.transpose(out=pb[:, j * 16:(j + 1) * 16], in_=TMPS[0:16, j * 128:(j + 1) * 128],
                                                     identity=ident_f[0:16, 0:16]) for j in range(8)]
            fns.append(lambda e, pb=pb: e.transpose(out=pb[0:16, 128:144], in_=TMPS[0:16, 1024:1040],
                                                    identity=ident_f[0:16, 0:16]))
            S.pe_group(fns, reads=[tr_buf, idf_buf], writes=[pbb])
            S.op("dve", lambda e, pb=pb, s=s: e.tensor_copy(out=NBS[:, s, :, :],
                                                            in_=pb[:, 0:144].rearrange("p (t h) -> p t h", t=9)),
                 reads=[pbb], writes=[nb_buf])

    def nbp_base(Q):
        return sum(4 * q + 4 for q in range(Q))

    def prompt_units(kind, c):
        QT0, QT1, KT, VB, v1 = MV["QT0"], MV["QT1"], MV["KT"], MV["VB"], MV["v1"]
        BIAS, NBP = MV.get("BIAS"), MV.get("NBP")
        units = []
        for Q in range(4):
            ob, obuf = bank("a")
            if kind != 2:
                dbk, dbuf = bank("a")
            if kind == 0:
                jl = list(range(max(0, 4 * Q - 4), 4 * Q + 4))
            elif kind == 1:
                jl = list(range(0, 4 * Q + 4))
            else:
                jl = list(range(4 * Q + 3, -1, -1))
            for ji, j in enumerate(jl):
                for hh in range(2):
                    h = 2 * c + hh
                    pb0 = 64 * hh
                    if kind == 0:
                        lo = max(0, 128 * j - 512 * Q)
                        hi = min(512, 128 * j + 640 - 512 * Q)
                    else:
                        lo = max(0, 128 * j - 512 * Q)
                        hi = 512
                    n = hi - lo
                    q0 = 512 * Q + lo
                    u = dict(nk=128, n=n, kT=KT[:, j * 128:(j + 1) * 128], q=(QT0, QT1)[hh][:, q0:q0 + n],
                             qk_reads=[kt_buf, qt_buf], extras=[],
                             v=(VB[:, j, hh * 64:(hh + 1) * 64] if kind == 2 else VB[:, j, hh * 64:hh * 64 + 128]),
                             v_reads=[vb_buf],
                             pbase=pb0, ocol=lo, scol=lo, first=(ji == 0), last=(ji == len(jl) - 1), hh=hh)
                    if kind == 0:
                        b0 = q0 - 128 * j
                        u["extras"].append((anti_b, BIAS[:, hh, b0:b0 + n], 0, n))
                        u["qk_reads"] = [kt_buf, qt_buf, bias_buf]
                    else:
                        r = j - 4 * Q
                        if r >= 0:
                            u["extras"].append((ident_b, mask_le if kind == 1 else mask_lt, 0, 128))
                    if kind == 1:
                        u["bias"] = NBP[:, nbp_base(Q) + j, h:h + 1]
                    u["o"] = (ob, obuf)
                    if kind != 2:
                        u["d"] = (dbk, dbuf)
                    else:
                        u["seq"] = ("p", Q, hh)
                        u["seq_first"] = (ji == 0)
                        u["seq_last"] = (ji == len(jl) - 1)
                    if ji == len(jl) - 1 and hh == 1:
                        if kind != 2:
                            u["fin"] = softmax_fin(ob, obuf, dbk, dbuf, c, 512 * Q, 512, [Q])
                        else:
                            u["fin"] = copy_fin(ob, obuf, c, 512 * Q, 512, [Q])
                    units.append(u)
        return units

    def sample_units(kind, c, s, ksl, nt, ob, obuf, dbk, dbuf, is_last_seq):
        KT, VS, VCC = MV["KT"], MV["VS"], MV["VCC"]
        BS, NBS = MV.get("BS"), MV.get("NBS")
        units = []
        tl = list(range(nt)) + ["new"]
        if kind == 2:
            tl = ["new"] + list(range(nt - 1, -1, -1))
        q0 = TP + s * LS
        for ji, j in enumerate(tl):
            if j == "new":
                nk = LS
                kT = KT[:, q0:q0 + LS]
                qk_reads = [kt_buf, qs_buf]
                v_reads = [vs_buf]
                vsrc = lambda a, b: VS[0:LS, s, a:b]
            else:
                nk = 128
                kT = KTC[:, ksl, j * 128:(j + 1) * 128]
                qk_reads = [ktc_buf[ksl], qs_buf]
                v_reads = [vcc_buf[ksl]]
                vsrc = lambda a, b, j=j: VCC[:, ksl, j, a:b]
            if kind == 2:
                pv = [("o", vsrc(0, 64)), ("o", vsrc(64, 128))]
            else:
                pv = [("o", vsrc(0, 128)), ("d", vsrc(64, 192))]
            u = dict(nk=nk, n=2 * LS, kT=kT, q=QS[:, s, :, :].rearrange("p h i -> p (h i)"), qk_reads=qk_reads, extras=[],
                     v=None, v_reads=v_reads, pbase=0, ocol=s * LS, scol=0, first=(ji == 0), last=(ji == len(tl) - 1),
                     hh=0, pv=pv)
            jidx = nt if j == "new" else j
            if kind == 0:
                ja = 4 if j == "new" else j
                u["extras"].append(((anti_b if nk == 128 else anti16_b[0:LS, 0:LS]),
                                    BS[0:nk, ja, :, :].rearrange("p h i -> p (h i)"), 0, 2 * LS))
                u["qk_reads"] = qk_reads + [bs_buf]
            elif j == "new":
                mk = (mask_le if kind == 1 else mask_lt)[0:LS, 0:LS]
                u["extras"].append((ident_b[0:LS, 0:LS], mk, 0, LS))
                u["extras"].append((ident_b[0:LS, 0:LS], mk, LS, LS))
            if kind == 1:
                u["bias2"] = [NBS[0:nk, s, jidx, 2 * c + hh:2 * c + hh + 1] for hh in range(2)]
            u["o"] = (ob, obuf)
            if kind != 2:
                u["d"] = (dbk, dbuf)
            else:
                u["seq"] = ("s", s)
                u["seq_first"] = (ji == 0)
                u["seq_last"] = (ji == len(tl) - 1)
            if is_last_seq and ji == len(tl) - 1:
                if kind != 2:
                    u["fin"] = softmax_fin(ob, obuf, dbk, dbuf, c, TP, TS, [4])
                else:
                    u["fin"] = copy_fin(ob, obuf, c, TP, TS, [4])
            units.append(u)
        return units

    def oproj(wo_d, swapped):
        WO = MV["WO"]
        wsrc = wo_d.rearrange("(c p) f -> p c f", p=128)
        for dp in range(NCH):
            sl = cnt["wo"] % 2
            cnt["wo"] += 1
            if swapped:
                S.dma("pool", "wo%d_a" % sl, WO[0:64, sl, :, :], wsrc[64:128, :, dp * 128:(dp + 1) * 128],
                      writes=[wo_buf[sl]])
                S.dma("pool", "wo%d_b" % sl, WO[64:128, sl, :, :], wsrc[0:64, :, dp * 128:(dp + 1) * 128],
                      writes=[wo_buf[sl]])
            else:
                S.dma("pool", "wo%d_a" % sl, WO[:, sl, :, :], wsrc[:, :, dp * 128:(dp + 1) * 128], writes=[wo_buf[sl]])
            def xadd(po, pob, tt, t0, n, dp=dp):
                S.op("dve", lambda e, po=po, dp=dp, t0=t0, n=n: e.tensor_tensor(
                    out=X[:, dp, t0:t0 + n], in0=po[:, 0:n], in1=X[:, dp, t0:t0 + n], op=ALU.add),
                    reads=[pob, x_buf[tt][dp]], writes=[x_buf[tt][dp]])

            for tt, (t0, n) in enumerate(TTILES[0:3]):
                po, pob = bank("a")
                fns = [mm(po[:, 0:n], WO[:, sl, c, :], OT[:, c, t0:t0 + n], c == 0, c == NCH - 1) for c in range(NCH)]
                S.pe_group(fns, reads=[wo_buf[sl]] + [ot_buf[c][tt] for c in range(NCH)], writes=[pob])
                xadd(po, pob, tt, t0, n)
            (t3, n3), (t4, n4) = TTILES[3], TTILES[4]
            po3, pob3 = bank("a")
            po4, pob4 = bank("a")
            fns = []
            for c in range(NCH):
                fns.append(mm(po3[:, 0:n3], WO[:, sl, c, :], OT[:, c, t3:t3 + n3], c == 0, c == NCH - 1))
                fns.append(mm(po4[:, 0:n4], WO[:, sl, c, :], OT[:, c, t4:t4 + n4], c == 0, c == NCH - 1))
            S.pe_group(fns, reads=[wo_buf[sl]] + [ot_buf[c][3] for c in range(NCH)] + [ot_buf[c][4] for c in range(NCH)],
                       writes=[pob3, pob4])
            xadd(po3, pob3, 3, t3, n3)
            xadd(po4, pob4, 4, t4, n4)

    def mixer(l):
        kind, slot = l % 3, l // 3
        norm(1, l, H, h_buf)
        S.barrier()
        set_views(kind)
        QT0, QT1 = MV["QT0"], MV["QT1"]
        if kind == 0:
            wqkv_d, wo_d = a_qkv[slot], a_o[slot]
            outs = (o_a_kp[slot], o_a_vp[slot], o_a_ks[slot], o_a_vs[slot])
            ck, cv, nt = ca_k[slot], ca_v[slot], 4
            prep_A(slot)
        elif kind == 1:
            wqkv_d, wo_d = b_qkv[slot], b_o[slot]
            outs = (o_b_kp, o_b_vp, o_b_ks, o_b_vs)
            ck, cv, nt = cb_k, cb_v, 8
            prep_B(slot)
        else:
            wqkv_d, wo_d = c_qkv[slot], c_o[slot]
            outs = (o_c_kp, o_c_vp, o_c_ks, o_c_vs)
            ck, cv, nt = cc_k, cc_v, 8
        S.barrier()
        S.op("dve", lambda e: e.memset(QS[:, :, :, :], 0.0), writes=[qs_buf])
        S.op("dve", lambda e: e.memset(QT0[64:128, :], 0.0), writes=[qt_buf])
        S.op("dve", lambda e: e.memset(QT1[0:64, :], 0.0), writes=[qt_buf])
        if kind != 2:
            VB_, VS_, VCC_ = MV["VB"], MV["VS"], MV["VCC"]
            S.op("dve", lambda e: e.memset(VB_[:, :, 64:128], 1.0), writes=[vb_buf])
            S.op("dve", lambda e: e.memset(VS_[:, :, 64:128], 1.0), writes=[vs_buf])
            S.op("dve", lambda e: e.memset(VCC_[:, :, :, 64:128], 1.0), writes=vcc_buf)
        load_wqkv(0, wqkv_d)
        for c in range(NCH):
            if "no_sample" not in DBG:
                issue_cache_dma(0, c, ck, cv, nt, 0)
            project_pair(kind, slot, c, wqkv_d, outs)
            if c + 1 < NCH:
                load_wqkv(c + 1, wqkv_d)
            if kind == 0 and "no_bias" not in DBG:
                load_bias_A(c)
            units = prompt_units(kind, c)
            if "no_bias" in DBG:
                for u in units:
                    u["extras"] = [x for x in u["extras"] if x[0] is not anti_b]
                    u["qk_reads"] = [kt_buf, qt_buf]
            ob, obuf = bank("a")
            dbk = dbuf = None
            if kind != 2:
                dbk, dbuf = bank("a")
            su = []
            for s in range(NS):
                su.append(sample_units(kind, c, s, s % 2, nt, ob, obuf, dbk, dbuf, s == NS - 1))

            def mk_load(s, c=c):
                return lambda: load_cache_pair(s, c, ck, cv, nt, s % 2)

            def mk_pre(c=c):
                def pre():
                    cache_transposes(nt, 0)
                    load_cache_pair(1, c, ck, cv, nt, 1)
                return pre
            if "no_sample" not in DBG:
                units[0]["pre"] = mk_pre()
                for s in range(NS - 2):
                    su[s][-1]["post"] = mk_load(s + 2)
                for s in range(NS):
                    units += su[s]
            if "no_attn" in DBG:
                units = []
            if kind != 2:
                run_softmax_units(units)
            else:
                run_stick_units(units)
        oproj(wo_d, kind != 2)
        S.barrier()

    yt_buf = [Buf("yt%d" % t) for t in range(NTT)]

    def final_out():
        S.barrier()
        norm(3, 0, YT, yt_buf)
        S.barrier()
        ntile = TP // 128 + 1
        for t in range(ntile):
            rows = 128 if t < TP // 128 else TS
            dstd = y_p[t * 128:(t + 1) * 128, :] if t < TP // 128 else y_s[:, :]
            tt = min(t // 4, 4)
            sl = cnt["xin"] % 4
            cnt["xin"] += 1
            for half in range(2):
                pb, pbb = bank("s")
                fns = []
                for j in range(4):
                    c = half * 4 + j
                    fns.append(lambda e, pb=pb, j=j, c=c, t=t, rows=rows: e.transpose(
                        out=pb[0:rows, j * 128:(j + 1) * 128], in_=YT[:, c, t * 128:t * 128 + rows],
                        identity=ident_f))
                S.pe_group(fns, reads=[yt_buf[tt], idf_buf], writes=[pbb])
                dst = XIN[0:rows, sl, half * 512:(half + 1) * 512]
                if half == 0:
                    S.op("act", lambda e, dst=dst, pb=pb, rows=rows: e.copy(out=dst, in_=pb[0:rows, :]),
                         reads=[pbb], writes=[xin_buf[sl]])
                else:
                    S.op("dve", lambda e, dst=dst, pb=pb, rows=rows: e.tensor_copy(out=dst, in_=pb[0:rows, :]),
                         reads=[pbb], writes=[xin_buf[sl]])
            S.dma("sp", "xin%d" % sl, dstd, XIN[0:rows, sl, :], reads=[xin_buf[sl]])

    load_x()
    S.barrier()
    done = False
    for l in range(DEPTH):
        S.new_epoch()
        for stage in ("ffn1", "mix", "ffn2"):
            if stage == "ffn1":
                ffn(l, 1)
            elif stage == "mix":
                mixer(l)
            else:
                ffn(l, 2)
            if stop_after == (l, stage):
                done = True
                break
        if done:
            break
    S.new_epoch()
    final_out()
    S.final_wait("sp")
    S.replay()
    st.close()
    return nc


W_NAMES = ["norm_ffn1", "norm_mix", "norm_ffn2", "ffn1_gate", "ffn1_up", "ffn1_down", "ffn2_gate", "ffn2_up",
           "ffn2_down", "a_w_qkv", "a_w_o", "a_rel_bias", "b_w_qkv", "b_w_o", "b_w_f", "b_b_f", "c_w_qkv", "c_w_o"]

STOP_AFTER = None
DBG = set()


def kernel(**inputs):
    n = 8
    f = lambda a: np.ascontiguousarray(np.asarray(a, dtype=np.float32))
    nc = build(STOP_AFTER)
    cf, cb = _consts_np()
    shared = {k: f(inputs[k]) for k in W_NAMES if not ("skip_ffn" in DBG and k.startswith("ffn"))}
    shared["norm_final"] = f(inputs["norm_final"]).reshape(1, D)
    shared["consts_f"] = cf
    shared["consts_b"] = cb
    in_maps = []
    for c in range(n):
        m = dict(shared)
        sl = slice(NS * c, NS * (c + 1))
        m["x_p"] = f(inputs["x_prompt"][c])
        m["x_s"] = f(inputs["x_sample"][sl]).reshape(TS, D)
        m["ca_k"] = f(inputs["cache_a_k"][:, sl]).reshape(2, NS, 512, D)
        m["ca_v"] = f(inputs["cache_a_v"][:, sl]).reshape(2, NS, 512, D)
        m["cb_k"] = f(inputs["cache_b_k"][0, sl]).reshape(NS, 1024, D)
        m["cb_v"] = f(inputs["cache_b_v"][0, sl]).reshape(NS, 1024, D)
        m["cb_f"] = f(inputs["cache_b_logf"][0, sl])
        m["cc_k"] = f(inputs["cache_c_k"][0, sl]).reshape(NS, 1024, D)
        m["cc_v"] = f(inputs["cache_c_v"][0, sl]).reshape(NS, 1024, D)
        in_maps.append(m)
    res = run_bass_kernel_spmd(nc, in_maps, core_ids=list(range(n)))
    R = res.results

    def gp(name, shape):
        return np.stack([R[c][name].reshape(shape) for c in range(n)], axis=0)

    y_prompt = gp("y_p", (TP, D))
    y_sample = gp("y_s", (NS, LS, D)).reshape(n * NS, LS, D)
    a_kp = gp("a_kp", (2, 512, NH, HD)).transpose(1, 0, 2, 3, 4)
    a_vp = gp("a_vp", (2, 512, NH, HD)).transpose(1, 0, 2, 3, 4)
    a_ks = gp("a_ks", (2, NS, LS, NH, HD)).transpose(1, 0, 2, 3, 4, 5).reshape(2, n * NS, LS, NH, HD)
    a_vs = gp("a_vs", (2, NS, LS, NH, HD)).transpose(1, 0, 2, 3, 4, 5).reshape(2, n * NS, LS, NH, HD)

    def one_p(name, last):
        return gp(name, (TP,) + last)[None]

    def one_s(name, last):
        return gp(name, (NS, LS) + last).reshape((1, n * NS, LS) + last)

    outs = (y_prompt, y_sample, a_kp, a_vp, a_ks, a_vs,
            one_p("b_kp", (NH, HD)), one_p("b_vp", (NH, HD)), one_p("b_fp", (NH,)),
            one_s("b_ks", (NH, HD)), one_s("b_vs", (NH, HD)), one_s("b_fs", (NH,)),
            one_p("c_kp", (NH, HD)), one_p("c_vp", (NH, HD)), one_s("c_ks", (NH, HD)), one_s("c_vs", (NH, HD)))
    return tuple(np.ascontiguousarray(o, dtype=np.float32) for o in outs)
```

```python
import numpy as np
import concourse.bass as bass
import concourse.mybir as mybir
from concourse.bass_utils import run_bass_kernel_spmd
from contextlib import ExitStack

F32 = mybir.dt.float32
BF16 = mybir.dt.bfloat16
U8 = mybir.dt.uint8
AF = mybir.ActivationFunctionType
ALU = mybir.AluOpType

D = 1024
NCH = 8
TP = 2048
NS = 4
LS = 16
TS = NS * LS
TT = TP + TS
DFF = 2816
NFC = 22
DEPTH = 4
NH = 16
HD = 64
EPS = 1e-6
TTILES = [(0, 512), (512, 512), (1024, 512), (1536, 512), (2048, 64)]
NEG = -30000.0

ENGS = ["pe", "act", "dve", "pool", "sp"]


class Buf:
    __slots__ = ("name", "w", "r", "excl")

    def __init__(self, name, excl=False):
        self.name = name
        self.w = None
        self.r = []
        self.excl = excl


class Sched:
    def __init__(self, nc, stack):
        self.nc = nc
        self.stack = stack
        self.q = {e: [] for e in ENGS}
        self.cnt = {}
        self.semh = {}
        self.epoch = 0
        self.known = {e: {} for e in ENGS}
        self.nsem = 0

    def _key_init(self, key):
        if key not in self.cnt:
            self.cnt[key] = 0
            self.semh[key] = self.stack.enter_context(self.nc.semaphore("s%d" % self.nsem))
            self.nsem += 1

    def pkey(self, eng):
        key = ("p", eng, self.epoch)
        self._key_init(key)
        return key

    def _deps(self, eng, reads, writes, extra):
        deps = {}

        def add(tok, same_ok):
            if tok is None:
                return
            key, val = tok
            if key[0] == "p" and key[1] == eng and not same_ok:
                return
            if deps.get(key, 0) < val:
                deps[key] = val

        same = eng != "pe"
        for b in reads:
            add(b.w, same)
            if b.excl:
                for t in b.r:
                    add(t, False)
        for b in writes:
            add(b.w, same)
            for t in b.r:
                add(t, same)
        for t in extra:
            add(t, True)
        out = []
        kn = self.known[eng]
        for key, val in deps.items():
            if kn.get(key, 0) >= val:
                continue
            kn[key] = val
            out.append((key, val))
        return out

    def _post(self, tok, reads, writes):
        for b in writes:
            b.w = tok
            b.r = []
        for b in reads:
            b.r.append(tok)

    def op(self, eng, fn, reads=(), writes=(), extra=()):
        waits = self._deps(eng, reads, writes, extra)
        key = self.pkey(eng)
        self.cnt[key] += 1
        tok = (key, self.cnt[key])
        self.q[eng].append((fn, waits, (key, 1)))
        self._post(tok, reads, writes)
        return tok

    def pe_group(self, fns, reads=(), writes=(), extra=()):
        waits = self._deps("pe", reads, writes, extra)
        key = self.pkey("pe")
        self.cnt[key] += 1
        tok = (key, self.cnt[key])
        n = len(fns)
        for i, fn in enumerate(fns):
            self.q["pe"].append((fn, waits if i == 0 else [], (key, 1) if i == n - 1 else None))
        self._post(tok, reads, writes)
        return tok

    def dma(self, queue, semname, out, in_, reads=(), writes=(), extra=()):
        waits = self._deps(queue, reads, writes, extra)
        key = ("d", semname)
        self._key_init(key)
        self.cnt[key] += 16
        tok = (key, self.cnt[key])

        def fn(eng, out=out, in_=in_):
            return eng.dma_start(out=out, in_=in_)

        self.q[queue].append((fn, waits, (key, 16)))
        self._post(tok, reads, writes)
        return tok

    def barrier(self):
        toks = [(k, v) for k, v in self.cnt.items() if v > 0]
        for e in ENGS:
            waits = []
            kn = self.known[e]
            for key, val in toks:
                if key[0] == "p" and key[1] == e:
                    continue
                if kn.get(key, 0) >= val:
                    continue
                kn[key] = val
                waits.append((key, val))
            if waits:
                self.q[e].append((None, waits, None))

    def new_epoch(self):
        self.epoch += 1

    def final_wait(self, eng="sp"):
        waits = [(k, v) for k, v in self.cnt.items() if v > 0 and k[0] == "d"]
        self.q[eng].append((None, waits, None))

    def replay(self):
        nc = self.nc
        semh = self.semh

        def run(eng, lst):
            for fn, waits, inc in lst:
                for key, val in waits:
                    eng.wait_ge(semh[key], val)
                if fn is not None:
                    ins = fn(eng)
                    if inc is not None:
                        ins.then_inc(semh[inc[0]], inc[1])

        with nc.Block() as block:
            @block.tensor
            def _(e):
                run(e, self.q["pe"])

            @block.scalar
            def _(e):
                run(e, self.q["act"])

            @block.vector
            def _(e):
                run(e, self.q["dve"])

            @block.gpsimd
            def _(e):
                run(e, self.q["pool"])

            @block.sync
            def _(e):
                run(e, self.q["sp"])


def _consts_np():
    c = np.zeros((128, 9, 128), np.float32)
    k = np.arange(128)[:, None]
    q = np.arange(128)[None, :]
    c[:, 0, :] = np.eye(128, dtype=np.float32)
    c[:, 1, :] = 1.0
    c[:, 2, :] = np.where(k > q, NEG, 0.0)
    c[:, 3, :] = np.where(k >= q, NEG, 0.0)
    c[:, 4, :] = np.where(k >= q, -1.0, 0.0)
    c[:, 5, :] = -1.0
    c[:, 6, :] = np.where(k + q == 127, 1.0, 0.0)
    c[:16, 7, :16] = np.where(k[:16] + q[:, :16] == 15, 1.0, 0.0)
    c[:, 8, :] = np.where(np.abs(k - q) == 64, 1.0, 0.0)
    return np.eye(128, dtype=np.float32), c.reshape(128, 1152)


def build(stop_after=None):
    nc = bass.Bass("TRN2", target_bir_lowering=False)
    st = ExitStack()

    def din(name, shape):
        return nc.dram_tensor(name, list(shape), F32, kind="ExternalInput").ap()

    def dout(name, shape):
        return nc.dram_tensor(name, list(shape), F32, kind="ExternalOutput").ap()

    x_p = din("x_p", (TP, D))
    x_s = din("x_s", (TS, D))
    consts_f = din("consts_f", (128, 128))
    consts_b = din("consts_b", (128, 1152))
    norm_ffn1 = din("norm_ffn1", (DEPTH, D))
    norm_mix = din("norm_mix", (DEPTH, D))
    norm_ffn2 = din("norm_ffn2", (DEPTH, D))
    norm_final = din("norm_final", (1, D))
    ffn_w = {}
    if "skip_ffn" not in DBG:
        for which in (1, 2):
            ffn_w[which] = (din("ffn%d_gate" % which, (DEPTH, D, DFF)), din("ffn%d_up" % which, (DEPTH, D, DFF)),
                            din("ffn%d_down" % which, (DEPTH, DFF, D)))
    ca_k = din("ca_k", (2, NS, 512, D))
    ca_v = din("ca_v", (2, NS, 512, D))
    cb_k = din("cb_k", (NS, 1024, D))
    cb_v = din("cb_v", (NS, 1024, D))
    cb_f = din("cb_f", (NS, 1024, NH))
    cc_k = din("cc_k", (NS, 1024, D))
    cc_v = din("cc_v", (NS, 1024, D))
    a_qkv = din("a_w_qkv", (2, D, 3 * D))
    a_o = din("a_w_o", (2, D, D))
    a_rel = din("a_rel_bias", (2, 513, NH))
    b_qkv = din("b_w_qkv", (1, D, 3 * D))
    b_o = din("b_w_o", (1, D, D))
    b_wf = din("b_w_f", (1, D, NH))
    b_bf = din("b_b_f", (1, NH))
    c_qkv = din("c_w_qkv", (1, D, 3 * D))
    c_o = din("c_w_o", (1, D, D))
    y_p = dout("y_p", (TP, D))
    y_s = dout("y_s", (TS, D))
    o_a_kp = dout("a_kp", (2, 512, D))
    o_a_vp = dout("a_vp", (2, 512, D))
    o_a_ks = dout("a_ks", (2, TS, D))
    o_a_vs = dout("a_vs", (2, TS, D))
    o_b_kp = dout("b_kp", (TP, D))
    o_b_vp = dout("b_vp", (TP, D))
    o_b_fp = dout("b_fp", (TP, NH))
    o_b_ks = dout("b_ks", (TS, D))
    o_b_vs = dout("b_vs", (TS, D))
    o_b_fs = dout("b_fs", (TS, NH))
    o_c_kp = dout("c_kp", (TP, D))
    o_c_vp = dout("c_vp", (TP, D))
    o_c_ks = dout("c_ks", (TS, D))
    o_c_vs = dout("c_vs", (TS, D))
    EH = nc.dram_tensor("eh_scratch", [NH * 768], F32)

    S = Sched(nc, st)

    X = st.enter_context(nc.sbuf_tensor("X", [128, NCH, TT], F32))
    HU = st.enter_context(nc.sbuf_tensor("HU", [128, 2 * NCH * TT * 2], U8))
    Wr = st.enter_context(nc.sbuf_tensor("Wr", [128, 34816], U8))
    SQr = st.enter_context(nc.sbuf_tensor("SQr", [128, 8192], U8))
    RSr = st.enter_context(nc.sbuf_tensor("RSr", [128, 4096], U8))
    IDF = st.enter_context(nc.sbuf_tensor("IDF", [128, 128], F32))
    CB = st.enter_context(nc.sbuf_tensor("CB", [128, 9, 128], BF16))
    GAIN = st.enter_context(nc.sbuf_tensor("GAIN", [128, 13 * NCH], F32))
    QS = st.enter_context(nc.sbuf_tensor("QS", [128, NS, 2, LS], BF16))
    ESZ = 26752
    Er = st.enter_context(nc.sbuf_tensor("Er", [128, ESZ], U8))

    def view(raw, a, b, dt, pat=None, **kw):
        v = raw[:, a:b].bitcast(dt)
        if pat is not None:
            v = v.rearrange(pat, **kw)
        return v

    HB = NCH * TT * 2
    H = view(HU, 0, HB, BF16, "p (c t) -> p c t", c=NCH)
    ACT = view(HU, HB, 2 * HB, BF16, "p (c t) -> p c t", c=NCH)
    OT = ACT
    YT = view(HU, 0, 2 * HB, F32, "p (c t) -> p c t", c=NCH)
    SQ = view(SQr, 0, 8192, BF16, "p (c t) -> p c t", c=NCH)
    GST = SQr[0:104, 0:512].bitcast(F32)
    RS = view(RSr, 0, 4096, F32, "p (s t) -> p s t", s=2)
    WGU = view(Wr, 0, 16384, BF16, "p (s g c f) -> p s g c f", s=4, g=2, c=NCH)
    WD = view(Wr, 16384, 32768, BF16, "p (s f) -> p s f", s=8)
    SG = view(Wr, 32768, 34816, BF16, "p (s f) -> p s f", s=2)
    XIN = view(Wr, 0, 16384, F32, "p (s f) -> p s f", s=4)
    WQKV = view(Wr, 0, 12288, BF16, "p (s m c f) -> p s m c f", s=2, m=3, c=NCH)
    PT = view(Er, 0, 4096, BF16, "p (s f) -> p s f", s=4)
    KST = view(Er, 4096, 6144, F32, "p (s j f) -> p s j f", s=2, j=2)
    VST = view(Er, 6144, 8192, F32, "p (s j f) -> p s j f", s=2, j=2)
    KTC = view(Er, 8192, 12288, BF16, "p (s f) -> p s f", s=2)
    MV = {}

    def set_views(kind):
        MV.clear()
        MV["WO"] = view(Wr, 12288, 16384, BF16, "p (s c f) -> p s c f", s=2, c=NCH)
        MV["QT0"] = view(Wr, 16384, 20608, BF16)
        MV["KT"] = view(Wr, 20608, 24832, BF16)
        if kind != 2:
            vw = 192
            MV["VB"] = view(Wr, 24832, 31360, BF16, "p (t f) -> p t f", t=17)
            MV["BS"] = view(Wr, 31360, 31680, BF16, "p (j h i) -> p j h i", j=5, h=2)
            MV["NBP"] = view(Wr, 31360, 33920, F32, "p (t h) -> p t h", h=NH)
            MV["VCC"] = view(Er, 12288, 18432, BF16, "p (s j f) -> p s j f", s=2, j=8)
            MV["QT1"] = view(Er, 18432, 22656, BF16)
            MV["BIAS"] = view(Er, 22656, 25216, BF16, "p (h f) -> p h f", h=2)
            MV["NBS"] = view(Er, 22656, 24960, F32, "p (s j h) -> p s j h", s=NS, j=9)
            MV["VS"] = view(Er, 25216, 26752, BF16, "p (s f) -> p s f", s=NS)
        else:
            vw = 128
            MV["VB"] = view(Wr, 24832, 29184, BF16, "p (t f) -> p t f", t=17)
            MV["VS"] = view(Wr, 29184, 30208, BF16, "p (s f) -> p s f", s=NS)
            MV["VCC"] = view(Er, 12288, 16384, BF16, "p (s j f) -> p s j f", s=2, j=8)
            MV["QT1"] = view(Er, 16384, 20608, BF16)
        MV["vw"] = vw
        MV["v1"] = vw - 64

    LB = view(Er, 20608, 22656, BF16, "p (s f) -> p s f", s=2)
    ACC = view(Er, 22656, 26752, BF16, "p (s f) -> p s f", s=4)
    EF = view(SQr, 0, 4096, F32, "p (s f) -> p s f", s=2)
    KCS = view(SQr, 4096, 8192, F32, "p (j f) -> p j f", j=8)
    RCP = RS
    def tview(a, b, dt, pat=None, **kw):
        return view(HU, HB + a, HB + b, dt, pat, **kw)

    ident_f = IDF[:, :]
    ident_b = CB[:, 0, :]
    ones_b = CB[:, 1, :]
    mask_le = CB[:, 2, :]
    mask_lt = CB[:, 3, :]
    ntri_b = CB[:, 4, :]
    nones_b = CB[:, 5, :]
    anti_b = CB[:, 6, :]
    anti16_b = CB[:, 7, :]
    swap_b = CB[:, 8, :]

    banks = [st.enter_context(nc.psum_tensor("pb%d" % i, [128, 512], F32)) for i in range(8)]
    bank_buf = [Buf("bank%d" % i, excl=True) for i in range(8)]
    ring = {"s": [0, 1, 2, 3], "a": [4, 5, 6, 7]}
    ring_pos = {"s": 0, "a": 0}

    def bank(pool):
        i = ring[pool][ring_pos[pool] % len(ring[pool])]
        ring_pos[pool] += 1
        return banks[i], bank_buf[i]

    NTT = len(TTILES)
    x_buf = [[Buf("x%d_%d" % (t, c)) for c in range(NCH)] for t in range(NTT)]
    h_buf = [Buf("h%d" % t) for t in range(NTT)]
    act_buf = [[Buf("a%d_%d" % (i, t)) for t in range(NTT)] for i in range(8)]
    ot_buf = [[Buf("ot%d_%d" % (c, t)) for t in range(NTT)] for c in range(NCH)]
    wgu_buf = [Buf("wgu%d" % i) for i in range(4)]
    wd_buf = [Buf("wd%d" % i) for i in range(8)]
    sg_buf = [Buf("sg%d" % i) for i in range(2)]
    xin_buf = [Buf("xin%d" % i) for i in range(4)]
    sq_buf = Buf("sq")
    rs_buf = [Buf("rs0"), Buf("rs1")]
    idf_buf = Buf("idf")
    cb_buf = Buf("cb")
    gain_buf = Buf("gain")
    gst_buf = Buf("gst")
    wqkv_buf = [Buf("wqkv%d" % i) for i in range(2)]
    wo_buf = [Buf("wo%d" % i) for i in range(2)]
    qt_buf = Buf("qt")
    qs_buf = Buf("qs")
    kt_buf = Buf("kt")
    vb_buf = Buf("vb")
    vs_buf = Buf("vs")
    pt_buf = [Buf("pt%d" % i) for i in range(4)]
    kst_buf = [Buf("kst0"), Buf("kst1")]
    vst_buf = [Buf("vst0"), Buf("vst1")]
    ktc_buf = [Buf("ktc%d" % i) for i in range(2)]
    vcc_buf = [Buf("vcc%d" % i) for i in range(2)]
    kcs_buf = Buf("kcs")
    bias_buf = Buf("bias")
    bs_buf = Buf("bs")
    nb_buf = Buf("nb")
    rcp_buf = [Buf("rcp0"), Buf("rcp1")]
    ef_buf = [Buf("ef0"), Buf("ef1")]
    lb_buf = [Buf("lb0"), Buf("lb1")]
    acc_buf = [Buf("acc%d" % i) for i in range(4)]
    tr_buf = Buf("transient")
    cnt = {"wgu": 0, "sg": 0, "xin": 0, "wqkv": 0, "wo": 0, "pt": 0, "kst": 0, "vst": 0, "kvc": 0, "rcp": 0,
           "ef": 0, "lb": 0, "accs": 0}

    S.dma("sp", "const", IDF[:, :], consts_f, writes=[idf_buf])
    S.dma("pool", "constb", CB[:, :, :].rearrange("p a b -> p (a b)"), consts_b, writes=[cb_buf])
    for i, g in enumerate([norm_ffn1, norm_mix, norm_ffn2]):
        S.dma("sp", "const", GST[i * 32:(i + 1) * 32, :], g.rearrange("l (c p) -> (l c) p", p=128), writes=[gst_buf])
    S.dma("sp", "const", GST[96:104, :], norm_final.rearrange("l (c p) -> (l c) p", p=128), writes=[gst_buf])
    pb, pbb = bank("s")
    S.pe_group([lambda e, pb=pb: e.transpose(out=pb[:, 0:104], in_=GST[:, :], identity=ident_f[0:104, 0:104])],
               reads=[gst_buf, idf_buf], writes=[pbb])
    S.op("dve", lambda e, pb=pb: e.tensor_copy(out=GAIN[:, :], in_=pb[:, 0:104]), reads=[pbb], writes=[gain_buf])

    def gain_col(kind, l, c):
        j = {0: 0, 1: 32, 2: 64, 3: 96}[kind] + (l * 8 if kind < 3 else 0) + c
        return GAIN[:, j:j + 1]

    def load_x():
        ntile = TP // 128 + 1
        for t in range(ntile):
            rows = 128 if t < TP // 128 else TS
            src = x_p[t * 128:(t + 1) * 128, :] if t < TP // 128 else x_s[:, :]
            sl = cnt["xin"] % 4
            cnt["xin"] += 1
            S.dma("sp", "xin%d" % sl, XIN[0:rows, sl, :], src, writes=[xin_buf[sl]])
            tt = min(t // 4, 4)
            for half in range(2):
                pb, pbb = bank("s")
                fns = []
                for j in range(4):
                    c = half * 4 + j
                    fns.append(lambda e, pb=pb, j=j, c=c, sl=sl, rows=rows: e.transpose(
                        out=pb[:, j * 128:j * 128 + rows], in_=XIN[0:rows, sl, c * 128:(c + 1) * 128],
                        identity=ident_f[0:rows, 0:rows]))
                S.pe_group(fns, reads=[xin_buf[sl], idf_buf], writes=[pbb])
                dst = X[:, half * 4:half * 4 + 4, t * 128:t * 128 + rows]
                srcp = pb[:, :].rearrange("p (j k) -> p j k", j=4)[:, :, 0:rows]
                wl = [x_buf[tt][half * 4 + j] for j in range(4)]
                if half == 0:
                    S.op("act", lambda e, dst=dst, srcp=srcp: e.copy(out=dst, in_=srcp), reads=[pbb], writes=wl)
                else:
                    S.op("dve", lambda e, dst=dst, srcp=srcp: e.tensor_copy(out=dst, in_=srcp), reads=[pbb], writes=wl)

    def norm(kind, l, dst, dst_bufs):
        for tt, (t0, n) in enumerate(TTILES):
            S.op("act", lambda e, t0=t0, n=n: e.activation(out=SQ[:, :, 0:n], in_=X[:, :, t0:t0 + n], func=AF.Square),
                 reads=x_buf[tt], writes=[sq_buf])
            pb, pbb = bank("s")
            fns = [lambda e, pb=pb, c=c, n=n: e.matmul(pb[:, 0:n], lhsT=ones_b, rhs=SQ[:, c, 0:n],
                                                       start=(c == 0), stop=(c == NCH - 1)) for c in range(NCH)]
            S.pe_group(fns, reads=[sq_buf, cb_buf], writes=[pbb])
            S.op("act", lambda e, pb=pb, n=n: e.activation(out=RS[:, 0, 0:n], in_=pb[:, 0:n], func=AF.Ln,
                                                           bias=EPS, scale=1.0 / D),
                 reads=[pbb], writes=[rs_buf[0]])
            S.op("act", lambda e, n=n: e.activation(out=RS[:, 1, 0:n], in_=RS[:, 0, 0:n], func=AF.Exp, scale=-0.5),
                 reads=[rs_buf[0]], writes=[rs_buf[1]])
            for c in range(NCH):
                S.op("dve", lambda e, c=c, t0=t0, n=n: e.scalar_tensor_tensor(
                    out=dst[:, c, t0:t0 + n], in0=X[:, c, t0:t0 + n], scalar=gain_col(kind, l, c),
                    in1=RS[:, 1, 0:n], op0=ALU.mult, op1=ALU.mult),
                    reads=[x_buf[tt][c], rs_buf[1], gain_buf], writes=[dst_bufs[tt]])

    FGROUPS = [list(range(0, 8)), list(range(8, 15)), list(range(15, 22))]

    def ffn(l, which):
        if "skip_ffn" in DBG:
            return
        wg_d, wu_d, wd_d = ffn_w[which]
        norm(0 if which == 1 else 2, l, H, h_buf)
        t3, n3 = TTILES[3]
        t4, n4 = TTILES[4]
        for grp in FGROUPS:
            for i, fc in enumerate(grp):
                sl = cnt["wgu"] % 4
                cnt["wgu"] += 1
                S.dma("pool", "wg%d" % sl, WGU[:, sl, 0, :, :],
                      wg_d[l].rearrange("(c p) f -> p c f", p=128)[:, :, fc * 128:(fc + 1) * 128], writes=[wgu_buf[sl]])
                S.dma("pool", "wu%d" % sl, WGU[:, sl, 1, :, :],
                      wu_d[l].rearrange("(c p) f -> p c f", p=128)[:, :, fc * 128:(fc + 1) * 128], writes=[wgu_buf[sl]])
                S.dma("pool", "wd%d" % i, WD[:, i, :], wd_d[l][fc * 128:(fc + 1) * 128, :], writes=[wd_buf[i]])

                def evac(pg, pgb, pu, pub, tt, t0, n, i=i):
                    ss = cnt["sg"] % 2
                    cnt["sg"] += 1
                    S.op("act", lambda e, pg=pg, ss=ss, n=n: e.activation(out=SG[:, ss, 0:n], in_=pg[:, 0:n], func=AF.Silu),
                         reads=[pgb], writes=[sg_buf[ss]])
                    S.op("dve", lambda e, pu=pu, ss=ss, i=i, t0=t0, n=n: e.tensor_tensor(
                        out=ACT[:, i, t0:t0 + n], in0=pu[:, 0:n], in1=SG[:, ss, 0:n], op=ALU.mult),
                        reads=[pub, sg_buf[ss]], writes=[act_buf[i][tt]])

                for tt, (t0, n) in enumerate(TTILES[0:3]):
                    pg, pgb = bank("s")
                    pu, pub = bank("s")
                    for gi, (pp, ppb) in enumerate(((pg, pgb), (pu, pub))):
                        fns = [lambda e, pp=pp, gi=gi, c=c, sl=sl, t0=t0, n=n: e.matmul(
                            pp[:, 0:n], lhsT=WGU[:, sl, gi, c, :], rhs=H[:, c, t0:t0 + n],
                            start=(c == 0), stop=(c == NCH - 1)) for c in range(NCH)]
                        S.pe_group(fns, reads=[wgu_buf[sl], h_buf[tt]], writes=[ppb])
                    evac(pg, pgb, pu, pub, tt, t0, n)
                pg, pgb = bank("s")
                pu, pub = bank("s")
                qg, qgb = bank("a")
                qu, qub = bank("a")
                for gi, (pp, ppb, qq, qqb) in enumerate(((pg, pgb, qg, qgb), (pu, pub, qu, qub))):
                    fns = []
                    for c in range(NCH):
                        fns.append(lambda e, pp=pp, gi=gi, c=c, sl=sl: e.matmul(
                            pp[:, 0:n3], lhsT=WGU[:, sl, gi, c, :], rhs=H[:, c, t3:t3 + n3],
                            start=(c == 0), stop=(c == NCH - 1)))
                        fns.append(lambda e, qq=qq, gi=gi, c=c, sl=sl: e.matmul(
                            qq[:, 0:n4], lhsT=WGU[:, sl, gi, c, :], rhs=H[:, c, t4:t4 + n4],
                            start=(c == 0), stop=(c == NCH - 1)))
                    S.pe_group(fns, reads=[wgu_buf[sl], h_buf[3], h_buf[4]], writes=[ppb, qqb])
                evac(pg, pgb, pu, pub, 3, t3, n3)
                evac(qg, qgb, qu, qub, 4, t4, n4)
            ng = len(grp)

            def xupd(po, pob, dp, tt, t0, n):
                S.op("dve", lambda e, po=po, dp=dp, t0=t0, n=n: e.scalar_tensor_tensor(
                    out=X[:, dp, t0:t0 + n], in0=po[:, 0:n], scalar=0.5, in1=X[:, dp, t0:t0 + n],
                    op0=ALU.mult, op1=ALU.add),
                    reads=[pob, x_buf[tt][dp]], writes=[x_buf[tt][dp]])

            for tt, (t0, n) in enumerate(TTILES[0:3]):
                for dp in range(NCH):
                    po, pob = bank("a")
                    fns = [lambda e, po=po, i=i, dp=dp, t0=t0, n=n, ng=ng: e.matmul(
                        po[:, 0:n], lhsT=WD[:, i, dp * 128:(dp + 1) * 128], rhs=ACT[:, i, t0:t0 + n],
                        start=(i == 0), stop=(i == ng - 1)) for i in range(ng)]
                    S.pe_group(fns, reads=[wd_buf[i] for i in range(ng)] + [act_buf[i][tt] for i in range(ng)],
                               writes=[pob])
                    xupd(po, pob, dp, tt, t0, n)
            for dp in range(NCH):
                po, pob = bank("a")
                qo, qob = bank("a")
                fns = []
                for i in range(ng):
                    fns.append(lambda e, po=po, i=i, dp=dp, ng=ng: e.matmul(
                        po[:, 0:n3], lhsT=WD[:, i, dp * 128:(dp + 1) * 128], rhs=ACT[:, i, t3:t3 + n3],
                        start=(i == 0), stop=(i == ng - 1)))
                    fns.append(lambda e, qo=qo, i=i, dp=dp, ng=ng: e.matmul(
                        qo[:, 0:n4], lhsT=WD[:, i, dp * 128:(dp + 1) * 128], rhs=ACT[:, i, t4:t4 + n4],
                        start=(i == 0), stop=(i == ng - 1)))
                S.pe_group(fns, reads=[wd_buf[i] for i in range(ng)] + [act_buf[i][3] for i in range(ng)]
                           + [act_buf[i][4] for i in range(ng)], writes=[pob, qob])
                xupd(po, pob, dp, 3, t3, n3)
                xupd(qo, qob, dp, 4, t4, n4)

    def mm(out, lhsT, rhs, start, stop):
        return lambda e: e.matmul(out, lhsT=lhsT, rhs=rhs, start=start, stop=stop, skip_group_check=True)

    def project_pair(kind, slot, c, wqkv_d, outs):
        k_out_p, v_out_p, k_out_s, v_out_s = outs
        QT0, QT1, KT, VB, VS, v1 = MV["QT0"], MV["QT1"], MV["KT"], MV["VB"], MV["VS"], MV["v1"]
        sl = c % 2
        for tt, (t0, n) in enumerate(TTILES):
            for m in range(2):
                pb, pbb = bank("s")
                fns = [mm(pb[:, 0:n], WQKV[:, sl, m, dc, :], H[:, dc, t0:t0 + n], dc == 0, dc == NCH - 1)
                       for dc in range(NCH)]
                S.pe_group(fns, reads=[wqkv_buf[sl], h_buf[tt]], writes=[pbb])
                if m == 0 and tt == 4:
                    for hh in range(2):
                        S.op("act", lambda e, pb=pb, hh=hh: e.activation(
                            out=QS[hh * 64:(hh + 1) * 64, :, hh, :],
                            in_=pb[hh * 64:(hh + 1) * 64, 0:TS].rearrange("p (s i) -> p s i", s=NS),
                            func=AF.Copy, scale=0.125), reads=[pbb], writes=[qs_buf])
                elif m == 0:
                    S.op("act", lambda e, pb=pb, t0=t0, n=n: e.activation(out=QT0[0:64, t0:t0 + n], in_=pb[0:64, 0:n],
                                                                          func=AF.Copy, scale=0.125),
                         reads=[pbb], writes=[qt_buf])
                    S.op("act", lambda e, pb=pb, t0=t0, n=n: e.activation(out=QT1[64:128, t0:t0 + n],
                                                                          in_=pb[64:128, 0:n], func=AF.Copy, scale=0.125),
                         reads=[pbb], writes=[qt_buf])
                else:
                    S.op("dve", lambda e, pb=pb, t0=t0, n=n: e.tensor_copy(out=KT[:, t0:t0 + n], in_=pb[:, 0:n]),
                         reads=[pbb], writes=[kt_buf])
        groups = [[0, 1, 2, 3], [4, 5, 6, 7], [8, 9, 10, 11], [12, 13, 14, 15], [16]]
        for gi, grp in enumerate(groups):
            for m in (2, 1):
                if m == 1 and kind == 0 and gi < 3:
                    continue
                pb, pbb = bank("s")
                fns = []
                for jj, t in enumerate(grp):
                    rows = 128 if t < 16 else TS
                    for dc in range(NCH):
                        fns.append(mm(pb[0:rows, jj * 128:(jj + 1) * 128], H[:, dc, t * 128:t * 128 + rows],
                                      WQKV[:, sl, m, dc, :], dc == 0, dc == NCH - 1))
                rows = 128 if gi < 4 else TS
                ng = len(grp)
                S.pe_group(fns, reads=[wqkv_buf[sl], h_buf[min(gi, 4)]], writes=[pbb])
                psv = pb[0:rows, 0:ng * 128].rearrange("p (j f) -> p j f", j=ng)
                need_out = not (kind == 0 and gi < 3)
                if m == 2:
                    for hh in range(2):
                        S.op("dve", lambda e, psv=psv, rows=rows, grp=grp, ng=ng, hh=hh: e.tensor_copy(
                            out=VB[0:rows, grp[0]:grp[0] + ng, hh * v1:hh * v1 + 64],
                            in_=psv[:, :, hh * 64:(hh + 1) * 64]), reads=[pbb], writes=[vb_buf])
                if need_out:
                    key = "vst" if m == 2 else "kst"
                    stg = VST if m == 2 else KST
                    sbufs = vst_buf if m == 2 else kst_buf
                    halves = [(0, 2), (2, 2)] if gi < 4 else [(0, 1)]
                    for hi_, (j0, nj) in enumerate(halves):
                        ss = cnt[key] % 2
                        cnt[key] += 1
                        eng_ = "act" if hi_ == 0 else "dve"
                        src_ = psv[:, j0:j0 + nj, :]
                        if eng_ == "act":
                            S.op("act", lambda e, src_=src_, rows=rows, nj=nj, stg=stg, ss=ss: e.copy(
                                out=stg[0:rows, ss, 0:nj, :], in_=src_), reads=[pbb], writes=[sbufs[ss]])
                        else:
                            S.op("dve", lambda e, src_=src_, rows=rows, nj=nj, stg=stg, ss=ss: e.tensor_copy(
                                out=stg[0:rows, ss, 0:nj, :], in_=src_), reads=[pbb], writes=[sbufs[ss]])
                        if gi < 4:
                            od = v_out_p if m == 2 else k_out_p
                            r0 = (gi * 512 if kind != 0 else 0) + j0 * 128
                            dst = od[r0:r0 + 256, c * 128:(c + 1) * 128].rearrange("(j p) f -> p j f", p=128)
                            S.dma("sp", "%s%d" % (key, ss), dst, stg[:, ss, :, :], reads=[sbufs[ss]])
                        else:
                            od = v_out_s if m == 2 else k_out_s
                            S.dma("sp", "%s%d" % (key, ss), od[:, c * 128:(c + 1) * 128], stg[0:TS, ss, 0, :],
                                  reads=[sbufs[ss]])
        pb, pbb = bank("s")
        fns = []
        for s in range(NS):
            for dc in range(NCH):
                fns.append(mm(pb[0:LS, s * 128:(s + 1) * 128], H[:, dc, TP + s * LS:TP + (s + 1) * LS],
                              WQKV[:, sl, 2, dc, :], dc == 0, dc == NCH - 1))
        S.pe_group(fns, reads=[wqkv_buf[sl], h_buf[4]], writes=[pbb])
        for hh in range(2):
            S.op("dve", lambda e, pb=pb, hh=hh: e.tensor_copy(
                out=VS[0:LS, :, hh * v1:hh * v1 + 64],
                in_=pb[0:LS, :].rearrange("p (s f) -> p s f", s=NS)[:, :, hh * 64:(hh + 1) * 64]),
                reads=[pbb], writes=[vs_buf])

    def load_wqkv(c, wqkv_d):
        sl = c % 2
        for m in range(3):
            S.dma("pool", "wqkv%d_%d" % (sl, m), WQKV[:, sl, m, :, :],
                  wqkv_d.rearrange("(c p) f -> p c f", p=128)[:, :, m * D + c * 128:m * D + (c + 1) * 128],
                  writes=[wqkv_buf[sl]])

    def issue_cache_dma(s, c, ck, cv, nt, sl):
        S.dma("sp", "kcs", KCS[:, 0:nt, :], ck[s][:, c * 128:(c + 1) * 128].rearrange("(j p) f -> p j f", p=128),
              writes=[kcs_buf])
        VCC, v1 = MV["VCC"], MV["v1"]
        for hh in range(2):
            S.dma("pool", "vcc%d_%d" % (sl, hh), VCC[:, sl, 0:nt, hh * v1:hh * v1 + 64],
                  cv[s][:, c * 128 + hh * 64:c * 128 + (hh + 1) * 64].rearrange("(j p) f -> p j f", p=128),
                  writes=[vcc_buf[sl]])

    def cache_transposes(nt, sl):
        for g in range(nt // 4):
            pb, pbb = bank("s")
            fns = [lambda e, pb=pb, j=j, g=g: e.transpose(out=pb[:, j * 128:(j + 1) * 128], in_=KCS[:, g * 4 + j, :],
                                                          identity=ident_f) for j in range(4)]
            S.pe_group(fns, reads=[kcs_buf, idf_buf], writes=[pbb])
            if g % 2 == 0:
                S.op("act", lambda e, pb=pb, g=g, sl=sl: e.copy(out=KTC[:, sl, g * 512:(g + 1) * 512], in_=pb[:, :]),
                     reads=[pbb], writes=[ktc_buf[sl]])
            else:
                S.op("dve", lambda e, pb=pb, g=g, sl=sl: e.tensor_copy(out=KTC[:, sl, g * 512:(g + 1) * 512], in_=pb[:, :]),
                     reads=[pbb], writes=[ktc_buf[sl]])

    def load_cache_pair(s, c, ck, cv, nt, sl):
        issue_cache_dma(s, c, ck, cv, nt, sl)
        cache_transposes(nt, sl)

    def run_softmax_units(units, LA=2):
        pend = []
        deferred = []

        def do_pv(item):
            u, slot = item
            nk, n = u["nk"], u["n"]
            c0 = u["ocol"]
            if "pv" in u:
                fns = [mm(u[key][0][:, c0:c0 + LS], vv, PT[0:nk, slot, hh * LS:(hh + 1) * LS], u["first"], u["last"])
                       for hh, (key, vv) in enumerate(u["pv"])]
                S.pe_group(fns, reads=[pt_buf[slot]] + u["v_reads"], writes=[u["o"][1], u["d"][1]])
            else:
                ob, obuf = u["o"] if u["hh"] == 0 else u["d"]
                fns = [mm(ob[:, c0:c0 + n], u["v"], PT[0:nk, slot, 0:n], u["first"], u["last"])]
                S.pe_group(fns, reads=[pt_buf[slot]] + u["v_reads"], writes=[obuf])
            if u.get("fin") is not None:
                fb = u["fin"]()
                if fb is not None:
                    deferred.append([2, fb])
            if u.get("post") is not None:
                u["post"]()

        for u in units:
            for d in deferred:
                d[0] -= 1
            while deferred and deferred[0][0] <= 0:
                deferred.pop(0)[1]()
            if u.get("pre") is not None:
                u["pre"]()
            nk, n = u["nk"], u["n"]
            sb, sbb = bank("s")
            nx = len(u["extras"])
            fns = [mm(sb[0:nk, 0:n], u["kT"], u["q"], True, nx == 0)]
            for xi, (xl, xr, xo, xn) in enumerate(u["extras"]):
                fns.append(mm(sb[0:nk, xo:xo + xn], xl, xr, False, xi == nx - 1))
            S.pe_group(fns, reads=u["qk_reads"] + [cb_buf], writes=[sbb])
            slot = cnt["pt"] % 4
            cnt["pt"] += 1
            bias = u.get("bias")
            if "bias2" in u:
                for hh, bb in enumerate(u["bias2"]):
                    S.op("act", lambda e, sb=sb, nk=nk, slot=slot, hh=hh, bb=bb: e.activation(
                        out=PT[0:nk, slot, hh * LS:(hh + 1) * LS], in_=sb[0:nk, hh * LS:(hh + 1) * LS], func=AF.Exp,
                        bias=bb), reads=[sbb, nb_buf], writes=[pt_buf[slot]])
            elif bias is None:
                S.op("act", lambda e, sb=sb, nk=nk, n=n, slot=slot: e.activation(
                    out=PT[0:nk, slot, 0:n], in_=sb[0:nk, 0:n], func=AF.Exp), reads=[sbb], writes=[pt_buf[slot]])
            else:
                S.op("act", lambda e, sb=sb, nk=nk, n=n, slot=slot, bias=bias: e.activation(
                    out=PT[0:nk, slot, 0:n], in_=sb[0:nk, 0:n], func=AF.Exp, bias=bias),
                    reads=[sbb, nb_buf], writes=[pt_buf[slot]])
            pend.append((u, slot))
            if len(pend) > LA:
                do_pv(pend.pop(0))
        while pend:
            do_pv(pend.pop(0))
        while deferred:
            deferred.pop(0)[1]()

    def softmax_fin(b0, b0buf, b1, b1buf, c, col0, n, tts):
        def fin():
            rs = cnt["rcp"] % 2
            cnt["rcp"] += 1
            slot = cnt["pt"] % 4
            cnt["pt"] += 1
            S.op("act", lambda e: e.copy(out=RCP[0:64, rs, 0:n], in_=b1[0:64, 0:n]), reads=[b1buf], writes=[rcp_buf[rs]])
            S.op("act", lambda e: e.copy(out=RCP[64:128, rs, 0:n], in_=b0[64:128, 0:n]), reads=[b0buf],
                 writes=[rcp_buf[rs]])
            S.op("dve", lambda e: e.tensor_copy(out=PT[64:128, slot, 0:n], in_=b1[64:128, 0:n]), reads=[b1buf],
                 writes=[pt_buf[slot]])
            S.op("dve", lambda e: e.reciprocal(out=RCP[:, rs, 0:n], in_=RCP[:, rs, 0:n]), reads=[rcp_buf[rs]],
                 writes=[rcp_buf[rs]])
            S.op("act", lambda e: e.copy(out=PT[0:64, slot, 0:n], in_=b0[0:64, 0:n]), reads=[b0buf], writes=[pt_buf[slot]])


            def fin_b():
                xs, xsb = bank("s")
                S.pe_group([mm(xs[:, 0:n], swap_b, PT[:, slot, 0:n], True, True)], reads=[pt_buf[slot], cb_buf],
                           writes=[xsb])
                S.op("dve", lambda e: e.tensor_tensor(out=OT[:, c, col0:col0 + n], in0=xs[:, 0:n], in1=RCP[:, rs, 0:n],
                                                      op=ALU.mult),
                     reads=[xsb, rcp_buf[rs]], writes=[ot_buf[c][t] for t in tts])
            return fin_b
        return fin

    def run_stick_units(units):
        stA = []
        stB = []
        acc_state = {}

        def do_cs(item):
            u, sb, sbb, lslot, accprev = item
            nk, n, sc = u["nk"], u["n"], u["scol"]
            fns = [mm(sb[0:nk, sc:sc + n], ntri_b[0:nk, 0:nk], LB[0:nk, lslot, sc:sc + n], False, accprev is None)]
            rd = [lb_buf[lslot], cb_buf]
            if accprev is not None:
                aslot, ank = accprev
                fns.append(mm(sb[0:nk, sc:sc + n], nones_b[0:ank, 0:nk], ACC[0:ank, aslot, sc:sc + n], False, True))
                rd.append(acc_buf[aslot])
            S.pe_group(fns, reads=rd, writes=[sbb])
            slot = cnt["pt"] % 4
            cnt["pt"] += 1
            S.op("act", lambda e: e.activation(out=PT[0:nk, slot, sc:sc + n], in_=sb[0:nk, sc:sc + n], func=AF.Exp),
                 reads=[sbb], writes=[pt_buf[slot]])
            stB.append((u, slot))

        def do_pv(item):
            u, slot = item
            nk, n, sc = u["nk"], u["n"], u["scol"]
            ob, obuf = u["o"]
            pb0, c0 = u["pbase"], u["ocol"]
            if "pv" in u:
                fns = [mm(ob[hh * 64:(hh + 1) * 64, c0:c0 + LS], vv, PT[0:nk, slot, hh * LS:(hh + 1) * LS],
                          u["first"], u["last"]) for hh, (key, vv) in enumerate(u["pv"])]
            else:
                fns = [mm(ob[pb0:pb0 + 64, c0:c0 + n], u["v"], PT[0:nk, slot, sc:sc + n], u["first"], u["last"])]
            S.pe_group(fns, reads=[pt_buf[slot]] + u["v_reads"], writes=[obuf])
            if u.get("fin") is not None:
                u["fin"]()
            if u.get("post") is not None:
                u["post"]()

        for u in units:
            if u.get("pre") is not None:
                u["pre"]()
            nk, n, sc = u["nk"], u["n"], u["scol"]
            sb, sbb = bank("s")
            fns = [mm(sb[0:nk, sc:sc + n], u["kT"], u["q"], True, False)]
            for xi, (xl, xr, xo, xn) in enumerate(u["extras"]):
                fns.append(mm(sb[0:nk, sc + xo:sc + xo + xn], xl, xr, False, False))
            S.pe_group(fns, reads=u["qk_reads"] + [cb_buf], writes=[sbb])
            es = cnt["ef"] % 2
            cnt["ef"] += 1
            S.op("act", lambda e, sb=sb, nk=nk, n=n, sc=sc, es=es: e.activation(
                out=EF[0:nk, es, sc:sc + n], in_=sb[0:nk, sc:sc + n], func=AF.Exp), reads=[sbb], writes=[ef_buf[es]])
            ls = cnt["lb"] % 2
            cnt["lb"] += 1
            S.op("act", lambda e, nk=nk, n=n, sc=sc, es=es, ls=ls: e.activation(
                out=LB[0:nk, ls, sc:sc + n], in_=EF[0:nk, es, sc:sc + n], func=AF.Ln, bias=1.0),
                reads=[ef_buf[es]], writes=[lb_buf[ls]])
            sid = u["seq"]
            prev = None if u["seq_first"] else acc_state[sid]
            stA.append((u, sb, sbb, ls, prev))
            if not u["seq_last"]:
                base = 2 * u["hh"]
                W = sc + n
                rot = None
                if "pv" in u:
                    rot = cnt["accs"] % 4
                    cnt["accs"] += 1
                if prev is None:
                    ns_ = base if rot is None else rot
                    S.op("dve", lambda e, nk=nk, n=n, sc=sc, ls=ls, ns_=ns_: e.tensor_copy(
                        out=ACC[0:nk, ns_, sc:sc + n], in_=LB[0:nk, ls, sc:sc + n]),
                        reads=[lb_buf[ls]], writes=[acc_buf[ns_]])
                    acc_state[sid] = (ns_, nk)
                else:
                    pslot, pnk = prev
                    ns_ = base + (1 - (pslot - base)) if rot is None else rot
                    if pnk < nk:
                        S.op("dve", lambda e, nk=nk, n=n, sc=sc, ls=ls, ns_=ns_: e.tensor_copy(
                            out=ACC[0:nk, ns_, sc:sc + n], in_=LB[0:nk, ls, sc:sc + n]),
                            reads=[lb_buf[ls]], writes=[acc_buf[ns_]])
                        S.op("dve", lambda e, pnk=pnk, n=n, sc=sc, ns_=ns_, pslot=pslot: e.tensor_tensor(
                            out=ACC[0:pnk, ns_, sc:sc + n], in0=ACC[0:pnk, ns_, sc:sc + n],
                            in1=ACC[0:pnk, pslot, sc:sc + n], op=ALU.add),
                            reads=[acc_buf[ns_], acc_buf[pslot]], writes=[acc_buf[ns_]])
                    else:
                        S.op("dve", lambda e, nk=nk, n=n, sc=sc, ls=ls, ns_=ns_, pslot=pslot: e.tensor_tensor(
                            out=ACC[0:nk, ns_, sc:sc + n], in0=ACC[0:nk, pslot, sc:sc + n], in1=LB[0:nk, ls, sc:sc + n],
                            op=ALU.add),
                            reads=[acc_buf[pslot], lb_buf[ls]], writes=[acc_buf[ns_]])
                    acc_state[sid] = (ns_, max(nk, pnk))
                if sc > 0:
                    S.op("dve", lambda e, sc=sc, ns_=ns_: e.memset(ACC[:, ns_, 0:sc], 0.0), writes=[acc_buf[ns_]])
            if len(stA) > 1:
                do_cs(stA.pop(0))
            if len(stB) > 1:
                do_pv(stB.pop(0))
        while stA:
            do_cs(stA.pop(0))
        while stB:
            do_pv(stB.pop(0))

    def copy_fin(ob, obuf, c, col0, n, tts):
        def fin():
            S.op("dve", lambda e: e.tensor_copy(out=OT[:, c, col0:col0 + n], in_=ob[:, 0:n]),
                 reads=[obuf], writes=[ot_buf[c][t] for t in tts])
        return fin

    def prep_A(slot):
        RT = tview(0, 320, F32, "p (j h) -> p j h", j=5)
        TBT = tview(512, 512 + 513 * 4, F32)
        EE = tview(4096, 4096 + 3072, F32)
        S.dma("sp", "prep", RT[:, 0:4, :], a_rel[slot][0:512, :].rearrange("(j p) h -> p j h", p=128), writes=[tr_buf])
        S.dma("sp", "prep", RT[0:1, 4, :], a_rel[slot][512:513, :], writes=[tr_buf])
        pb, pbb = bank("s")
        pb2, pbb2 = bank("s")
        fns = [lambda e, j=j: e.transpose(out=pb[0:16, j * 128:(j + 1) * 128], in_=RT[:, j, :], identity=ident_f)
               for j in range(4)]
        fns.append(lambda e: e.transpose(out=pb2[0:16, 0:1], in_=RT[0:1, 4, :], identity=ident_f[0:1, 0:1]))
        S.pe_group(fns, reads=[tr_buf, idf_buf], writes=[pbb, pbb2])
        S.op("dve", lambda e: e.tensor_copy(out=TBT[0:16, 0:512], in_=pb[0:16, :]), reads=[pbb], writes=[tr_buf])
        S.op("dve", lambda e: e.tensor_copy(out=TBT[0:16, 512:513], in_=pb2[0:16, 0:1]), reads=[pbb2], writes=[tr_buf])
        S.op("dve", lambda e: e.tensor_copy(out=EE[0:16, 0:384], in_=TBT[0:16, 129:513]), reads=[tr_buf], writes=[tr_buf])
        S.op("dve", lambda e: e.tensor_copy(out=EE[0:16, 384:768], in_=TBT[0:16, 512:513].broadcast_to([16, 384])),
             reads=[tr_buf], writes=[tr_buf])
        S.dma("sp", "prep", bass.AP(EH, 0, [[768, 16], [1, 768]]), EE[0:16, :], reads=[tr_buf], writes=[tr_buf])

    def load_bias_A(c):
        BIAS, BS = MV["BIAS"], MV["BS"]
        for hh in range(2):
            h = 2 * c + hh
            S.dma("pool", "biasA", BIAS[:, hh, :], bass.AP(EH, h * 768, [[1, 128], [1, 640]]),
                  reads=[tr_buf], writes=[bias_buf])
        S.op("dve", lambda e: e.memset(BIAS[64:128, :, 576:640], NEG), writes=[bias_buf])
        S.op("dve", lambda e: e.memset(BIAS[0:64, :, 0:64], NEG), writes=[bias_buf])
        for hh in range(2):
            h = 2 * c + hh
            for j in range(5):
                rows = 128 if j < 4 else LS
                S.dma("pool", "biasS", BS[0:rows, j, hh, :],
                      bass.AP(EH, h * 768 + (512 - 128 * j if j < 4 else 112), [[1, rows], [1, LS]]),
                      reads=[tr_buf], writes=[bs_buf])

    def prep_B(slot):
        NBP, NBS = MV["NBP"], MV["NBS"]
        WF = tview(0, 256, BF16, "p (c h) -> p c h", c=NCH)
        BF_ = tview(256, 320, F32)
        ZF = tview(512, 512 + 1088, F32, "p (t h) -> p t h", t=17)
        LFN = tview(2048, 2048 + 1088, F32, "p (t h) -> p t h", t=17)
        LFT = tview(4096, 4096 + TT * 4, F32)
        TMPN = tview(4096 + 8448, 4096 + 8448 + 8192, F32)
        CL = tview(20992, 20992 + 512, F32, "p (j h) -> p j h", j=8)
        LFS = tview(21504, 21504 + 4160, F32)
        TMPS = tview(25664, 25664 + 4160, F32)
        S.dma("pool", "prepw", WF[:, :, :], b_wf[slot].rearrange("(c p) h -> p c h", p=128), writes=[tr_buf])
        S.dma("sp", "prep", BF_[:, :], bass.AP(b_bf.tensor, slot * NH, [[0, 128], [1, NH]]), writes=[tr_buf])
        pb, pbb = bank("s")
        fns = []
        for t in range(17):
            rows = 128 if t < 16 else TS
            for dc in range(NCH):
                fns.append(mm(pb[0:rows, t * 16:(t + 1) * 16], H[:, dc, t * 128:t * 128 + rows], WF[:, dc, :],
                              dc == 0, dc == NCH - 1))
        S.pe_group(fns, reads=[tr_buf] + h_buf, writes=[pbb])
        psv = pb[:, 0:272].rearrange("p (t h) -> p t h", t=17)
        S.op("dve", lambda e: e.tensor_tensor(out=ZF[:, :, :], in0=psv, in1=BF_[:, :].unsqueeze(1).broadcast_to([128, 17, 16]),
                                              op=ALU.add), reads=[pbb, tr_buf], writes=[tr_buf])
        S.op("act", lambda e: e.activation(out=ZF[:, :, :], in_=ZF[:, :, :], func=AF.Exp, scale=-1.0),
             reads=[tr_buf], writes=[tr_buf])
        S.op("act", lambda e: e.activation(out=ZF[:, :, :], in_=ZF[:, :, :], func=AF.Ln, bias=1.0),
             reads=[tr_buf], writes=[tr_buf])
        S.op("dve", lambda e: e.tensor_scalar(out=LFN[:, :, :], in0=ZF[:, :, :], scalar1=-1.0, scalar2=None, op0=ALU.mult),
             reads=[tr_buf], writes=[tr_buf])
        S.dma("sp", "prep", o_b_fp.rearrange("(t p) h -> p t h", p=128), LFN[:, 0:16, :], reads=[tr_buf])
        S.dma("sp", "prep", o_b_fs, LFN[0:TS, 16, :], reads=[tr_buf])
        for g in range(5):
            pb, pbb = bank("s")
            tl = list(range(g * 4, min(g * 4 + 4, 17)))
            fns = []
            for jj, t in enumerate(tl):
                rows = 128 if t < 16 else TS
                fns.append(lambda e, pb=pb, jj=jj, t=t, rows=rows: e.transpose(
                    out=pb[0:16, jj * 128:jj * 128 + rows], in_=LFN[0:rows, t, :], identity=ident_f[0:rows, 0:rows]))
            S.pe_group(fns, reads=[tr_buf, idf_buf], writes=[pbb])
            ncol = 512 if g < 4 else TS
            S.op("dve", lambda e, pb=pb, g=g, ncol=ncol: e.tensor_copy(out=LFT[0:16, g * 512:g * 512 + ncol],
                                                                        in_=pb[0:16, 0:ncol]), reads=[pbb], writes=[tr_buf])
        S.op("dve", lambda e: e.tensor_tensor_scan(out=LFT[0:16, 0:TP], data0=LFT[0:16, 0:TP], data1=LFT[0:16, 0:TP],
                                                   initial=0.0, op0=ALU.add, op1=ALU.bypass),
             reads=[tr_buf], writes=[tr_buf])
        for Q in range(4):
            nq = (4 * Q + 4) * 128
            ref = 512 * Q + 511
            S.op("dve", lambda e, nq=nq, ref=ref: e.tensor_scalar(out=TMPN[0:16, 0:nq], in0=LFT[0:16, 0:nq],
                                                                  scalar1=LFT[0:16, ref:ref + 1], scalar2=-1.0,
                                                                  op0=ALU.subtract, op1=ALU.mult),
                 reads=[tr_buf], writes=[tr_buf])
            pb, pbb = bank("s")
            nt = 4 * Q + 4
            fns = [lambda e, pb=pb, j=j: e.transpose(out=pb[:, j * 16:(j + 1) * 16], in_=TMPN[0:16, j * 128:(j + 1) * 128],
                                                     identity=ident_f[0:16, 0:16]) for j in range(nt)]
            S.pe_group(fns, reads=[tr_buf, idf_buf], writes=[pbb])
            base = nbp_base(Q)
            S.op("dve", lambda e, pb=pb, nt=nt, base=base: e.tensor_copy(
                out=NBP[:, base:base + nt, :], in_=pb[:, 0:nt * 16].rearrange("p (t h) -> p t h", t=nt)),
                reads=[pbb], writes=[nb_buf])
        for s in range(NS):
            S.dma("sp", "prep", CL[:, :, :], cb_f[s].rearrange("(j p) h -> p j h", p=128), writes=[tr_buf])
            for g in range(2):
                pb, pbb = bank("s")
                fns = [lambda e, pb=pb, j=j, g=g: e.transpose(out=pb[0:16, j * 128:(j + 1) * 128], in_=CL[:, g * 4 + j, :],
                                                              identity=ident_f) for j in range(4)]
                S.pe_group(fns, reads=[tr_buf, idf_buf], writes=[pbb])
                S.op("dve", lambda e, pb=pb, g=g: e.tensor_copy(out=LFS[0:16, g * 512:(g + 1) * 512], in_=pb[0:16, :]),
                     reads=[pbb], writes=[tr_buf])
            S.op("dve", lambda e, s=s: e.tensor_copy(out=LFS[0:16, 1024:1040], in_=LFT[0:16, TP + s * LS:TP + (s + 1) * LS]),
                 reads=[tr_buf], writes=[tr_buf])
            S.op("dve", lambda e: e.tensor_tensor_scan(out=LFS[0:16, 0:1040], data0=LFS[0:16, 0:1040],
                                                       data1=LFS[0:16, 0:1040], initial=0.0, op0=ALU.add, op1=ALU.bypass),
                 reads=[tr_buf], writes=[tr_buf])
            S.op("dve", lambda e: e.tensor_scalar(out=TMPS[0:16, 0:1040], in0=LFS[0:16, 0:1040],
                                                  scalar1=LFS[0:16, 1039:1040], scalar2=-1.0,
                                                  op0=ALU.subtract, op1=ALU.mult), reads=[tr_buf], writes=[tr_buf])
            pb, pbb = bank("s")
            fns = [lambda e, pb=pb, j=j: e.transpose(out=pb[:, j * 16:(j + 1) * 16], in_=TMPS[0:16, j * 128:(j + 1) * 128],
                                                     identity=ident_f[0:16, 0:16]) for j in range(8)]
            fns.append(lambda e, pb=pb: e.transpose(out=pb[0:16, 128:144], in_=TMPS[0:16, 1024:1040],
                                                    identity=ident_f[0:16, 0:16]))
            S.pe_group(fns, reads=[tr_buf, idf_buf], writes=[pbb])
            S.op("dve", lambda e, pb=pb, s=s: e.tensor_copy(out=NBS[:, s, :, :],
                                                            in_=pb[:, 0:144].rearrange("p (t h) -> p t h", t=9)),
                 reads=[pbb], writes=[nb_buf])

    def nbp_base(Q):
        return sum(4 * q + 4 for q in range(Q))

    def prompt_units(kind, c):
        QT0, QT1, KT, VB, v1 = MV["QT0"], MV["QT1"], MV["KT"], MV["VB"], MV["v1"]
        BIAS, NBP = MV.get("BIAS"), MV.get("NBP")
        units = []
        for Q in range(4):
            ob, obuf = bank("a")
            if kind != 2:
                dbk, dbuf = bank("a")
            if kind == 0:
                jl = list(range(max(0, 4 * Q - 4), 4 * Q + 4))
            elif kind == 1:
                jl = list(range(0, 4 * Q + 4))
            else:
                jl = list(range(4 * Q + 3, -1, -1))
            for ji, j in enumerate(jl):
                for hh in range(2):
                    h = 2 * c + hh
                    pb0 = 64 * hh
                    if kind == 0:
                        lo = max(0, 128 * j - 512 * Q)
                        hi = min(512, 128 * j + 640 - 512 * Q)
                    else:
                        lo = max(0, 128 * j - 512 * Q)
                        hi = 512
                    n = hi - lo
                    q0 = 512 * Q + lo
                    u = dict(nk=128, n=n, kT=KT[:, j * 128:(j + 1) * 128], q=(QT0, QT1)[hh][:, q0:q0 + n],
                             qk_reads=[kt_buf, qt_buf], extras=[],
                             v=(VB[:, j, hh * 64:(hh + 1) * 64] if kind == 2 else VB[:, j, hh * 64:hh * 64 + 128]),
                             v_reads=[vb_buf],
                             pbase=pb0, ocol=lo, scol=lo, first=(ji == 0), last=(ji == len(jl) - 1), hh=hh)
                    if kind == 0:
                        b0 = q0 - 128 * j
                        u["extras"].append((anti_b, BIAS[:, hh, b0:b0 + n], 0, n))
                        u["qk_reads"] = [kt_buf, qt_buf, bias_buf]
                    else:
                        r = j - 4 * Q
                        if r >= 0:
                            u["extras"].append((ident_b, mask_le if kind == 1 else mask_lt, 0, 128))
                    if kind == 1:
                        u["bias"] = NBP[:, nbp_base(Q) + j, h:h + 1]
                    u["o"] = (ob, obuf)
                    if kind != 2:
                        u["d"] = (dbk, dbuf)
                    else:
                        u["seq"] = ("p", Q, hh)
                        u["seq_first"] = (ji == 0)
                        u["seq_last"] = (ji == len(jl) - 1)
                    if ji == len(jl) - 1 and hh == 1:
                        if kind != 2:
                            u["fin"] = softmax_fin(ob, obuf, dbk, dbuf, c, 512 * Q, 512, [Q])
                        else:
                            u["fin"] = copy_fin(ob, obuf, c, 512 * Q, 512, [Q])
                    units.append(u)
        return units

    def sample_units(kind, c, s, ksl, nt, ob, obuf, dbk, dbuf, is_last_seq):
        KT, VS, VCC = MV["KT"], MV["VS"], MV["VCC"]
        BS, NBS = MV.get("BS"), MV.get("NBS")
        units = []
        tl = list(range(nt)) + ["new"]
        if kind == 2:
            tl = ["new"] + list(range(nt - 1, -1, -1))
        q0 = TP + s * LS
        for ji, j in enumerate(tl):
            if j == "new":
                nk = LS
                kT = KT[:, q0:q0 + LS]
                qk_reads = [kt_buf, qs_buf]
                v_reads = [vs_buf]
                vsrc = lambda a, b: VS[0:LS, s, a:b]
            else:
                nk = 128
                kT = KTC[:, ksl, j * 128:(j + 1) * 128]
                qk_reads = [ktc_buf[ksl], qs_buf]
                v_reads = [vcc_buf[ksl]]
                vsrc = lambda a, b, j=j: VCC[:, ksl, j, a:b]
            if kind == 2:
                pv = [("o", vsrc(0, 64)), ("o", vsrc(64, 128))]
            else:
                pv = [("o", vsrc(0, 128)), ("d", vsrc(64, 192))]
            u = dict(nk=nk, n=2 * LS, kT=kT, q=QS[:, s, :, :].rearrange("p h i -> p (h i)"), qk_reads=qk_reads, extras=[],
                     v=None, v_reads=v_reads, pbase=0, ocol=s * LS, scol=0, first=(ji == 0), last=(ji == len(tl) - 1),
                     hh=0, pv=pv)
            jidx = nt if j == "new" else j
            if kind == 0:
                ja = 4 if j == "new" else j
                u["extras"].append(((anti_b if nk == 128 else anti16_b[0:LS, 0:LS]),
                                    BS[0:nk, ja, :, :].rearrange("p h i -> p (h i)"), 0, 2 * LS))
                u["qk_reads"] = qk_reads + [bs_buf]
            elif j == "new":
                mk = (mask_le if kind == 1 else mask_lt)[0:LS, 0:LS]
                u["extras"].append((ident_b[0:LS, 0:LS], mk, 0, LS))
                u["extras"].append((ident_b[0:LS, 0:LS], mk, LS, LS))
            if kind == 1:
                u["bias2"] = [NBS[0:nk, s, jidx, 2 * c + hh:2 * c + hh + 1] for hh in range(2)]
            u["o"] = (ob, obuf)
            if kind != 2:
                u["d"] = (dbk, dbuf)
            else:
                u["seq"] = ("s", s)
                u["seq_first"] = (ji == 0)
                u["seq_last"] = (ji == len(tl) - 1)
            if is_last_seq and ji == len(tl) - 1:
                if kind != 2:
                    u["fin"] = softmax_fin(ob, obuf, dbk, dbuf, c, TP, TS, [4])
                else:
                    u["fin"] = copy_fin(ob, obuf, c, TP, TS, [4])
            units.append(u)
        return units

    def oproj(wo_d, swapped):
        WO = MV["WO"]
        wsrc = wo_d.rearrange("(c p) f -> p c f", p=128)
        for dp in range(NCH):
            sl = cnt["wo"] % 2
            cnt["wo"] += 1
            if swapped:
                S.dma("pool", "wo%d_a" % sl, WO[0:64, sl, :, :], wsrc[64:128, :, dp * 128:(dp + 1) * 128],
                      writes=[wo_buf[sl]])
                S.dma("pool", "wo%d_b" % sl, WO[64:128, sl, :, :], wsrc[0:64, :, dp * 128:(dp + 1) * 128],
                      writes=[wo_buf[sl]])
            else:
                S.dma("pool", "wo%d_a" % sl, WO[:, sl, :, :], wsrc[:, :, dp * 128:(dp + 1) * 128], writes=[wo_buf[sl]])
            for tt, (t0, n) in enumerate(TTILES):
                po, pob = bank("a")
                fns = [mm(po[:, 0:n], WO[:, sl, c, :], OT[:, c, t0:t0 + n], c == 0, c == NCH - 1) for c in range(NCH)]
                S.pe_group(fns, reads=[wo_buf[sl]] + [ot_buf[c][tt] for c in range(NCH)], writes=[pob])
                S.op("dve", lambda e, po=po, dp=dp, t0=t0, n=n: e.tensor_tensor(
                    out=X[:, dp, t0:t0 + n], in0=po[:, 0:n], in1=X[:, dp, t0:t0 + n], op=ALU.add),
                    reads=[pob, x_buf[tt][dp]], writes=[x_buf[tt][dp]])

    def mixer(l):
        kind, slot = l % 3, l // 3
        norm(1, l, H, h_buf)
        S.barrier()
        set_views(kind)
        QT0, QT1 = MV["QT0"], MV["QT1"]
        if kind == 0:
            wqkv_d, wo_d = a_qkv[slot], a_o[slot]
            outs = (o_a_kp[slot], o_a_vp[slot], o_a_ks[slot], o_a_vs[slot])
            ck, cv, nt = ca_k[slot], ca_v[slot], 4
            prep_A(slot)
        elif kind == 1:
            wqkv_d, wo_d = b_qkv[slot], b_o[slot]
            outs = (o_b_kp, o_b_vp, o_b_ks, o_b_vs)
            ck, cv, nt = cb_k, cb_v, 8
            prep_B(slot)
        else:
            wqkv_d, wo_d = c_qkv[slot], c_o[slot]
            outs = (o_c_kp, o_c_vp, o_c_ks, o_c_vs)
            ck, cv, nt = cc_k, cc_v, 8
        S.barrier()
        S.op("dve", lambda e: e.memset(QS[:, :, :, :], 0.0), writes=[qs_buf])
        S.op("dve", lambda e: e.memset(QT0[64:128, :], 0.0), writes=[qt_buf])
        S.op("dve", lambda e: e.memset(QT1[0:64, :], 0.0), writes=[qt_buf])
        if kind != 2:
            VB_, VS_, VCC_ = MV["VB"], MV["VS"], MV["VCC"]
            S.op("dve", lambda e: e.memset(VB_[:, :, 64:128], 1.0), writes=[vb_buf])
            S.op("dve", lambda e: e.memset(VS_[:, :, 64:128], 1.0), writes=[vs_buf])
            S.op("dve", lambda e: e.memset(VCC_[:, :, :, 64:128], 1.0), writes=vcc_buf)
        load_wqkv(0, wqkv_d)
        for c in range(NCH):
            if "no_sample" not in DBG:
                issue_cache_dma(0, c, ck, cv, nt, 0)
            project_pair(kind, slot, c, wqkv_d, outs)
            if c + 1 < NCH:
                load_wqkv(c + 1, wqkv_d)
            if kind == 0 and "no_bias" not in DBG:
                load_bias_A(c)
            units = prompt_units(kind, c)
            if "no_bias" in DBG:
                for u in units:
                    u["extras"] = [x for x in u["extras"] if x[0] is not anti_b]
                    u["qk_reads"] = [kt_buf, qt_buf]
            ob, obuf = bank("a")
            dbk = dbuf = None
            if kind != 2:
                dbk, dbuf = bank("a")
            su = []
            for s in range(NS):
                su.append(sample_units(kind, c, s, s % 2, nt, ob, obuf, dbk, dbuf, s == NS - 1))

            def mk_load(s, c=c):
                return lambda: load_cache_pair(s, c, ck, cv, nt, s % 2)

            def mk_pre(c=c):
                def pre():
                    cache_transposes(nt, 0)
                    load_cache_pair(1, c, ck, cv, nt, 1)
                return pre
            if "no_sample" not in DBG:
                units[0]["pre"] = mk_pre()
                for s in range(NS - 2):
                    su[s][-1]["post"] = mk_load(s + 2)
                for s in range(NS):
                    units += su[s]
            if "no_attn" in DBG:
                units = []
            if kind != 2:
                run_softmax_units(units)
            else:
                run_stick_units(units)
        oproj(wo_d, kind != 2)
        S.barrier()

    yt_buf = [Buf("yt%d" % t) for t in range(NTT)]

    def final_out():
        S.barrier()
        norm(3, 0, YT, yt_buf)
        S.barrier()
        ntile = TP // 128 + 1
        for t in range(ntile):
            rows = 128 if t < TP // 128 else TS
            dstd = y_p[t * 128:(t + 1) * 128, :] if t < TP // 128 else y_s[:, :]
            tt = min(t // 4, 4)
            sl = cnt["xin"] % 4
            cnt["xin"] += 1
            for half in range(2):
                pb, pbb = bank("s")
                fns = []
                for j in range(4):
                    c = half * 4 + j
                    fns.append(lambda e, pb=pb, j=j, c=c, t=t, rows=rows: e.transpose(
                        out=pb[0:rows, j * 128:(j + 1) * 128], in_=YT[:, c, t * 128:t * 128 + rows],
                        identity=ident_f))
                S.pe_group(fns, reads=[yt_buf[tt], idf_buf], writes=[pbb])
                dst = XIN[0:rows, sl, half * 512:(half + 1) * 512]
                if half == 0:
                    S.op("act", lambda e, dst=dst, pb=pb, rows=rows: e.copy(out=dst, in_=pb[0:rows, :]),
                         reads=[pbb], writes=[xin_buf[sl]])
                else:
                    S.op("dve", lambda e, dst=dst, pb=pb, rows=rows: e.tensor_copy(out=dst, in_=pb[0:rows, :]),
                         reads=[pbb], writes=[xin_buf[sl]])
            S.dma("sp", "xin%d" % sl, dstd, XIN[0:rows, sl, :], reads=[xin_buf[sl]])

    load_x()
    S.barrier()
    done = False
    for l in range(DEPTH):
        S.new_epoch()
        for stage in ("ffn1", "mix", "ffn2"):
            if stage == "ffn1":
                ffn(l, 1)
            elif stage == "mix":
                mixer(l)
            else:
                ffn(l, 2)
            if stop_after == (l, stage):
                done = True
                break
        if done:
            break
    S.new_epoch()
    final_out()
    S.final_wait("sp")
    S.replay()
    st.close()
    return nc


W_NAMES = ["norm_ffn1", "norm_mix", "norm_ffn2", "ffn1_gate", "ffn1_up", "ffn1_down", "ffn2_gate", "ffn2_up",
           "ffn2_down", "a_w_qkv", "a_w_o", "a_rel_bias", "b_w_qkv", "b_w_o", "b_w_f", "b_b_f", "c_w_qkv", "c_w_o"]

STOP_AFTER = None
DBG = set()


def kernel(**inputs):
    n = 8
    f = lambda a: np.ascontiguousarray(np.asarray(a, dtype=np.float32))
    nc = build(STOP_AFTER)
    cf, cb = _consts_np()
    shared = {k: f(inputs[k]) for k in W_NAMES if not ("skip_ffn" in DBG and k.startswith("ffn"))}
    shared["norm_final"] = f(inputs["norm_final"]).reshape(1, D)
    shared["consts_f"] = cf
    shared["consts_b"] = cb
    in_maps = []
    for c in range(n):
        m = dict(shared)
        sl = slice(NS * c, NS * (c + 1))
        m["x_p"] = f(inputs["x_prompt"][c])
        m["x_s"] = f(inputs["x_sample"][sl]).reshape(TS, D)
        m["ca_k"] = f(inputs["cache_a_k"][:, sl]).reshape(2, NS, 512, D)
        m["ca_v"] = f(inputs["cache_a_v"][:, sl]).reshape(2, NS, 512, D)
        m["cb_k"] = f(inputs["cache_b_k"][0, sl]).reshape(NS, 1024, D)
        m["cb_v"] = f(inputs["cache_b_v"][0, sl]).reshape(NS, 1024, D)
        m["cb_f"] = f(inputs["cache_b_logf"][0, sl])
        m["cc_k"] = f(inputs["cache_c_k"][0, sl]).reshape(NS, 1024, D)
        m["cc_v"] = f(inputs["cache_c_v"][0, sl]).reshape(NS, 1024, D)
        in_maps.append(m)
    res = run_bass_kernel_spmd(nc, in_maps, core_ids=list(range(n)))
    R = res.results

    def gp(name, shape):
        return np.stack([R[c][name].reshape(shape) for c in range(n)], axis=0)

    y_prompt = gp("y_p", (TP, D))
    y_sample = gp("y_s", (NS, LS, D)).reshape(n * NS, LS, D)
    a_kp = gp("a_kp", (2, 512, NH, HD)).transpose(1, 0, 2, 3, 4)
    a_vp = gp("a_vp", (2, 512, NH, HD)).transpose(1, 0, 2, 3, 4)
    a_ks = gp("a_ks", (2, NS, LS, NH, HD)).transpose(1, 0, 2, 3, 4, 5).reshape(2, n * NS, LS, NH, HD)
    a_vs = gp("a_vs", (2, NS, LS, NH, HD)).transpose(1, 0, 2, 3, 4, 5).reshape(2, n * NS, LS, NH, HD)

    def one_p(name, last):
        return gp(name, (TP,) + last)[None]

    def one_s(name, last):
        return gp(name, (NS, LS) + last).reshape((1, n * NS, LS) + last)

    outs = (y_prompt, y_sample, a_kp, a_vp, a_ks, a_vs,
            one_p("b_kp", (NH, HD)), one_p("b_vp", (NH, HD)), one_p("b_fp", (NH,)),
            one_s("b_ks", (NH, HD)), one_s("b_vs", (NH, HD)), one_s("b_fs", (NH,)),
            one_p("c_kp", (NH, HD)), one_p("c_vp", (NH, HD)), one_s("c_ks", (NH, HD)), one_s("c_vs", (NH, HD)))
    return tuple(np.ascontiguousarray(o, dtype=np.float32) for o in outs)
```
